# Optimizing a Trainium2 kernel written in Bass

```python
import math
import jax, jax.numpy as jnp
from jax import lax
import numpy as np

D_MODEL = 1024
BATCH = 8
SEQ = 8192
DEPTH = 2

CTX_LEN = 256
GRID_W = 64
N_MIXERS = 2
N_LAYERS_A = (DEPTH + N_MIXERS - 1) // N_MIXERS
N_LAYERS_B = DEPTH // N_MIXERS
EPS = 1e-6

GM_WIDTH = 2 * D_MODEL
GM_GROUPS = 8
GM_GROUP_DIM = GM_WIDTH // GM_GROUPS
CHUNK = 128

DA_HEADS = D_MODEL // 128
DA_HEAD_DIM = 64
DA_WIDTH = DA_HEADS * 2 * DA_HEAD_DIM
Q_BLOCK = 128
ROPE_THETA = 10000.0

FFN_DIM = 2816
CONV_W = 3

kernel_name = 'hybrid_gmlp_diffattn_dit_block'


def _rms(x, eps=EPS):
    xf = x.astype(jnp.float32)
    return (xf * lax.rsqrt(jnp.mean(xf * xf, axis=-1, keepdims=True) + eps)).astype(x.dtype)


def _modulate(x, shift, scale):
    return _rms(x) * (1 + scale) + shift


def _ada_params(cond, w, b):
    m = jnp.dot(jax.nn.silu(cond), w) + b
    return jnp.split(m[..., None, :], 6, axis=-1)


def _chunk_gmlp(h, w_in, norm_g, w_s, b_s, w_out):
    bsz, length, _ = h.shape
    u, v = jnp.split(jax.nn.gelu(h @ w_in), 2, axis=-1)
    v = (_rms(v) * norm_g).reshape(bsz, length // CHUNK, CHUNK, GM_GROUPS, GM_GROUP_DIM)
    v = jnp.einsum('gij,bcjgd->bcigd', w_s, v) + b_s.T[None, None, :, :, None]
    return (u * v.reshape(bsz, length, GM_WIDTH)) @ w_out


def _axial_rope_tables(rows):
    row_pos = jnp.broadcast_to(jnp.arange(rows, dtype=jnp.float32)[:, None], (rows, GRID_W)).reshape(-1)
    col_pos = jnp.broadcast_to(jnp.arange(GRID_W, dtype=jnp.float32)[None, :], (rows, GRID_W)).reshape(-1)
    n_freq = DA_HEAD_DIM // 4
    inv_freq = ROPE_THETA ** (-jnp.arange(n_freq, dtype=jnp.float32) / n_freq)
    ang_r = row_pos[:, None] * inv_freq
    ang_c = col_pos[:, None] * inv_freq
    ang = jnp.concatenate([ang_r, ang_r, ang_c, ang_c], axis=-1)
    return jnp.cos(ang), jnp.sin(ang)


def _apply_rope(x, cos, sin):
    xr = x.reshape(*x.shape[:-1], 2, 2, DA_HEAD_DIM // 4)
    rot = jnp.concatenate([-xr[..., 1:, :], xr[..., :1, :]], axis=-2).reshape(x.shape)
    return x * cos[:, None, :].astype(x.dtype) + rot * sin[:, None, :].astype(x.dtype)


def _diff_attend(q, k, v, lam):
    s = jnp.einsum('bqhmd,bkhmd->bhmqk', q * (DA_HEAD_DIM ** -0.5), k, preferred_element_type=jnp.float32)
    p = jax.nn.softmax(s, axis=-1)
    a = p[:, :, 0] - lam * p[:, :, 1]
    return jnp.einsum('bhqk,bkhe->bqhe', a.astype(v.dtype), v, preferred_element_type=jnp.float32)


def _diff_head_out(o, subln_g, w_out, lambda_init, dtype):
    o = _rms(o) * subln_g * (1.0 - lambda_init)
    return o.reshape(*o.shape[:2], DA_WIDTH).astype(dtype) @ w_out


def _diff_attention(h_lat, h_ctx, w_qkv, lq1, lk1, lq2, lk2, subln_g, w_out, cos, sin, lambda_init, need_ctx_out):
    bsz, n_lat, _ = h_lat.shape
    n_ctx = h_ctx.shape[1]
    lam = jnp.exp(jnp.sum(lq1 * lk1)) - jnp.exp(jnp.sum(lq2 * lk2)) + lambda_init
    q_l, k_l, v_l = jnp.split(h_lat @ w_qkv, 3, axis=-1)
    q_l = _apply_rope(q_l.reshape(bsz, n_lat, 2 * DA_HEADS, DA_HEAD_DIM), cos, sin)
    k_l = _apply_rope(k_l.reshape(bsz, n_lat, 2 * DA_HEADS, DA_HEAD_DIM), cos, sin)
    k_c, v_c = jnp.split(h_ctx @ w_qkv[:, DA_WIDTH:], 2, axis=-1)
    k_c = k_c.reshape(bsz, n_ctx, DA_HEADS, 2, DA_HEAD_DIM)
    v_c = v_c.reshape(bsz, n_ctx, DA_HEADS, 2 * DA_HEAD_DIM)
    k_all = jnp.concatenate([k_c, k_l.reshape(bsz, n_lat, DA_HEADS, 2, DA_HEAD_DIM)], axis=1)
    v_all = jnp.concatenate([v_c, v_l.reshape(bsz, n_lat, DA_HEADS, 2 * DA_HEAD_DIM)], axis=1)
    n_blk = n_lat // Q_BLOCK
    q_blocks = q_l.reshape(bsz, n_blk, Q_BLOCK, DA_HEADS, 2, DA_HEAD_DIM).transpose(1, 0, 2, 3, 4, 5)
    o_blocks = lax.map(lambda qb: _diff_attend(qb, k_all, v_all, lam), q_blocks)
    o_lat = o_blocks.transpose(1, 0, 2, 3, 4).reshape(bsz, n_lat, DA_HEADS, 2 * DA_HEAD_DIM)
    out_lat = _diff_head_out(o_lat, subln_g, w_out, lambda_init, h_lat.dtype)
    if not need_ctx_out:
        return out_lat, None
    q_c = (h_ctx @ w_qkv[:, :DA_WIDTH]).reshape(bsz, n_ctx, DA_HEADS, 2, DA_HEAD_DIM)
    o_ctx = _diff_attend(q_c, k_c, v_c, lam)
    return out_lat, _diff_head_out(o_ctx, subln_g, w_out, lambda_init, h_ctx.dtype)


def _conv_ffn(h, w_up, conv_w, conv_b, w_down):
    z = h @ w_up
    length = z.shape[1]
    pad = CONV_W // 2
    zp = jnp.pad(z, ((0, 0), (pad, CONV_W - 1 - pad), (0, 0)))
    z = sum(zp[:, j:j + length] * conv_w[j] for j in range(CONV_W)) + conv_b
    a, g = jnp.split(z, 2, axis=-1)
    return (jax.nn.silu(a) * g) @ w_down


def setup_inputs(seed: int = 0) -> dict:
    key = jax.random.key(seed)
    ks = jax.random.split(key, 24)
    nrm = jax.random.normal
    f32 = jnp.float32
    return {
        'x': nrm(ks[0], (BATCH, SEQ, D_MODEL), f32),
        'c': nrm(ks[1], (BATCH, D_MODEL), f32),
        'ctx': nrm(ks[2], (BATCH, CTX_LEN, D_MODEL), f32),
        'c_ctx': nrm(ks[3], (D_MODEL,), f32),
        'ada_w': nrm(ks[4], (DEPTH, D_MODEL, 6 * D_MODEL), f32) * D_MODEL ** -0.5,
        'ada_b': nrm(ks[5], (DEPTH, 6 * D_MODEL), f32) * 0.02,
        'gm_w_in': nrm(ks[6], (N_LAYERS_A, D_MODEL, 2 * GM_WIDTH), f32) * D_MODEL ** -0.5,
        'gm_norm_g': 1.0 + 0.02 * nrm(ks[7], (N_LAYERS_A, GM_WIDTH), f32),
        'gm_w_s': nrm(ks[8], (N_LAYERS_A, GM_GROUPS, CHUNK, CHUNK), f32) * CHUNK ** -0.5,
        'gm_b_s': 1.0 + 0.02 * nrm(ks[9], (N_LAYERS_A, GM_GROUPS, CHUNK), f32),
        'gm_w_out': nrm(ks[10], (N_LAYERS_A, GM_WIDTH, D_MODEL), f32) * GM_WIDTH ** -0.5,
        'da_w_qkv': nrm(ks[11], (N_LAYERS_B, D_MODEL, 3 * DA_WIDTH), f32) * D_MODEL ** -0.5,
        'da_lambda_q1': 0.1 * nrm(ks[12], (N_LAYERS_B, DA_HEAD_DIM), f32),
        'da_lambda_k1': 0.1 * nrm(ks[13], (N_LAYERS_B, DA_HEAD_DIM), f32),
        'da_lambda_q2': 0.1 * nrm(ks[14], (N_LAYERS_B, DA_HEAD_DIM), f32),
        'da_lambda_k2': 0.1 * nrm(ks[15], (N_LAYERS_B, DA_HEAD_DIM), f32),
        'da_subln_g': 1.0 + 0.02 * nrm(ks[16], (N_LAYERS_B, 2 * DA_HEAD_DIM), f32),
        'da_w_out': nrm(ks[17], (N_LAYERS_B, DA_WIDTH, D_MODEL), f32) * DA_WIDTH ** -0.5,
        'ffn_w_up': nrm(ks[18], (DEPTH, D_MODEL, 2 * FFN_DIM), f32) * D_MODEL ** -0.5,
        'ffn_conv_w': nrm(ks[19], (DEPTH, CONV_W, 2 * FFN_DIM), f32) * CONV_W ** -0.5,
        'ffn_conv_b': 0.02 * nrm(ks[20], (DEPTH, 2 * FFN_DIM), f32),
        'ffn_w_down': nrm(ks[21], (DEPTH, FFN_DIM, D_MODEL), f32) * FFN_DIM ** -0.5,
        'final_norm_g': 1.0 + 0.02 * nrm(ks[22], (D_MODEL,), f32),
    }


def reference(x, c, ctx, c_ctx, ada_w, ada_b, gm_w_in, gm_norm_g, gm_w_s, gm_b_s, gm_w_out,
              da_w_qkv, da_lambda_q1, da_lambda_k1, da_lambda_q2, da_lambda_k2, da_subln_g, da_w_out,
              ffn_w_up, ffn_conv_w, ffn_conv_b, ffn_w_down, final_norm_g):
    n_lat = x.shape[1]
    ROWS = n_lat // GRID_W
    cos, sin = _axial_rope_tables(ROWS)
    h, hc = x, ctx
    for i in range(DEPTH):
        j = i // N_MIXERS
        last = i == DEPTH - 1
        sh1, sc1, g1, sh2, sc2, g2 = _ada_params(c, ada_w[i], ada_b[i])
        csh1, csc1, cg1, csh2, csc2, cg2 = _ada_params(c_ctx, ada_w[i], ada_b[i])
        xl = _modulate(h, sh1, sc1)
        if i % N_MIXERS == 0:
            m_lat = _chunk_gmlp(xl, gm_w_in[j], gm_norm_g[j], gm_w_s[j], gm_b_s[j], gm_w_out[j])
            m_ctx = None if last else _chunk_gmlp(_modulate(hc, csh1, csc1), gm_w_in[j], gm_norm_g[j],
                                                  gm_w_s[j], gm_b_s[j], gm_w_out[j])
        else:
            lambda_init = 0.8 - 0.6 * math.exp(-0.3 * i)
            m_lat, m_ctx = _diff_attention(xl, _modulate(hc, csh1, csc1), da_w_qkv[j],
                                           da_lambda_q1[j], da_lambda_k1[j], da_lambda_q2[j], da_lambda_k2[j],
                                           da_subln_g[j], da_w_out[j], cos, sin, lambda_init, not last)
        h = h + g1 * m_lat
        h = h + g2 * _conv_ffn(_modulate(h, sh2, sc2), ffn_w_up[i], ffn_conv_w[i], ffn_conv_b[i], ffn_w_down[i])
        if not last:
            hc = hc + cg1 * m_ctx
            hc = hc + cg2 * _conv_ffn(_modulate(hc, csh2, csc2), ffn_w_up[i], ffn_conv_w[i], ffn_conv_b[i],
                                      ffn_w_down[i])
    return _rms(h) * final_norm_g
```

```python
import math
from contextlib import ExitStack

import numpy as np
import ml_dtypes
import concourse.bass as bass
import concourse.mybir as mybir
from concourse.bass_utils import run_bass_kernel_spmd

F32 = mybir.dt.float32
BF16 = mybir.dt.bfloat16
AF = mybir.ActivationFunctionType
ALU = mybir.AluOpType

D = 1024
KC = 8
CTX = 256
NCORES = 8
SEQ = 8192
EPS = 1e-6
GMW = 2048
FFN = 2816
NCC = 44
NPAIR = 22
HEADS = 8
LAMBDA_INIT = 0.8 - 0.6 * math.exp(-0.3 * 1)


class Eng:
    def __init__(self, K, name, eng):
        self.K, self.name, self.e = K, name, eng
        self.sem = K.newsem("e_" + name)
        self.cnt = 0
        self.seen = {}

    def wait(self, *tks):
        for tk in tks:
            if tk is None:
                continue
            if isinstance(tk, list):
                self.wait(*tk)
                continue
            sem, v = tk
            if self.seen.get(sem, 0) < v:
                self.e.wait_ge(sem, v)
                self.seen[sem] = v

    def mark(self, ins):
        self.cnt += 1
        ins.then_inc(self.sem, 1)
        return (self.sem, self.cnt)

    def last(self):
        return (self.sem, self.cnt) if self.cnt else None


class DSem:
    def __init__(self, K, name):
        self.sem = K.newsem(name)
        self.cnt = 0

    def tk(self):
        return (self.sem, self.cnt) if self.cnt else None


class Ring:
    def __init__(self, bufs):
        self.bufs = list(bufs)
        self.rel = [None] * len(self.bufs)
        self.i = 0

    def get(self, *engs):
        i = self.i % len(self.bufs)
        self.i += 1
        for e in engs:
            e.wait(self.rel[i])
        self.rel[i] = None
        return i, self.bufs[i]

    def release(self, i, *tks):
        cur = self.rel[i]
        lst = [] if cur is None else list(cur)
        lst.extend(t for t in tks if t is not None)
        self.rel[i] = lst


class Kern:
    def __init__(self, nc, es):
        self.nc, self.es = nc, es
        self.nsem = 0
        self.pe = Eng(self, "pe", nc.tensor)
        self.act = Eng(self, "act", nc.scalar)
        self.dve = Eng(self, "dve", nc.vector)
        self.pool = Eng(self, "pool", nc.gpsimd)
        self.sp = Eng(self, "sp", nc.sync)
        self.engs = [self.pe, self.act, self.dve, self.pool, self.sp]
        self.dsems = []

    def newsem(self, name):
        self.nsem += 1
        return self.es.enter_context(self.nc.semaphore(name))

    def dsem(self, name):
        d = DSem(self, name)
        self.dsems.append(d)
        return d

    def dma(self, q, ds, out, in_):
        ins = q.e.dma_start(out=out, in_=in_)
        ds.cnt += 16
        ins.then_inc(ds.sem, 16)
        return (ds.sem, ds.cnt)

    def barrier(self):
        tks = [e.last() for e in self.engs] + [d.tk() for d in self.dsems]
        for e in self.engs:
            e.wait(*tks)


_UID = [0]


def sb(nc, st, name, shape, dt):
    _UID[0] += 1
    return st.enter_context(nc.sbuf_tensor("s%d_%s" % (_UID[0], name), list(shape), dt))


def build_program(S=SEQ, stop_after=None, dbg=False):
    nc = bass.Bass("TRN2", target_bir_lowering=False)
    ST = S + CTX
    NT = ST // 128
    kind_scr = "ExternalOutput" if dbg else "Internal"

    def din(name, shape, dt=F32):
        return nc.dram_tensor(name, list(shape), dt, kind="ExternalInput").ap()

    def dscr(name, shape, dt):
        return nc.dram_tensor(name, list(shape), dt, kind=kind_scr).ap()

    x = din("x", [S, D])
    ctx = din("ctx", [CTX, D])
    cvec = din("cvec", [128, KC * 2])
    ada_w = din("ada_w", [2, D, 6 * D])
    ada_bT = din("ada_bT", [128, 2 * 48])
    gm_w_in = din("gm_w_in", [D, 2 * GMW])
    gm_gT = din("gm_gT", [128, 16])
    gm_wsT = din("gm_wsT", [128, 8 * 128])
    gm_bs = din("gm_bs", [1, 8 * 128])
    gm_w_out = din("gm_w_out", [GMW, D])
    w_qkv = din("w_qkv", [D, 3 * D])
    lamv = din("lamv", [1, 256])
    sublnT = din("sublnT", [128, 1])
    w_o = din("w_o", [D, D])
    w_up = din("w_up", [2, D, 2 * FFN])
    convT = din("convT", [128, 2 * 4 * NCC])
    w_dn = din("w_dn", [2, FFN, D])
    fngT = din("fngT", [128, KC])
    ident_d = din("ident", [128, 128])
    cosT_d = din("cosT", [128, S])
    sinT_d = din("sinT", [128, S])
    out = nc.dram_tensor("out", [S, D], F32, kind="ExternalOutput").ap()

    H1T = dscr("H1T", [D, ST], F32)
    H2T = dscr("H2T", [D, ST], F32)
    QT = dscr("QT", [HEADS, 128, S], BF16)
    KT = dscr("KT", [HEADS, 128, ST], BF16)
    VH = dscr("VH", [HEADS, 128, NT * 128], BF16)
    AT = dscr("AT", [D, S], BF16)

    es = ExitStack()
    with es:
        K = Kern(nc, es)
        pe, act, dve, pool, sp = K.pe, K.act, K.dve, K.pool, K.sp

        PSD = [es.enter_context(nc.psum_tensor("psd%d" % i, [128, 1024], F32)) for i in range(4)]
        PSB = [PSD[i // 2][:, (i % 2) * 512:(i % 2 + 1) * 512] for i in range(8)]
        ident = sb(nc, es, "ident", [128, 128], F32)
        ones_bf = sb(nc, es, "ones_bf", [128, 128], BF16)
        modT = sb(nc, es, "modT", [128, 2, 2, 48], F32)
        convS = sb(nc, es, "convS", [128, 2, 4, NCC], F32)
        gmg = sb(nc, es, "gmg", [128, 16], F32)
        subg = sb(nc, es, "subg", [128, 1], F32)
        fng = sb(nc, es, "fng", [128, KC], F32)
        neglam = sb(nc, es, "neglam", [128, 1], F32)
        epsb = sb(nc, es, "epsb", [128, 1], F32)

        ld_c = K.dsem("ld_c")
        K.dma(sp, ld_c, ident[:], ident_d[:])
        K.dma(sp, ld_c, convS[:].rearrange("p a b c -> p (a b c)"), convT[:])
        K.dma(sp, ld_c, gmg[:], gm_gT[:])
        K.dma(sp, ld_c, subg[:], sublnT[:])
        t_const = K.dma(sp, ld_c, fng[:], fngT[:])
        dve.mark(nc.vector.memset(ones_bf[:], 1.0))
        t_ones = dve.mark(nc.vector.memset(epsb[:], EPS))
        for e in (pe, act, dve):
            e.wait(t_const, t_ones)

        def modv(l, v, j):
            return modT[:, l, v, j:j + 1]

        def phase_mod():
            st = ExitStack()
            with st:
                c32 = sb(nc, st, "c32", [128, KC * 2], F32)
                csb = sb(nc, st, "csb", [128, KC, 2], BF16)
                abT = sb(nc, st, "abT", [128, 2, 48], F32)
                lam_s = sb(nc, st, "lam_s", [1, 256], F32)
                lam_p = sb(nc, st, "lam_p", [1, 128], F32)
                lam_r = sb(nc, st, "lam_r", [1, 4], F32)
                ones1 = sb(nc, st, "ones1", [1, 128], F32)
                wa = [sb(nc, st, "wa%d" % i, [128, KC, 1024], BF16) for i in range(2)]
                wring = Ring(wa)
                ld = K.dsem("ld_mod")
                ldw = [K.dsem("ld_wa%d" % i) for i in range(2)]
                K.dma(sp, ld, c32[:], cvec[:])
                K.dma(sp, ld, abT[:].rearrange("p a b -> p (a b)"), ada_bT[:])
                t_in = K.dma(sp, ld, lam_s[:], lamv[:])
                act.wait(t_in)
                t_c = act.mark(nc.scalar.activation(out=csb[:].rearrange("p a b -> p (a b)"), in_=c32[:], func=AF.Silu))
                dve.wait(t_in)
                dve.mark(nc.vector.memset(ones1[:], 1.0))
                dve.mark(nc.vector.tensor_tensor(out=lam_p[:, 0:64], in0=lam_s[:, 0:64], in1=lam_s[:, 64:128], op=ALU.mult))
                t1 = dve.mark(nc.vector.tensor_tensor(out=lam_p[:, 64:128], in0=lam_s[:, 128:192], in1=lam_s[:, 192:256], op=ALU.mult))
                dve.wait(t1)
                t2 = dve.mark(nc.vector.reduce_sum(out=lam_r[:, 0:2], in_=lam_p[:].rearrange("p (a b) -> p a b", a=2),
                                                   axis=mybir.AxisListType.X))
                act.wait(t2)
                t3 = act.mark(nc.scalar.activation(out=lam_r[:, 2:4], in_=lam_r[:, 0:2], func=AF.Exp))
                dve.wait(t3)
                t4 = dve.mark(nc.vector.tensor_tensor(out=lam_r[:, 0:1], in0=lam_r[:, 3:4], in1=lam_r[:, 2:3], op=ALU.subtract))
                dve.wait(t4)
                t5 = dve.mark(nc.vector.tensor_scalar(out=lam_r[:, 1:2], in0=lam_r[:, 0:1], scalar1=-LAMBDA_INIT, scalar2=None, op0=ALU.add))
                pe.wait(t5)
                t6 = pe.mark(nc.tensor.matmul(PSB[7][:, 0:1], ones1[:], lam_r[:, 1:2], start=True, stop=True))
                dve.wait(t6)
                t7 = dve.mark(nc.vector.tensor_copy(out=neglam[:], in_=PSB[7][:, 0:1]))
                pring = Ring([PSB[0], PSB[1]])
                for l in range(2):
                    for cg in range(6):
                        wi, wbuf = wring.get(pool)
                        for kc in range(KC):
                            tw = K.dma(pool, ldw[wi], wbuf[:, kc, :], ada_w[l, kc * 128:(kc + 1) * 128, cg * 1024:(cg + 1) * 1024])
                        pi, pb = pring.get(pe)
                        pe.wait(tw, t_c)
                        for fcl in range(8):
                            for kc in range(KC):
                                mm = nc.tensor.matmul(pb[:, fcl * 2:fcl * 2 + 2], wbuf[:, kc, fcl * 128:(fcl + 1) * 128],
                                                      csb[:, kc, :], start=(kc == 0), stop=(kc == KC - 1))
                        tm = pe.mark(mm)
                        wring.release(wi, tm)
                        dve.wait(tm, t_in)
                        td = dve.mark(nc.vector.tensor_tensor(
                            out=modT[:, l, :, cg * 8:(cg + 1) * 8],
                            in0=pb[:, 0:16].rearrange("p (f v) -> p v f", v=2),
                            in1=abT[:, l, cg * 8:(cg + 1) * 8].unsqueeze(1).to_broadcast([128, 2, 8]),
                            op=ALU.add))
                        pring.release(pi, td)
                dve.wait(dve.last())
                for l in range(2):
                    for j0 in (8, 32):
                        dve.mark(nc.vector.tensor_scalar(out=modT[:, l, :, j0:j0 + 8], in0=modT[:, l, :, j0:j0 + 8],
                                                         scalar1=1.0, scalar2=None, op0=ALU.add))
                K.barrier()

        def stats_modulate(hsrc, n, l, v, jsc, jsh, xdst, bufs, t_h, ps_ring, extra_wait=None):
            sqr, sdb, rb, tmr = bufs["sq"], bufs["sd"], bufs["r"], bufs["tm"]
            pi, pb = ps_ring.get(pe)
            for kc in range(KC):
                si, sq = sqr.get(act)
                act.wait(t_h)
                ts = act.mark(nc.scalar.activation(out=sq[:, 0:n], in_=hsrc(kc), func=AF.Square))
                pe.wait(ts)
                mm = nc.tensor.matmul(pb[:, 0:n], ones_bf[:], sq[:, 0:n], start=(kc == 0), stop=(kc == KC - 1))
                tm = pe.mark(mm)
                sqr.release(si, tm)
            act.wait(tm, bufs.get("sd_rel"))
            t_sd = act.mark(nc.scalar.activation(out=sdb[:, 0:n], in_=pb[:, 0:n], func=AF.Sqrt, bias=epsb[:, 0:1], scale=1.0 / D))
            ps_ring.release(pi, t_sd)
            dve.wait(t_sd, bufs.get("r_rel"))
            t_r = dve.mark(nc.vector.reciprocal(out=rb[:, 0:n], in_=sdb[:, 0:n]))
            bufs["sd_rel"] = t_r
            tks = []
            for kc in range(KC):
                ti, tmb = tmr.get(dve)
                dve.wait(t_r, t_h)
                tt = dve.mark(nc.vector.scalar_tensor_tensor(out=tmb[:, 0:n], in0=hsrc(kc), scalar=modv(l, v, jsc + kc),
                                                             in1=rb[:, 0:n], op0=ALU.mult, op1=ALU.mult))
                act.wait(tt, extra_wait)
                ta = act.mark(nc.scalar.activation(out=xdst(kc), in_=tmb[:, 0:n], func=AF.Identity,
                                                   bias=modv(l, v, jsh + kc), scale=1.0))
                tmr.release(ti, ta)
                tks.append(ta)
            bufs["r_rel"] = tt
            return tks

        def phase_G(TBG=3):
            st = ExitStack()
            with st:
                NB = TBG * 128
                WinU = sb(nc, st, "WinU", [128, KC, GMW], BF16)
                WinV = sb(nc, st, "WinV", [128, KC, GMW], BF16)
                WsT = sb(nc, st, "WsT", [128, 8, 128], BF16)
                Wout = sb(nc, st, "Wout", [128, 16, D], BF16)
                bsr = sb(nc, st, "bsr", [128, 8, 128], F32)
                xt = [sb(nc, st, "xt%d" % i, [128, D], F32) for i in range(TBG)]
                hT = [sb(nc, st, "hT%d" % i, [128, KC, NB], F32) for i in range(2)]
                xlT = sb(nc, st, "xlT", [128, KC, NB], BF16)
                uT = sb(nc, st, "uT", [128, 16, NB], BF16)
                gv = [sb(nc, st, "gv%d" % i, [128, GMW], F32) for i in range(2)]
                vn = sb(nc, st, "vn", [128, TBG, GMW], BF16)
                junk = sb(nc, st, "junk", [128, GMW], BF16)
                vst = sb(nc, st, "vst", [128, 4 * TBG * 2], F32)
                bufs = {
                    "sq": Ring([sb(nc, st, "sq%d" % i, [128, NB], BF16) for i in range(2)]),
                    "sd": sb(nc, st, "sdb", [128, NB], F32),
                    "r": sb(nc, st, "rb", [128, NB], F32),
                    "tm": Ring([sb(nc, st, "tm%d" % i, [128, NB], F32) for i in range(2)]),
                }
                t2r = Ring([sb(nc, st, "t2_%d" % i, [128, NB], F32) for i in range(2)])
                ldw = K.dsem("ldw_G")
                for kc in range(KC):
                    K.dma(pool, ldw, WinU[:, kc, :], gm_w_in[kc * 128:(kc + 1) * 128, 0:GMW])
                    K.dma(pool, ldw, WinV[:, kc, :], gm_w_in[kc * 128:(kc + 1) * 128, GMW:2 * GMW])
                for fc in range(16):
                    K.dma(pool, ldw, Wout[:, fc, :], gm_w_out[fc * 128:(fc + 1) * 128, :])
                K.dma(pool, ldw, WsT[:].rearrange("p a b -> p (a b)"), gm_wsT[:])
                t_w = K.dma(pool, ldw, bsr[:].rearrange("p a b -> p (a b)"), gm_bs.partition_broadcast(128))
                pe.wait(t_w)
                dve.wait(t_w)

                ldx = [K.dsem("ldx_G%d" % i) for i in range(TBG)]
                st_h = [K.dsem("st_G%d" % i) for i in range(2)]
                xring = Ring(xt)
                hring = Ring(hT)
                psT = Ring([PSB[0], PSB[1]])
                psA = Ring([PSB[2], PSB[3]])
                psM = Ring([PSB[4], PSB[5]])
                psO = Ring([PSB[6], PSB[7]])
                gvr = Ring(gv)
                vn_rel = None
                xl_rel = None
                u_rel = None

                blocks = []
                t0 = 0
                while t0 < S:
                    nt = min(TBG, (S - t0) // 128)
                    blocks.append((x, t0, t0, nt, 0))
                    t0 += nt * 128
                blocks.append((ctx, 0, S, CTX // 128, 1))

                for bidx, (src, r0, c0, nt, v) in enumerate(blocks):
                    n = nt * 128
                    tl = []
                    for j in range(nt):
                        xi, xb = xring.get(sp)
                        tl.append((xi, xb, K.dma(sp, ldx[xi], xb[:], src[r0 + j * 128:r0 + (j + 1) * 128, :])))
                    hi, hb = hring.get(act)
                    tev = []
                    for kc in range(KC):
                        pi, pb = psT.get(pe)
                        for j in range(nt):
                            pe.wait(tl[j][2])
                            mm = nc.tensor.transpose(pb[:, j * 128:(j + 1) * 128], tl[j][1][:, kc * 128:(kc + 1) * 128], ident[:])
                        tm = pe.mark(mm)
                        act.wait(tm)
                        ta = act.mark(nc.scalar.copy(out=hb[:, kc, 0:n], in_=pb[:, 0:n]))
                        psT.release(pi, ta)
                        tev.append(ta)
                    for j in range(nt):
                        xring.release(tl[j][0], tm)
                    tx = stats_modulate(lambda kc: hb[:, kc, 0:n], n, 0, v, 8, 0, lambda kc: xlT[:, kc, 0:n], bufs,
                                        tev, psT, extra_wait=xl_rel)
                    tu = []
                    for fc in range(16):
                        pi, pb = psA.get(pe)
                        pe.wait(tx)
                        for kc in range(KC):
                            mm = nc.tensor.matmul(pb[:, 0:n], WinU[:, kc, fc * 128:(fc + 1) * 128], xlT[:, kc, 0:n],
                                                  start=(kc == 0), stop=(kc == KC - 1))
                        tm = pe.mark(mm)
                        act.wait(tm, u_rel)
                        ta = act.mark(nc.scalar.activation(out=uT[:, fc, 0:n], in_=pb[:, 0:n], func=AF.Gelu_apprx_tanh))
                        psA.release(pi, ta)
                        tu.append(ta)
                    tvn = []
                    for j in range(nt):
                        gi, gb = gvr.get(act)
                        for cb in range(4):
                            pi, pb = psA.get(pe)
                            for kc in range(KC):
                                mm = nc.tensor.matmul(pb[:, :], xlT[:, kc, j * 128:(j + 1) * 128], WinV[:, kc, cb * 512:(cb + 1) * 512],
                                                      start=(kc == 0), stop=(kc == KC - 1))
                            tm = pe.mark(mm)
                            act.wait(tm)
                            ta = act.mark(nc.scalar.activation(out=gb[:, cb * 512:(cb + 1) * 512], in_=pb[:, :], func=AF.Gelu_apprx_tanh))
                            psA.release(pi, ta)
                        act.wait(ta)
                        c = j * 4 + (bidx % 2) * 4 * TBG
                        ts = act.mark(nc.scalar.activation(out=junk[:], in_=gb[:], func=AF.Square, accum_out=vst[:, c:c + 1]))
                        act.wait(ts)
                        ts2 = act.mark(nc.scalar.activation(out=vst[:, c + 1:c + 2], in_=vst[:, c:c + 1], func=AF.Sqrt,
                                                            bias=epsb[:, 0:1], scale=1.0 / GMW))
                        dve.wait(ts2)
                        tr = dve.mark(nc.vector.reciprocal(out=vst[:, c + 2:c + 3], in_=vst[:, c + 1:c + 2]))
                        dve.wait(tr, vn_rel if j == 0 else None)
                        tn = dve.mark(nc.vector.tensor_scalar(out=vn[:, j, :], in0=gb[:], scalar1=vst[:, c + 2:c + 3], scalar2=None,
                                                              op0=ALU.mult))
                        gvr.release(gi, tn)
                        tvn.append(tn)
                    xl_rel = tm
                    tuv = []
                    for fc in range(16):
                        g = fc // 2
                        pi, pb = psM.get(pe)
                        pe.wait(tvn)
                        for j in range(nt):
                            mm = nc.tensor.matmul(pb[:, j * 128:(j + 1) * 128], vn[:, j, fc * 128:(fc + 1) * 128], WsT[:, g, :],
                                                  start=True, stop=True)
                        tm = pe.mark(mm)
                        ti, tb = t2r.get(dve)
                        dve.wait(tm)
                        td = dve.mark(nc.vector.scalar_tensor_tensor(
                            out=tb[:, 0:n].rearrange("p (a b) -> p a b", a=nt),
                            in0=pb[:, 0:n].rearrange("p (a b) -> p a b", a=nt), scalar=gmg[:, fc:fc + 1],
                            in1=bsr[:, g, :].unsqueeze(1).to_broadcast([128, nt, 128]), op0=ALU.mult, op1=ALU.add))
                        psM.release(pi, td)
                        dve.wait(td, tu[fc])
                        td2 = dve.mark(nc.vector.tensor_tensor(out=uT[:, fc, 0:n], in0=tb[:, 0:n], in1=uT[:, fc, 0:n], op=ALU.mult))
                        t2r.release(ti, td2)
                        tuv.append(td2)
                    vn_rel = tm
                    for dc in range(KC):
                        pi, pb = psO.get(pe)
                        pe.wait(tuv)
                        for fc in range(16):
                            mm = nc.tensor.matmul(pb[:, 0:n], Wout[:, fc, dc * 128:(dc + 1) * 128], uT[:, fc, 0:n],
                                                  start=(fc == 0), stop=(fc == 15))
                        tm = pe.mark(mm)
                        dve.wait(tm, tev[dc])
                        td = dve.mark(nc.vector.scalar_tensor_tensor(out=hb[:, dc, 0:n], in0=pb[:, 0:n], scalar=modv(0, v, 16 + dc),
                                                                     in1=hb[:, dc, 0:n], op0=ALU.mult, op1=ALU.add))
                        psO.release(pi, td)
                    u_rel = tm
                    pool.wait(td)
                    tst = K.dma(pool, st_h[hi], H1T.rearrange("(k p) t -> p k t", p=128)[:, :, c0:c0 + n], hb[:, :, 0:n])
                    hring.release(hi, tst)
                K.barrier()

        def phase_F(l, SRC, DST, with_o, final, NV):
            st = ExitStack()
            with st:
                NCOL = NV + 2
                Wup = sb(nc, st, "Wup", [128, KC, 2 * FFN], BF16)
                Wdn = sb(nc, st, "Wdn", [128, NPAIR, D], BF16)
                hT = [sb(nc, st, "fh%d" % i, [128, KC, NCOL], F32) for i in range(2)]
                x2T = sb(nc, st, "x2T", [128, KC, NCOL], BF16)
                actT = sb(nc, st, "actT", [128, NPAIR, NCOL], BF16)
                zc = Ring([sb(nc, st, "zc%d" % i, [128, NCOL], F32) for i in range(2)])
                sa = Ring([sb(nc, st, "sa%d" % i, [128, NCOL], F32) for i in range(2)])
                bufs = {
                    "sq": Ring([sb(nc, st, "fsq%d" % i, [128, NCOL], BF16) for i in range(2)]),
                    "sd": sb(nc, st, "fsd", [128, NCOL], F32),
                    "r": sb(nc, st, "frb", [128, NCOL], F32),
                    "tm": Ring([sb(nc, st, "ftm%d" % i, [128, NCOL], F32) for i in range(2)]),
                }
                ldw = K.dsem("ldw_F%d" % l)
                for kc in range(KC):
                    K.dma(pool, ldw, Wup[:, kc, :], w_up[l, kc * 128:(kc + 1) * 128, :])
                for c in range(NPAIR):
                    t_w = K.dma(pool, ldw, Wdn[:, c, :], w_dn[l, c * 128:(c + 1) * 128, :])
                if with_o:
                    Wo = sb(nc, st, "Wo", [128, KC, D], BF16)
                    aT = [sb(nc, st, "aT%d" % i, [128, KC, NCOL], BF16) for i in range(2)]
                    for kc in range(KC):
                        t_w = K.dma(pool, ldw, Wo[:, kc, :], w_o[kc * 128:(kc + 1) * 128, :])
                if final:
                    ost = [sb(nc, st, "ost%d" % i, [128, D], F32) for i in range(2)]
                    oring = Ring(ost)
                    st_o = [K.dsem("st_o%d" % i) for i in range(2)]
                pe.wait(t_w)
                ldh = [K.dsem("ldh_F%d_%d" % (l, i)) for i in range(2)]
                st_h = [K.dsem("st_F%d_%d" % (l, i)) for i in range(2)]
                hring = Ring(hT)
                psZ = Ring([PSB[0], PSB[1], PSB[2]])
                psO = Ring([PSB[3], PSB[4]])
                psS = Ring([PSB[5]])
                psT = Ring([PSB[6], PSB[7]])
                x2_rel = None
                a_rel = None
                for i in range(2):
                    dve.mark(nc.vector.memset(hT[i][:].rearrange("p a b -> p (a b)"), 0.0))
                    if with_o:
                        dve.mark(nc.vector.memset(aT[i][:].rearrange("p a b -> p (a b)"), 0.0))
                t_ms = dve.last()

                segs = [(0, S)] + ([] if final else [(S, CTX)])
                blocks = []
                for (base, L) in segs:
                    nb = -(-L // NV)
                    nv = -(-L // nb)
                    ts = 0
                    while ts < L:
                        blocks.append((base, L, ts, min(nv, L - ts), 1 if base == S else 0))
                        ts += nv
                SRCv = SRC.rearrange("(k p) t -> p k t", p=128)
                if DST is not None:
                    DSTv = DST.rearrange("(k p) t -> p k t", p=128)
                ATv = AT.rearrange("(k p) t -> p k t", p=128)

                for (base, L, ts, nv, v) in blocks:
                    ncol = nv + 2
                    lo = max(ts - 1, 0)
                    hi_ = min(ts + nv + 1, L)
                    c_lo = lo - (ts - 1)
                    c_hi = c_lo + (hi_ - lo)
                    hi, hb = hring.get(sp)
                    sp.wait(t_ms)
                    t_h = K.dma(sp, ldh[hi], hb[:, :, c_lo:c_hi], SRCv[:, :, base + lo:base + hi_])
                    if with_o:
                        ab = aT[hi]
                        t_h = K.dma(sp, ldh[hi], ab[:, :, c_lo:c_hi], ATv[:, :, lo:hi_])
                        tres = []
                        for dc in range(KC):
                            pi, pb = psO.get(pe)
                            pe.wait(t_h)
                            for hc in range(KC):
                                mm = nc.tensor.matmul(pb[:, 0:ncol], Wo[:, hc, dc * 128:(dc + 1) * 128], ab[:, hc, 0:ncol],
                                                      start=(hc == 0), stop=(hc == KC - 1))
                            tm = pe.mark(mm)
                            dve.wait(tm, t_h)
                            td = dve.mark(nc.vector.scalar_tensor_tensor(out=hb[:, dc, 0:ncol], in0=pb[:, 0:ncol],
                                                                         scalar=modv(l, v, 16 + dc), in1=hb[:, dc, 0:ncol],
                                                                         op0=ALU.mult, op1=ALU.add))
                            psO.release(pi, td)
                            tres.append(td)
                        t_hh = tres
                    else:
                        t_hh = [t_h]
                    tx = stats_modulate(lambda kc: hb[:, kc, 0:ncol], ncol, l, v, 32, 24, lambda kc: x2T[:, kc, 0:ncol], bufs,
                                        t_hh, psS, extra_wait=x2_rel)
                    tz = []
                    if c_lo > 0:
                        dve.wait(tx)
                        tz.append(dve.mark(nc.vector.memset(x2T[:, :, 0:1], 0.0)))
                    if c_hi < ncol:
                        dve.wait(tx)
                        tz.append(dve.mark(nc.vector.memset(x2T[:, :, ncol - 1:ncol], 0.0)))
                    tact = []
                    for i in range(NPAIR):
                        res = []
                        for half in range(2):
                            cc = i + half * NPAIR
                            pi, pb = psZ.get(pe)
                            pe.wait(tx, tz)
                            for kc in range(KC):
                                mm = nc.tensor.matmul(pb[:, 0:ncol], Wup[:, kc, cc * 128:(cc + 1) * 128], x2T[:, kc, 0:ncol],
                                                      start=(kc == 0), stop=(kc == KC - 1))
                            tm = pe.mark(mm)
                            zi, zb = zc.get(act)
                            act.wait(tm)
                            ta = act.mark(nc.scalar.activation(out=zb[:, 0:ncol], in_=pb[:, 0:ncol], func=AF.Identity,
                                                               bias=convS[:, l, 3, cc:cc + 1], scale=convS[:, l, 1, cc:cc + 1]))
                            dve.wait(ta)
                            td1 = dve.mark(nc.vector.scalar_tensor_tensor(out=zb[:, 1:nv + 1], in0=pb[:, 0:nv],
                                                                          scalar=convS[:, l, 0, cc:cc + 1], in1=zb[:, 1:nv + 1],
                                                                          op0=ALU.mult, op1=ALU.add))
                            dve.wait(td1)
                            td2 = dve.mark(nc.vector.scalar_tensor_tensor(out=zb[:, 1:nv + 1], in0=pb[:, 2:nv + 2],
                                                                          scalar=convS[:, l, 2, cc:cc + 1], in1=zb[:, 1:nv + 1],
                                                                          op0=ALU.mult, op1=ALU.add))
                            psZ.release(pi, td2)
                            res.append((zi, zb, td2))
                        (zia, zba, tda), (zig, zbg, tdg) = res
                        si, sbuf_ = sa.get(act)
                        act.wait(tda)
                        tsl = act.mark(nc.scalar.activation(out=sbuf_[:, 1:nv + 1], in_=zba[:, 1:nv + 1], func=AF.Silu))
                        zc.release(zia, tsl)
                        dve.wait(tsl, tdg, a_rel if i == 0 else None)
                        tg = dve.mark(nc.vector.tensor_tensor(out=actT[:, i, 1:nv + 1], in0=sbuf_[:, 1:nv + 1], in1=zbg[:, 1:nv + 1],
                                                              op=ALU.mult))
                        zc.release(zig, tg)
                        sa.release(si, tg)
                        tact.append(tg)
                    x2_rel = tm
                    tdn = []
                    for dc in range(KC):
                        pi, pb = psO.get(pe)
                        pe.wait(tact)
                        for c in range(NPAIR):
                            mm = nc.tensor.matmul(pb[:, 0:nv], Wdn[:, c, dc * 128:(dc + 1) * 128], actT[:, c, 1:nv + 1],
                                                  start=(c == 0), stop=(c == NPAIR - 1))
                        tm = pe.mark(mm)
                        dve.wait(tm)
                        td = dve.mark(nc.vector.scalar_tensor_tensor(out=hb[:, dc, 1:nv + 1], in0=pb[:, 0:nv], scalar=modv(l, v, 40 + dc),
                                                                     in1=hb[:, dc, 1:nv + 1], op0=ALU.mult, op1=ALU.add))
                        psO.release(pi, td)
                        tdn.append(td)
                    a_rel = tm
                    if not final:
                        pool.wait(tdn)
                        tst = K.dma(pool, st_h[hi], DSTv[:, :, base + ts:base + ts + nv], hb[:, :, 1:nv + 1])
                        hring.release(hi, tst)
                    else:
                        pi, pb = psS.get(pe)
                        for kc in range(KC):
                            si, sq = bufs["sq"].get(act)
                            act.wait(tdn[kc])
                            tsq = act.mark(nc.scalar.activation(out=sq[:, 0:nv], in_=hb[:, kc, 1:nv + 1], func=AF.Square))
                            pe.wait(tsq)
                            tm = pe.mark(nc.tensor.matmul(pb[:, 0:nv], ones_bf[:], sq[:, 0:nv], start=(kc == 0), stop=(kc == KC - 1)))
                            bufs["sq"].release(si, tm)
                        act.wait(tm, bufs.get("sd_rel"))
                        t_sd = act.mark(nc.scalar.activation(out=bufs["sd"][:, 0:nv], in_=pb[:, 0:nv], func=AF.Sqrt,
                                                             bias=epsb[:, 0:1], scale=1.0 / D))
                        psS.release(pi, t_sd)
                        dve.wait(t_sd, bufs.get("r_rel"))
                        t_r = dve.mark(nc.vector.reciprocal(out=bufs["r"][:, 0:nv], in_=bufs["sd"][:, 0:nv]))
                        bufs["sd_rel"] = t_r
                        tyn = []
                        for kc in range(KC):
                            dve.wait(t_r, tdn[kc])
                            tyn.append(dve.mark(nc.vector.scalar_tensor_tensor(out=hb[:, kc, 1:nv + 1], in0=hb[:, kc, 1:nv + 1],
                                                                               scalar=fng[:, kc:kc + 1], in1=bufs["r"][:, 0:nv],
                                                                               op0=ALU.mult, op1=ALU.mult)))
                        bufs["r_rel"] = tyn[-1]
                        j0 = 0
                        while j0 < nv:
                            rows = min(128, nv - j0)
                            oi, ob = oring.get(act)
                            for q4 in range(2):
                                pi, pb = psT.get(pe)
                                pe.wait(tyn)
                                for kk in range(4):
                                    kc = q4 * 4 + kk
                                    mm = nc.tensor.transpose(pb[0:rows, kk * 128:(kk + 1) * 128], hb[:, kc, 1 + j0:1 + j0 + rows], ident[:])
                                tm = pe.mark(mm)
                                act.wait(tm)
                                ta = act.mark(nc.scalar.copy(out=ob[0:rows, q4 * 512:(q4 + 1) * 512], in_=pb[0:rows, :]))
                                psT.release(pi, ta)
                            pool.wait(ta)
                            tst = K.dma(pool, st_o[oi], out[ts + j0:ts + j0 + rows, :], ob[0:rows, :])
                            oring.release(oi, tst)
                            j0 += rows
                        hring.release(hi, tm)
                K.barrier()

        def phase_Q():
            st = ExitStack()
            with st:
                NB = 512
                Wq = sb(nc, st, "Wq", [128, KC, D], BF16)
                Wk = sb(nc, st, "Wk", [128, KC, D], BF16)
                Wv = sb(nc, st, "Wv", [128, KC, D], BF16)
                Wqr = sb(nc, st, "Wqr", [128, KC, D], BF16)
                Wkr = sb(nc, st, "Wkr", [128, KC, D], BF16)
                hT = [sb(nc, st, "qh%d" % i, [128, KC, NB], F32) for i in range(2)]
                cs = [sb(nc, st, "cs%d" % i, [128, 2, NB], F32) for i in range(2)]
                xlT = sb(nc, st, "qxl", [128, KC, NB], BF16)
                qst = [sb(nc, st, "qst%d" % i, [128, HEADS, NB], BF16) for i in range(2)]
                kst = [sb(nc, st, "kst%d" % i, [128, HEADS, NB], BF16) for i in range(2)]
                vst = [sb(nc, st, "vst%d" % i, [128, D], BF16) for i in range(2)]
                r1 = Ring([sb(nc, st, "r1_%d" % i, [128, NB], F32) for i in range(2)])
                r2 = Ring([sb(nc, st, "r2_%d" % i, [128, NB], F32) for i in range(2)])
                bufs = {
                    "sq": Ring([sb(nc, st, "qsq%d" % i, [128, NB], BF16) for i in range(2)]),
                    "sd": sb(nc, st, "qsd", [128, NB], F32),
                    "r": sb(nc, st, "qrb", [128, NB], F32),
                    "tm": Ring([sb(nc, st, "qtm%d" % i, [128, NB], F32) for i in range(2)]),
                }
                ldw = K.dsem("ldw_Q")
                for kc in range(KC):
                    K.dma(pool, ldw, Wq[:, kc, :], w_qkv[kc * 128:(kc + 1) * 128, 0:D])
                    K.dma(pool, ldw, Wk[:, kc, :], w_qkv[kc * 128:(kc + 1) * 128, D:2 * D])
                    t_w = K.dma(pool, ldw, Wv[:, kc, :], w_qkv[kc * 128:(kc + 1) * 128, 2 * D:3 * D])
                act.wait(t_w)
                dve.wait(t_w)
                for (Wsrc, Wrot) in ((Wq, Wqr), (Wk, Wkr)):
                    sv = Wsrc[:].rearrange("p k (a t b) -> p (k a) t b", t=2, b=16)
                    dv = Wrot[:].rearrange("p k (a t b) -> p (k a) t b", t=2, b=16)
                    act.mark(nc.scalar.mul(out=dv[:, :, 0, :], in_=sv[:, :, 1, :], mul=-1.0))
                    dve.mark(nc.vector.tensor_copy(out=dv[:, :, 1, :], in_=sv[:, :, 0, :]))
                pe.wait(t_w, act.last(), dve.last())

                ldh = [K.dsem("ldh_Q%d" % i) for i in range(2)]
                stq = [K.dsem("st_Q%d" % i) for i in range(2)]
                stv = [K.dsem("st_V%d" % i) for i in range(2)]
                hring = Ring(hT)
                vring = Ring(vst)
                psQ = Ring([(PSB[0], PSB[1]), (PSB[2], PSB[3])])
                psV = Ring([PSB[4], PSB[5]])
                psS = Ring([PSB[6]])
                xl_rel = None
                H2v = H2T.rearrange("(k p) t -> p k t", p=128)
                blocks = [(S, CTX, True)] + [(t0, NB, False) for t0 in range(0, S, NB)]
                st_rel = [None, None]
                for bi, (c0, n, is_ctx) in enumerate(blocks):
                    v = 1 if is_ctx else 0
                    hi, hb = hring.get(sp)
                    t_h = K.dma(sp, ldh[hi], hb[:, :, 0:n], H2v[:, :, c0:c0 + n])
                    csb_ = cs[hi]
                    if not is_ctx:
                        K.dma(sp, ldh[hi], csb_[:, 0, 0:n], cosT_d[:, c0:c0 + n])
                        t_h = K.dma(sp, ldh[hi], csb_[:, 1, 0:n], sinT_d[:, c0:c0 + n])
                    tx = stats_modulate(lambda kc: hb[:, kc, 0:n], n, 1, v, 8, 0, lambda kc: xlT[:, kc, 0:n], bufs,
                                        [t_h], psS, extra_wait=xl_rel)
                    si = bi % 2
                    qb, kb = qst[si], kst[si]
                    dve.wait(st_rel[si])
                    act.wait(st_rel[si])
                    tq = []
                    for (W, Wr, dstb, skip) in ((Wq, Wqr, qb, is_ctx), (Wk, Wkr, kb, False)):
                        if skip:
                            continue
                        for h in range(HEADS):
                            pi, (pa, pbr) = psQ.get(pe)
                            pe.wait(tx)
                            for kc in range(KC):
                                mm = nc.tensor.matmul(pa[:, 0:n], W[:, kc, h * 128:(h + 1) * 128], xlT[:, kc, 0:n],
                                                      start=(kc == 0), stop=(kc == KC - 1))
                            if is_ctx:
                                tm = pe.mark(mm)
                                act.wait(tm)
                                ta = act.mark(nc.scalar.copy(out=dstb[:, h, 0:n], in_=pa[:, 0:n]))
                                psQ.release(pi, ta)
                                tq.append(ta)
                                continue
                            for kc in range(KC):
                                mm = nc.tensor.matmul(pbr[:, 0:n], Wr[:, kc, h * 128:(h + 1) * 128], xlT[:, kc, 0:n],
                                                      start=(kc == 0), stop=(kc == KC - 1))
                            tm = pe.mark(mm)
                            i1, b1 = r1.get(dve)
                            i2, b2 = r2.get(dve)
                            dve.wait(tm, t_h)
                            ta = dve.mark(nc.vector.tensor_tensor(out=b1[:, 0:n], in0=pa[:, 0:n], in1=csb_[:, 0, 0:n], op=ALU.mult))
                            tb = dve.mark(nc.vector.tensor_tensor(out=b2[:, 0:n], in0=pbr[:, 0:n], in1=csb_[:, 1, 0:n], op=ALU.mult))
                            psQ.release(pi, tb)
                            dve.wait(ta, tb)
                            tc_ = dve.mark(nc.vector.tensor_tensor(out=dstb[:, h, 0:n], in0=b1[:, 0:n], in1=b2[:, 0:n], op=ALU.add))
                            r1.release(i1, tc_)
                            r2.release(i2, tc_)
                            tq.append(tc_)
                    pool.wait(tq)
                    koff = 0 if is_ctx else CTX
                    kc0 = (c0 - S) if is_ctx else c0
                    if not is_ctx:
                        K.dma(pool, stq[si], QT.rearrange("h p t -> p h t")[:, :, c0:c0 + n], qb[:, :, 0:n])
                    tstq = K.dma(pool, stq[si], KT.rearrange("h p t -> p h t")[:, :, koff + kc0:koff + kc0 + n], kb[:, :, 0:n])
                    st_rel[si] = tstq
                    for j in range(n // 128):
                        vi, vb = vring.get(act)
                        for dh in range(2):
                            pi, pb = psV.get(pe)
                            pe.wait(tx)
                            for kc in range(KC):
                                mm = nc.tensor.matmul(pb[:, :], xlT[:, kc, j * 128:(j + 1) * 128], Wv[:, kc, dh * 512:(dh + 1) * 512],
                                                      start=(kc == 0), stop=(kc == KC - 1))
                            tm = pe.mark(mm)
                            act.wait(tm)
                            ta = act.mark(nc.scalar.copy(out=vb[:, dh * 512:(dh + 1) * 512], in_=pb[:, :]))
                            psV.release(pi, ta)
                        kt = (koff + kc0) // 128 + j
                        pool.wait(ta)
                        tsv = K.dma(pool, stv[vi], VH.rearrange("h p (t e) -> p h t e", e=128)[:, :, kt, :],
                                    vb[:].rearrange("p (h e) -> p h e", e=128))
                        vring.release(vi, tsv)
                    xl_rel = tm
                    hring.release(hi, tx[-1], tq[-1])
                K.barrier()

        def phase_A():
            st = ExitStack()
            with st:
                QB = 512
                kTb = [sb(nc, st, "kTb%d" % i, [128, ST], BF16) for i in range(2)]
                vHb = [sb(nc, st, "vHb%d" % i, [128, NT, 128], BF16) for i in range(2)]
                qb = [sb(nc, st, "qb%d" % i, [128, QB], BF16) for i in range(2)]
                pT = Ring([sb(nc, st, "pT%d" % i, [128, 2 * QB], BF16) for i in range(4)])
                sel = sb(nc, st, "sel", [64, 128], F32)
                sums_sb = sb(nc, st, "sums_sb", [64, QB], F32)
                pocp = sb(nc, st, "pocp", [128, 2 * QB], F32)
                rs = sb(nc, st, "rs", [128, 2 * QB], F32)
                o0 = sb(nc, st, "o0", [128, QB], F32)
                o1 = sb(nc, st, "o1", [128, QB], F32)
                osq = sb(nc, st, "osq", [128, QB], BF16)
                osd = sb(nc, st, "osd", [128, QB], F32)
                orr = sb(nc, st, "orr", [128, QB], F32)
                ob = [sb(nc, st, "aob%d" % i, [128, QB], BF16) for i in range(2)]
                subg2 = sb(nc, st, "subg2", [128, 1], F32)
                dve.mark(nc.vector.memset(sel[:], 1.0 / 32.0))
                t_init = dve.mark(nc.vector.tensor_scalar(out=subg2[:], in0=subg[:], scalar1=(1.0 - LAMBDA_INIT), scalar2=None, op0=ALU.mult))
                pe.wait(t_init)
                ldk = [K.dsem("ldk%d" % i) for i in range(2)]
                ldq = [K.dsem("ldq%d" % i) for i in range(2)]
                sto = [K.dsem("sto%d" % i) for i in range(2)]
                kring = Ring(list(zip(kTb, vHb)))
                qring = Ring(qb)
                oring = Ring(ob)
                psS = Ring([PSD[0], PSD[1]])
                po = PSD[2]
                PSx = PSB[6]
                nqb = S // QB
                units = [(h, q_) for h in range(HEADS) for q_ in range(nqb)]
                kv = {}
                qd = {}
                state = {"fin_rel": None, "acc_rel": None}

                def load_kv(h):
                    ki, (kb_, vb_) = kring.get(sp)
                    K.dma(sp, ldk[ki], kb_[:], KT[h])
                    t_k = K.dma(sp, ldk[ki], vb_[:].rearrange("p t e -> p (t e)"), VH[h])
                    kv[h] = (ki, kb_, vb_, t_k)

                def load_q(ui):
                    h, q_ = units[ui]
                    qi, qbuf = qring.get(sp)
                    t_q = K.dma(sp, ldq[qi], qbuf[:], QT[h, :, q_ * QB:(q_ + 1) * QB])
                    qd[ui] = (qi, qbuf, t_q)

                def issue_S(kb_, qbuf, kt):
                    pi, sp_ = psS.get(pe)
                    nc.tensor.matmul(sp_[:, 0:QB], kb_[0:64, kt * 128:(kt + 1) * 128], qbuf[0:64, :], start=True, stop=True)
                    ts = pe.mark(nc.tensor.matmul(sp_[:, QB:2 * QB], kb_[64:128, kt * 128:(kt + 1) * 128], qbuf[64:128, :],
                                                  start=True, stop=True))
                    ppi, pbuf = pT.get(act)
                    act.wait(ts)
                    te = act.mark(nc.scalar.activation(out=pbuf[:, :], in_=sp_[:, :], func=AF.Exp, scale=0.125))
                    psS.release(pi, te)
                    return (ppi, pbuf, te)

                def finalize(tv, h, q_):
                    dve.wait(tv, state["fin_rel"])
                    dve.mark(nc.vector.tensor_copy(out=sums_sb[:, :], in_=PSx[0:64, :]))
                    tcp = dve.mark(nc.vector.tensor_copy(out=pocp[:, :], in_=po[:, :]))
                    state["acc_rel"] = tcp
                    si, sp_ = psS.get(pe)
                    pe.wait(tcp)
                    nc.tensor.matmul(sp_[:, 0:QB], sel[0:32, :], sums_sb[0:32, :], start=True, stop=True)
                    tsum = pe.mark(nc.tensor.matmul(sp_[:, QB:2 * QB], sel[32:64, :], sums_sb[32:64, :], start=True, stop=True))
                    dve.wait(tsum)
                    t1 = dve.mark(nc.vector.reciprocal(out=rs[:, :], in_=sp_[:, :]))
                    psS.release(si, t1)
                    dve.wait(t1)
                    dve.mark(nc.vector.tensor_tensor(out=o0[:], in0=pocp[:, 0:QB], in1=rs[:, 0:QB], op=ALU.mult))
                    t2 = dve.mark(nc.vector.tensor_tensor(out=o1[:], in0=pocp[:, QB:2 * QB], in1=rs[:, QB:2 * QB], op=ALU.mult))
                    dve.wait(t2)
                    t3 = dve.mark(nc.vector.scalar_tensor_tensor(out=o0[:], in0=o1[:], scalar=neglam[:, 0:1], in1=o0[:],
                                                                 op0=ALU.mult, op1=ALU.add))
                    act.wait(t3)
                    t4 = act.mark(nc.scalar.activation(out=osq[:], in_=o0[:], func=AF.Square))
                    si2, sp2 = psS.get(pe)
                    pe.wait(t4)
                    t5 = pe.mark(nc.tensor.matmul(sp2[:, 0:QB], ones_bf[:], osq[:], start=True, stop=True))
                    act.wait(t5)
                    t6 = act.mark(nc.scalar.activation(out=osd[:], in_=sp2[:, 0:QB], func=AF.Sqrt, bias=epsb[:, 0:1], scale=1.0 / 128))
                    psS.release(si2, t6)
                    dve.wait(t6)
                    t7 = dve.mark(nc.vector.reciprocal(out=orr[:], in_=osd[:]))
                    oi2, obuf = oring.get(dve)
                    dve.wait(t7)
                    t8 = dve.mark(nc.vector.scalar_tensor_tensor(out=obuf[:], in0=o0[:], scalar=subg2[:, 0:1], in1=orr[:],
                                                                 op0=ALU.mult, op1=ALU.mult))
                    state["fin_rel"] = t8
                    pool.wait(t8)
                    tso = K.dma(pool, sto[oi2], AT[h * 128:(h + 1) * 128, q_ * QB:(q_ + 1) * QB], obuf[:])
                    oring.release(oi2, tso)

                load_kv(0)
                load_q(0)
                pending = None
                for ui, (h, q_) in enumerate(units):
                    if q_ == 0 and h + 1 < HEADS:
                        load_kv(h + 1)
                    if ui + 1 < len(units):
                        load_q(ui + 1)
                    ki, kb_, vb_, t_k = kv[h]
                    qi, qbuf, t_q = qd[ui]
                    if pending is None:
                        pe.wait(t_k, t_q)
                        cur = issue_S(kb_, qbuf, 0)
                    else:
                        cur = pending
                    for kt in range(NT):
                        nxt = issue_S(kb_, qbuf, kt + 1) if kt + 1 < NT else None
                        ppi, pbuf, te = cur
                        pe.wait(te, state["acc_rel"] if kt == 0 else None)
                        f, l_ = (kt == 0), (kt == NT - 1)
                        nc.tensor.matmul(po[:, 0:QB], vb_[:, kt, :], pbuf[:, 0:QB], start=f, stop=l_)
                        nc.tensor.matmul(po[:, QB:2 * QB], vb_[:, kt, :], pbuf[:, QB:2 * QB], start=f, stop=l_)
                        nc.tensor.matmul(PSx[0:32, :], ones_bf[:, 0:32], pbuf[:, 0:QB], start=f, stop=l_, tile_position=(0, 0))
                        tv = pe.mark(nc.tensor.matmul(PSx[32:64, :], ones_bf[:, 0:32], pbuf[:, QB:2 * QB], start=f, stop=l_,
                                                      tile_position=(0, 32)))
                        pT.release(ppi, tv)
                        cur = nxt
                    qring.release(qi, tv)
                    if q_ == nqb - 1:
                        kring.release(ki, tv)
                    if ui + 1 < len(units):
                        h2, _q2 = units[ui + 1]
                        _ki2, kb2, _vb2, t_k2 = kv[h2]
                        _qi2, qbuf2, t_q2 = qd[ui + 1]
                        pe.wait(t_k2, t_q2)
                        pending = issue_S(kb2, qbuf2, 0)
                    else:
                        pending = None
                    finalize(tv, h, q_)
                K.barrier()

        def phase_O():
            st = ExitStack()
            with st:
                NB = 512
                Wo = sb(nc, st, "Wo", [128, KC, D], BF16)
                hT = [sb(nc, st, "oh%d" % i, [128, KC, NB], F32) for i in range(2)]
                aT = [sb(nc, st, "oa%d" % i, [128, KC, NB], BF16) for i in range(2)]
                ldw = K.dsem("ldw_O")
                for kc in range(KC):
                    t_w = K.dma(pool, ldw, Wo[:, kc, :], w_o[kc * 128:(kc + 1) * 128, :])
                pe.wait(t_w)
                ldh = [K.dsem("ldh_O%d" % i) for i in range(2)]
                st_h = [K.dsem("st_O%d" % i) for i in range(2)]
                hring = Ring(hT)
                psO = Ring([PSB[0], PSB[1], PSB[2], PSB[3]])
                H2v = H2T.rearrange("(k p) t -> p k t", p=128)
                H1v = H1T.rearrange("(k p) t -> p k t", p=128)
                ATv = AT.rearrange("(k p) t -> p k t", p=128)
                for c0 in range(0, S, NB):
                    hi, hb = hring.get(sp)
                    ab = aT[hi]
                    K.dma(sp, ldh[hi], hb[:], H2v[:, :, c0:c0 + NB])
                    t_h = K.dma(sp, ldh[hi], ab[:], ATv[:, :, c0:c0 + NB])
                    for dc in range(KC):
                        pi, pb = psO.get(pe)
                        pe.wait(t_h)
                        for hc in range(KC):
                            mm = nc.tensor.matmul(pb[:, :], Wo[:, hc, dc * 128:(dc + 1) * 128], ab[:, hc, :],
                                                  start=(hc == 0), stop=(hc == KC - 1))
                        tm = pe.mark(mm)
                        dve.wait(tm, t_h)
                        td = dve.mark(nc.vector.scalar_tensor_tensor(out=hb[:, dc, :], in0=pb[:, :], scalar=modv(1, 0, 16 + dc),
                                                                     in1=hb[:, dc, :], op0=ALU.mult, op1=ALU.add))
                        psO.release(pi, td)
                    pool.wait(td)
                    tst = K.dma(pool, st_h[hi], H1v[:, :, c0:c0 + NB], hb[:])
                    hring.release(hi, tst)
                K.barrier()

        plan = [("mod", phase_mod), ("G", lambda: phase_G(3)),
                ("F0", lambda: phase_F(0, H1T, H2T, False, False, 360)),
                ("Q", phase_Q), ("A", phase_A), ("O", phase_O),
                ("F1", lambda: phase_F(1, H1T, None, False, True, 360))]
        for name, fn in plan:
            fn()
            if stop_after == name:
                break
        K.barrier()
    return nc


def _rope_tables(S):
    rows = S // 64
    row_pos = np.broadcast_to(np.arange(rows, dtype=np.float32)[:, None], (rows, 64)).reshape(-1)
    col_pos = np.broadcast_to(np.arange(64, dtype=np.float32)[None, :], (rows, 64)).reshape(-1)
    n_freq = 16
    inv_freq = (np.float32(10000.0) ** (-np.arange(n_freq, dtype=np.float32) / np.float32(n_freq))).astype(np.float32)
    ang_r = row_pos[:, None] * inv_freq
    ang_c = col_pos[:, None] * inv_freq
    ang = np.concatenate([ang_r, ang_r, ang_c, ang_c], axis=-1).astype(np.float32)
    cosT = np.ascontiguousarray(np.cos(ang).T.astype(np.float32))
    sinT = np.ascontiguousarray(np.sin(ang).T.astype(np.float32))
    return np.concatenate([cosT, cosT], 0), np.concatenate([sinT, sinT], 0)


def _fm(vec, nchunk):
    return np.ascontiguousarray(np.asarray(vec, np.float32).reshape(nchunk, 128).T)


def make_in_maps(inp, S, ncores):
    f = lambda a: np.ascontiguousarray(np.asarray(a, np.float32))
    cosT, sinT = _rope_tables(S)
    shared = {
        "ada_w": f(inp["ada_w"]),
        "ada_bT": np.ascontiguousarray(np.stack([_fm(inp["ada_b"][l], 48) for l in range(2)], 1).reshape(128, 96)),
        "gm_w_in": f(inp["gm_w_in"][0]),
        "gm_gT": _fm(inp["gm_norm_g"][0], 16),
        "gm_wsT": np.ascontiguousarray(np.transpose(f(inp["gm_w_s"][0]), (2, 0, 1)).reshape(128, 1024)),
        "gm_bs": f(inp["gm_b_s"][0]).reshape(1, 1024),
        "gm_w_out": f(inp["gm_w_out"][0]),
        "w_qkv": f(inp["da_w_qkv"][0]),
        "lamv": np.concatenate([f(inp["da_lambda_q1"][0]), f(inp["da_lambda_k1"][0]),
                                f(inp["da_lambda_q2"][0]), f(inp["da_lambda_k2"][0])]).reshape(1, 256),
        "sublnT": f(inp["da_subln_g"][0]).reshape(128, 1),
        "w_o": f(inp["da_w_out"][0]),
        "w_up": f(inp["ffn_w_up"]),
        "convT": np.ascontiguousarray(np.stack(
            [np.stack([_fm(inp["ffn_conv_w"][l][0], NCC), _fm(inp["ffn_conv_w"][l][1], NCC),
                       _fm(inp["ffn_conv_w"][l][2], NCC), _fm(inp["ffn_conv_b"][l], NCC)], 1) for l in range(2)], 1).reshape(128, 2 * 4 * NCC)),
        "w_dn": f(inp["ffn_w_down"]),
        "fngT": _fm(inp["final_norm_g"], KC),
        "ident": np.eye(128, dtype=np.float32),
        "cosT": cosT, "sinT": sinT,
    }
    cc = _fm(inp["c_ctx"], KC)
    maps = []
    for b in range(ncores):
        m = dict(shared)
        m["x"] = f(inp["x"][b])
        m["ctx"] = f(inp["ctx"][b])
        cb = _fm(inp["c"][b], KC)
        m["cvec"] = np.ascontiguousarray(np.stack([cb, cc], 2).reshape(128, 2 * KC))
        maps.append(m)
    return maps


_NC_CACHE = {}


def kernel(**inputs):
    S = int(inputs["x"].shape[1])
    B = int(inputs["x"].shape[0])
    key = (S,)
    if key not in _NC_CACHE:
        _NC_CACHE[key] = build_program(S)
    nc = _NC_CACHE[key]
    in_maps = make_in_maps(inputs, S, B)
    res = run_bass_kernel_spmd(nc, in_maps, core_ids=list(range(B)))
    return np.stack([np.asarray(r["out"], np.float32) for r in res.results], 0)
```

```python
import math
from contextlib import ExitStack

import numpy as np
import ml_dtypes
import concourse.bass as bass
import concourse.mybir as mybir
from concourse.bass_utils import run_bass_kernel_spmd

F32 = mybir.dt.float32
BF16 = mybir.dt.bfloat16
AF = mybir.ActivationFunctionType
ALU = mybir.AluOpType

D = 1024
KC = 8
CTX = 256
NCORES = 8
SEQ = 8192
EPS = 1e-6
GMW = 2048
FFN = 2816
NCC = 44
NPAIR = 22
HEADS = 8
LAMBDA_INIT = 0.8 - 0.6 * math.exp(-0.3 * 1)


class Eng:
    def __init__(self, K, name, eng):
        self.K, self.name, self.e = K, name, eng
        self.sem = K.newsem("e_" + name)
        self.cnt = 0
        self.seen = {}

    def wait(self, *tks):
        for tk in tks:
            if tk is None:
                continue
            if isinstance(tk, list):
                self.wait(*tk)
                continue
            sem, v = tk
            if self.seen.get(sem, 0) < v:
                self.e.wait_ge(sem, v)
                self.seen[sem] = v

    def mark(self, ins):
        self.cnt += 1
        ins.then_inc(self.sem, 1)
        return (self.sem, self.cnt)

    def last(self):
        return (self.sem, self.cnt) if self.cnt else None


class DSem:
    def __init__(self, K, name):
        self.sem = K.newsem(name)
        self.cnt = 0

    def tk(self):
        return (self.sem, self.cnt) if self.cnt else None


class Ring:
    def __init__(self, bufs):
        self.bufs = list(bufs)
        self.rel = [None] * len(self.bufs)
        self.i = 0

    def get(self, *engs):
        i = self.i % len(self.bufs)
        self.i += 1
        for e in engs:
            e.wait(self.rel[i])
        self.rel[i] = None
        return i, self.bufs[i]

    def release(self, i, *tks):
        cur = self.rel[i]
        lst = [] if cur is None else list(cur)
        lst.extend(t for t in tks if t is not None)
        self.rel[i] = lst


class Kern:
    def __init__(self, nc, es):
        self.nc, self.es = nc, es
        self.nsem = 0
        self.pe = Eng(self, "pe", nc.tensor)
        self.act = Eng(self, "act", nc.scalar)
        self.dve = Eng(self, "dve", nc.vector)
        self.pool = Eng(self, "pool", nc.gpsimd)
        self.sp = Eng(self, "sp", nc.sync)
        self.engs = [self.pe, self.act, self.dve, self.pool, self.sp]
        self.dsems = []

    def newsem(self, name):
        self.nsem += 1
        return self.es.enter_context(self.nc.semaphore(name))

    def dsem(self, name):
        d = DSem(self, name)
        self.dsems.append(d)
        return d

    def dma(self, q, ds, out, in_):
        ins = q.e.dma_start(out=out, in_=in_)
        ds.cnt += 16
        ins.then_inc(ds.sem, 16)
        return (ds.sem, ds.cnt)

    def barrier(self):
        tks = [e.last() for e in self.engs] + [d.tk() for d in self.dsems]
        for e in self.engs:
            e.wait(*tks)


_UID = [0]


def sb(nc, st, name, shape, dt):
    _UID[0] += 1
    return st.enter_context(nc.sbuf_tensor("s%d_%s" % (_UID[0], name), list(shape), dt))


def build_program(S=SEQ, stop_after=None, dbg=False):
    nc = bass.Bass("TRN2", target_bir_lowering=False)
    ST = S + CTX
    NT = ST // 128
    kind_scr = "ExternalOutput" if dbg else "Internal"

    def din(name, shape, dt=F32):
        return nc.dram_tensor(name, list(shape), dt, kind="ExternalInput").ap()

    def dscr(name, shape, dt):
        return nc.dram_tensor(name, list(shape), dt, kind=kind_scr).ap()

    x = din("x", [S, D])
    ctx = din("ctx", [CTX, D])
    cvec = din("cvec", [128, KC * 2])
    ada_w = din("ada_w", [2, D, 6 * D])
    ada_bT = din("ada_bT", [128, 2 * 48])
    gm_w_in = din("gm_w_in", [D, 2 * GMW])
    gm_gT = din("gm_gT", [128, 16])
    gm_wsT = din("gm_wsT", [128, 8 * 128])
    gm_bs = din("gm_bs", [1, 8 * 128])
    gm_w_out = din("gm_w_out", [GMW, D])
    w_qkv = din("w_qkv", [D, 3 * D])
    lamv = din("lamv", [1, 256])
    sublnT = din("sublnT", [128, 1])
    w_o = din("w_o", [D, D])
    w_up = din("w_up", [2, D, 2 * FFN])
    convT = din("convT", [128, 2 * 4 * NCC])
    w_dn = din("w_dn", [2, FFN, D])
    fngT = din("fngT", [128, KC])
    ident_d = din("ident", [128, 128])
    cosT_d = din("cosT", [128, S])
    sinT_d = din("sinT", [128, S])
    out = nc.dram_tensor("out", [S, D], F32, kind="ExternalOutput").ap()

    H1T = dscr("H1T", [D, ST], F32)
    H2T = dscr("H2T", [D, ST], F32)
    QT = dscr("QT", [HEADS, 128, S], BF16)
    KT = dscr("KT", [HEADS, 128, ST], BF16)
    VH = dscr("VH", [HEADS, 128, NT * 128], BF16)
    AT = dscr("AT", [D, S], BF16)

    es = ExitStack()
    with es:
        K = Kern(nc, es)
        pe, act, dve, pool, sp = K.pe, K.act, K.dve, K.pool, K.sp

        PSD = [es.enter_context(nc.psum_tensor("psd%d" % i, [128, 1024], F32)) for i in range(4)]
        PSB = [PSD[i // 2][:, (i % 2) * 512:(i % 2 + 1) * 512] for i in range(8)]
        ident = sb(nc, es, "ident", [128, 128], F32)
        ones_bf = sb(nc, es, "ones_bf", [128, 128], BF16)
        modT = sb(nc, es, "modT", [128, 2, 2, 48], F32)
        convS = sb(nc, es, "convS", [128, 2, 4, NCC], F32)
        gmg = sb(nc, es, "gmg", [128, 16], F32)
        subg = sb(nc, es, "subg", [128, 1], F32)
        fng = sb(nc, es, "fng", [128, KC], F32)
        neglam = sb(nc, es, "neglam", [128, 1], F32)
        epsb = sb(nc, es, "epsb", [128, 1], F32)

        ld_c = K.dsem("ld_c")
        K.dma(sp, ld_c, ident[:], ident_d[:])
        K.dma(sp, ld_c, convS[:].rearrange("p a b c -> p (a b c)"), convT[:])
        K.dma(sp, ld_c, gmg[:], gm_gT[:])
        K.dma(sp, ld_c, subg[:], sublnT[:])
        t_const = K.dma(sp, ld_c, fng[:], fngT[:])
        dve.mark(nc.vector.memset(ones_bf[:], 1.0))
        t_ones = dve.mark(nc.vector.memset(epsb[:], EPS))
        for e in (pe, act, dve):
            e.wait(t_const, t_ones)

        def modv(l, v, j):
            return modT[:, l, v, j:j + 1]

        def phase_mod():
            st = ExitStack()
            with st:
                c32 = sb(nc, st, "c32", [128, KC * 2], F32)
                csb = sb(nc, st, "csb", [128, KC, 2], BF16)
                abT = sb(nc, st, "abT", [128, 2, 48], F32)
                lam_s = sb(nc, st, "lam_s", [1, 256], F32)
                lam_p = sb(nc, st, "lam_p", [1, 128], F32)
                lam_r = sb(nc, st, "lam_r", [1, 4], F32)
                ones1 = sb(nc, st, "ones1", [1, 128], F32)
                wa = [sb(nc, st, "wa%d" % i, [128, KC, 1024], BF16) for i in range(2)]
                wring = Ring(wa)
                ld = K.dsem("ld_mod")
                ldw = [K.dsem("ld_wa%d" % i) for i in range(2)]
                K.dma(sp, ld, c32[:], cvec[:])
                K.dma(sp, ld, abT[:].rearrange("p a b -> p (a b)"), ada_bT[:])
                t_in = K.dma(sp, ld, lam_s[:], lamv[:])
                act.wait(t_in)
                t_c = act.mark(nc.scalar.activation(out=csb[:].rearrange("p a b -> p (a b)"), in_=c32[:], func=AF.Silu))
                dve.wait(t_in)
                dve.mark(nc.vector.memset(ones1[:], 1.0))
                dve.mark(nc.vector.tensor_tensor(out=lam_p[:, 0:64], in0=lam_s[:, 0:64], in1=lam_s[:, 64:128], op=ALU.mult))
                t1 = dve.mark(nc.vector.tensor_tensor(out=lam_p[:, 64:128], in0=lam_s[:, 128:192], in1=lam_s[:, 192:256], op=ALU.mult))
                dve.wait(t1)
                t2 = dve.mark(nc.vector.reduce_sum(out=lam_r[:, 0:2], in_=lam_p[:].rearrange("p (a b) -> p a b", a=2),
                                                   axis=mybir.AxisListType.X))
                act.wait(t2)
                t3 = act.mark(nc.scalar.activation(out=lam_r[:, 2:4], in_=lam_r[:, 0:2], func=AF.Exp))
                dve.wait(t3)
                t4 = dve.mark(nc.vector.tensor_tensor(out=lam_r[:, 0:1], in0=lam_r[:, 3:4], in1=lam_r[:, 2:3], op=ALU.subtract))
                dve.wait(t4)
                t5 = dve.mark(nc.vector.tensor_scalar(out=lam_r[:, 1:2], in0=lam_r[:, 0:1], scalar1=-LAMBDA_INIT, scalar2=None, op0=ALU.add))
                pe.wait(t5)
                t6 = pe.mark(nc.tensor.matmul(PSB[7][:, 0:1], ones1[:], lam_r[:, 1:2], start=True, stop=True))
                dve.wait(t6)
                t7 = dve.mark(nc.vector.tensor_copy(out=neglam[:], in_=PSB[7][:, 0:1]))
                pring = Ring([PSB[0], PSB[1]])
                for l in range(2):
                    for cg in range(6):
                        wi, wbuf = wring.get(pool)
                        for kc in range(KC):
                            tw = K.dma(pool, ldw[wi], wbuf[:, kc, :], ada_w[l, kc * 128:(kc + 1) * 128, cg * 1024:(cg + 1) * 1024])
                        pi, pb = pring.get(pe)
                        pe.wait(tw, t_c)
                        for fcl in range(8):
                            for kc in range(KC):
                                mm = nc.tensor.matmul(pb[:, fcl * 2:fcl * 2 + 2], wbuf[:, kc, fcl * 128:(fcl + 1) * 128],
                                                      csb[:, kc, :], start=(kc == 0), stop=(kc == KC - 1))
                        tm = pe.mark(mm)
                        wring.release(wi, tm)
                        dve.wait(tm, t_in)
                        td = dve.mark(nc.vector.tensor_tensor(
                            out=modT[:, l, :, cg * 8:(cg + 1) * 8],
                            in0=pb[:, 0:16].rearrange("p (f v) -> p v f", v=2),
                            in1=abT[:, l, cg * 8:(cg + 1) * 8].unsqueeze(1).to_broadcast([128, 2, 8]),
                            op=ALU.add))
                        pring.release(pi, td)
                dve.wait(dve.last())
                for l in range(2):
                    for j0 in (8, 32):
                        dve.mark(nc.vector.tensor_scalar(out=modT[:, l, :, j0:j0 + 8], in0=modT[:, l, :, j0:j0 + 8],
                                                         scalar1=1.0, scalar2=None, op0=ALU.add))
                K.barrier()

        def stats_modulate(hsrc, n, l, v, jsc, jsh, xdst, bufs, t_h, ps_ring, extra_wait=None):
            sqr, sdb, rb, tmr = bufs["sq"], bufs["sd"], bufs["r"], bufs["tm"]
            pi, pb = ps_ring.get(pe)
            for kc in range(KC):
                si, sq = sqr.get(act)
                act.wait(t_h)
                ts = act.mark(nc.scalar.activation(out=sq[:, 0:n], in_=hsrc(kc), func=AF.Square))
                pe.wait(ts)
                mm = nc.tensor.matmul(pb[:, 0:n], ones_bf[:], sq[:, 0:n], start=(kc == 0), stop=(kc == KC - 1))
                tm = pe.mark(mm)
                sqr.release(si, tm)
            act.wait(tm, bufs.get("sd_rel"))
            t_sd = act.mark(nc.scalar.activation(out=sdb[:, 0:n], in_=pb[:, 0:n], func=AF.Sqrt, bias=epsb[:, 0:1], scale=1.0 / D))
            ps_ring.release(pi, t_sd)
            dve.wait(t_sd, bufs.get("r_rel"))
            t_r = dve.mark(nc.vector.reciprocal(out=rb[:, 0:n], in_=sdb[:, 0:n]))
            bufs["sd_rel"] = t_r
            tks = []
            for kc in range(KC):
                ti, tmb = tmr.get(dve)
                dve.wait(t_r, t_h)
                tt = dve.mark(nc.vector.scalar_tensor_tensor(out=tmb[:, 0:n], in0=hsrc(kc), scalar=modv(l, v, jsc + kc),
                                                             in1=rb[:, 0:n], op0=ALU.mult, op1=ALU.mult))
                act.wait(tt, extra_wait)
                ta = act.mark(nc.scalar.activation(out=xdst(kc), in_=tmb[:, 0:n], func=AF.Identity,
                                                   bias=modv(l, v, jsh + kc), scale=1.0))
                tmr.release(ti, ta)
                tks.append(ta)
            bufs["r_rel"] = tt
            return tks

        def phase_G(TBG=3):
            st = ExitStack()
            with st:
                NB = TBG * 128
                WinU = sb(nc, st, "WinU", [128, KC, GMW], BF16)
                WinV = sb(nc, st, "WinV", [128, KC, GMW], BF16)
                WsT = sb(nc, st, "WsT", [128, 8, 128], BF16)
                Wout = sb(nc, st, "Wout", [128, 16, D], BF16)
                bsr = sb(nc, st, "bsr", [128, 8, 128], F32)
                xt = [sb(nc, st, "xt%d" % i, [128, D], F32) for i in range(TBG)]
                hT = [sb(nc, st, "hT%d" % i, [128, KC, NB], F32) for i in range(2)]
                xlT = sb(nc, st, "xlT", [128, KC, NB], BF16)
                uT = sb(nc, st, "uT", [128, 16, NB], BF16)
                gv = [sb(nc, st, "gv%d" % i, [128, GMW], F32) for i in range(2)]
                vn = sb(nc, st, "vn", [128, TBG, GMW], BF16)
                junk = sb(nc, st, "junk", [128, GMW], BF16)
                vst = sb(nc, st, "vst", [128, 4 * TBG * 2], F32)
                bufs = {
                    "sq": Ring([sb(nc, st, "sq%d" % i, [128, NB], BF16) for i in range(2)]),
                    "sd": sb(nc, st, "sdb", [128, NB], F32),
                    "r": sb(nc, st, "rb", [128, NB], F32),
                    "tm": Ring([sb(nc, st, "tm%d" % i, [128, NB], F32) for i in range(2)]),
                }
                t2r = Ring([sb(nc, st, "t2_%d" % i, [128, NB], F32) for i in range(2)])
                ldw = K.dsem("ldw_G")
                for kc in range(KC):
                    K.dma(pool, ldw, WinU[:, kc, :], gm_w_in[kc * 128:(kc + 1) * 128, 0:GMW])
                    K.dma(pool, ldw, WinV[:, kc, :], gm_w_in[kc * 128:(kc + 1) * 128, GMW:2 * GMW])
                for fc in range(16):
                    K.dma(pool, ldw, Wout[:, fc, :], gm_w_out[fc * 128:(fc + 1) * 128, :])
                K.dma(pool, ldw, WsT[:].rearrange("p a b -> p (a b)"), gm_wsT[:])
                t_w = K.dma(pool, ldw, bsr[:].rearrange("p a b -> p (a b)"), gm_bs.partition_broadcast(128))
                pe.wait(t_w)
                dve.wait(t_w)

                ldx = [K.dsem("ldx_G%d" % i) for i in range(TBG)]
                st_h = [K.dsem("st_G%d" % i) for i in range(2)]
                xring = Ring(xt)
                hring = Ring(hT)
                psT = Ring([PSB[0], PSB[1]])
                psA = Ring([PSB[2], PSB[3]])
                psM = Ring([PSB[4], PSB[5]])
                psO = Ring([PSB[6], PSB[7]])
                gvr = Ring(gv)
                vn_rel = None
                xl_rel = None
                u_rel = None

                blocks = []
                t0 = 0
                while t0 < S:
                    nt = min(TBG, (S - t0) // 128)
                    blocks.append((x, t0, t0, nt, 0))
                    t0 += nt * 128
                blocks.append((ctx, 0, S, CTX // 128, 1))

                for bidx, (src, r0, c0, nt, v) in enumerate(blocks):
                    n = nt * 128
                    tl = []
                    for j in range(nt):
                        xi, xb = xring.get(sp)
                        tl.append((xi, xb, K.dma(sp, ldx[xi], xb[:], src[r0 + j * 128:r0 + (j + 1) * 128, :])))
                    hi, hb = hring.get(act)
                    tev = []
                    for kc in range(KC):
                        pi, pb = psT.get(pe)
                        for j in range(nt):
                            pe.wait(tl[j][2])
                            mm = nc.tensor.transpose(pb[:, j * 128:(j + 1) * 128], tl[j][1][:, kc * 128:(kc + 1) * 128], ident[:])
                        tm = pe.mark(mm)
                        act.wait(tm)
                        ta = act.mark(nc.scalar.copy(out=hb[:, kc, 0:n], in_=pb[:, 0:n]))
                        psT.release(pi, ta)
                        tev.append(ta)
                    for j in range(nt):
                        xring.release(tl[j][0], tm)
                    tx = stats_modulate(lambda kc: hb[:, kc, 0:n], n, 0, v, 8, 0, lambda kc: xlT[:, kc, 0:n], bufs,
                                        tev, psT, extra_wait=xl_rel)
                    tvn = []
                    for j in range(nt):
                        gi, gb = gvr.get(act)
                        for cb in range(4):
                            pi, pb = psA.get(pe)
                            pe.wait(tx)
                            for kc in range(KC):
                                mm = nc.tensor.matmul(pb[:, :], xlT[:, kc, j * 128:(j + 1) * 128], WinV[:, kc, cb * 512:(cb + 1) * 512],
                                                      start=(kc == 0), stop=(kc == KC - 1))
                            tm = pe.mark(mm)
                            act.wait(tm)
                            ta = act.mark(nc.scalar.activation(out=gb[:, cb * 512:(cb + 1) * 512], in_=pb[:, :], func=AF.Gelu_apprx_tanh))
                            psA.release(pi, ta)
                        act.wait(ta)
                        c = j * 4 + (bidx % 2) * 4 * TBG
                        ts = act.mark(nc.scalar.activation(out=junk[:], in_=gb[:], func=AF.Square, accum_out=vst[:, c:c + 1]))
                        act.wait(ts)
                        ts2 = act.mark(nc.scalar.activation(out=vst[:, c + 1:c + 2], in_=vst[:, c:c + 1], func=AF.Sqrt,
                                                            bias=epsb[:, 0:1], scale=1.0 / GMW))
                        dve.wait(ts2)
                        tr = dve.mark(nc.vector.reciprocal(out=vst[:, c + 2:c + 3], in_=vst[:, c + 1:c + 2]))
                        dve.wait(tr, vn_rel if j == 0 else None)
                        tn = dve.mark(nc.vector.tensor_scalar(out=vn[:, j, :], in0=gb[:], scalar1=vst[:, c + 2:c + 3], scalar2=None,
                                                              op0=ALU.mult))
                        gvr.release(gi, tn)
                        tvn.append(tn)
                    tu = []
                    for fc in range(16):
                        pi, pb = psA.get(pe)
                        pe.wait(tx)
                        for kc in range(KC):
                            mm = nc.tensor.matmul(pb[:, 0:n], WinU[:, kc, fc * 128:(fc + 1) * 128], xlT[:, kc, 0:n],
                                                  start=(kc == 0), stop=(kc == KC - 1))
                        tm = pe.mark(mm)
                        act.wait(tm, u_rel)
                        ta = act.mark(nc.scalar.activation(out=uT[:, fc, 0:n], in_=pb[:, 0:n], func=AF.Gelu_apprx_tanh))
                        psA.release(pi, ta)
                        tu.append(ta)
                    xl_rel = tm
                    tuv = []
                    for fc in range(16):
                        g = fc // 2
                        pi, pb = psM.get(pe)
                        pe.wait(tvn)
                        for j in range(nt):
                            mm = nc.tensor.matmul(pb[:, j * 128:(j + 1) * 128], vn[:, j, fc * 128:(fc + 1) * 128], WsT[:, g, :],
                                                  start=True, stop=True)
                        tm = pe.mark(mm)
                        ti, tb = t2r.get(dve)
                        dve.wait(tm)
                        td = dve.mark(nc.vector.scalar_tensor_tensor(
                            out=tb[:, 0:n].rearrange("p (a b) -> p a b", a=nt),
                            in0=pb[:, 0:n].rearrange("p (a b) -> p a b", a=nt), scalar=gmg[:, fc:fc + 1],
                            in1=bsr[:, g, :].unsqueeze(1).to_broadcast([128, nt, 128]), op0=ALU.mult, op1=ALU.add))
                        psM.release(pi, td)
                        dve.wait(td, tu[fc])
                        td2 = dve.mark(nc.vector.tensor_tensor(out=uT[:, fc, 0:n], in0=tb[:, 0:n], in1=uT[:, fc, 0:n], op=ALU.mult))
                        t2r.release(ti, td2)
                        tuv.append(td2)
                    vn_rel = tm
                    for dc in range(KC):
                        pi, pb = psO.get(pe)
                        pe.wait(tuv)
                        for fc in range(16):
                            mm = nc.tensor.matmul(pb[:, 0:n], Wout[:, fc, dc * 128:(dc + 1) * 128], uT[:, fc, 0:n],
                                                  start=(fc == 0), stop=(fc == 15))
                        tm = pe.mark(mm)
                        dve.wait(tm, tev[dc])
                        td = dve.mark(nc.vector.scalar_tensor_tensor(out=hb[:, dc, 0:n], in0=pb[:, 0:n], scalar=modv(0, v, 16 + dc),
                                                                     in1=hb[:, dc, 0:n], op0=ALU.mult, op1=ALU.add))
                        psO.release(pi, td)
                    u_rel = tm
                    pool.wait(td)
                    tst = K.dma(pool, st_h[hi], H1T.rearrange("(k p) t -> p k t", p=128)[:, :, c0:c0 + n], hb[:, :, 0:n])
                    hring.release(hi, tst)
                K.barrier()

        def phase_F(l, SRC, DST, with_o, final, NV):
            st = ExitStack()
            with st:
                NCOL = NV + 2
                Wup = sb(nc, st, "Wup", [128, KC, 2 * FFN], BF16)
                Wdn = sb(nc, st, "Wdn", [128, NPAIR, D], BF16)
                hT = [sb(nc, st, "fh%d" % i, [128, KC, NCOL], F32) for i in range(2)]
                x2T = sb(nc, st, "x2T", [128, KC, NCOL], BF16)
                actT = sb(nc, st, "actT", [128, NPAIR, NCOL], BF16)
                zc = Ring([sb(nc, st, "zc%d" % i, [128, NCOL], F32) for i in range(2)])
                sa = Ring([sb(nc, st, "sa%d" % i, [128, NCOL], F32) for i in range(2)])
                bufs = {
                    "sq": Ring([sb(nc, st, "fsq%d" % i, [128, NCOL], BF16) for i in range(2)]),
                    "sd": sb(nc, st, "fsd", [128, NCOL], F32),
                    "r": sb(nc, st, "frb", [128, NCOL], F32),
                    "tm": Ring([sb(nc, st, "ftm%d" % i, [128, NCOL], F32) for i in range(2)]),
                }
                ldw = K.dsem("ldw_F%d" % l)
                for kc in range(KC):
                    K.dma(pool, ldw, Wup[:, kc, :], w_up[l, kc * 128:(kc + 1) * 128, :])
                for c in range(NPAIR):
                    t_w = K.dma(pool, ldw, Wdn[:, c, :], w_dn[l, c * 128:(c + 1) * 128, :])
                if with_o:
                    Wo = sb(nc, st, "Wo", [128, KC, D], BF16)
                    aT = [sb(nc, st, "aT%d" % i, [128, KC, NCOL], BF16) for i in range(2)]
                    for kc in range(KC):
                        t_w = K.dma(pool, ldw, Wo[:, kc, :], w_o[kc * 128:(kc + 1) * 128, :])
                if final:
                    ost = [sb(nc, st, "ost%d" % i, [128, D], F32) for i in range(2)]
                    oring = Ring(ost)
                    st_o = [K.dsem("st_o%d" % i) for i in range(2)]
                pe.wait(t_w)
                ldh = [K.dsem("ldh_F%d_%d" % (l, i)) for i in range(2)]
                st_h = [K.dsem("st_F%d_%d" % (l, i)) for i in range(2)]
                hring = Ring(hT)
                psZ = Ring([PSB[0], PSB[1], PSB[2]])
                psO = Ring([PSB[3], PSB[4]])
                psS = Ring([PSB[5]])
                psT = Ring([PSB[6], PSB[7]])
                x2_rel = None
                a_rel = None
                for i in range(2):
                    dve.mark(nc.vector.memset(hT[i][:].rearrange("p a b -> p (a b)"), 0.0))
                    if with_o:
                        dve.mark(nc.vector.memset(aT[i][:].rearrange("p a b -> p (a b)"), 0.0))
                t_ms = dve.last()

                segs = [(0, S)] + ([] if final else [(S, CTX)])
                blocks = []
                for (base, L) in segs:
                    nb = -(-L // NV)
                    nv = -(-L // nb)
                    ts = 0
                    while ts < L:
                        blocks.append((base, L, ts, min(nv, L - ts), 1 if base == S else 0))
                        ts += nv
                SRCv = SRC.rearrange("(k p) t -> p k t", p=128)
                if DST is not None:
                    DSTv = DST.rearrange("(k p) t -> p k t", p=128)
                ATv = AT.rearrange("(k p) t -> p k t", p=128)

                for (base, L, ts, nv, v) in blocks:
                    ncol = nv + 2
                    lo = max(ts - 1, 0)
                    hi_ = min(ts + nv + 1, L)
                    c_lo = lo - (ts - 1)
                    c_hi = c_lo + (hi_ - lo)
                    hi, hb = hring.get(sp)
                    sp.wait(t_ms)
                    t_h = K.dma(sp, ldh[hi], hb[:, :, c_lo:c_hi], SRCv[:, :, base + lo:base + hi_])
                    if with_o:
                        ab = aT[hi]
                        t_h = K.dma(sp, ldh[hi], ab[:, :, c_lo:c_hi], ATv[:, :, lo:hi_])
                        tres = []
                        for dc in range(KC):
                            pi, pb = psO.get(pe)
                            pe.wait(t_h)
                            for hc in range(KC):
                                mm = nc.tensor.matmul(pb[:, 0:ncol], Wo[:, hc, dc * 128:(dc + 1) * 128], ab[:, hc, 0:ncol],
                                                      start=(hc == 0), stop=(hc == KC - 1))
                            tm = pe.mark(mm)
                            dve.wait(tm, t_h)
                            td = dve.mark(nc.vector.scalar_tensor_tensor(out=hb[:, dc, 0:ncol], in0=pb[:, 0:ncol],
                                                                         scalar=modv(l, v, 16 + dc), in1=hb[:, dc, 0:ncol],
                                                                         op0=ALU.mult, op1=ALU.add))
                            psO.release(pi, td)
                            tres.append(td)
                        t_hh = tres
                    else:
                        t_hh = [t_h]
                    tx = stats_modulate(lambda kc: hb[:, kc, 0:ncol], ncol, l, v, 32, 24, lambda kc: x2T[:, kc, 0:ncol], bufs,
                                        t_hh, psS, extra_wait=x2_rel)
                    tz = []
                    if c_lo > 0:
                        dve.wait(tx)
                        tz.append(dve.mark(nc.vector.memset(x2T[:, :, 0:1], 0.0)))
                    if c_hi < ncol:
                        dve.wait(tx)
                        tz.append(dve.mark(nc.vector.memset(x2T[:, :, ncol - 1:ncol], 0.0)))
                    tact = []
                    for i in range(NPAIR):
                        res = []
                        for half in range(2):
                            cc = i + half * NPAIR
                            pi, pb = psZ.get(pe)
                            pe.wait(tx, tz)
                            for kc in range(KC):
                                mm = nc.tensor.matmul(pb[:, 0:ncol], Wup[:, kc, cc * 128:(cc + 1) * 128], x2T[:, kc, 0:ncol],
                                                      start=(kc == 0), stop=(kc == KC - 1))
                            tm = pe.mark(mm)
                            zi, zb = zc.get(act)
                            act.wait(tm)
                            ta = act.mark(nc.scalar.activation(out=zb[:, 0:ncol], in_=pb[:, 0:ncol], func=AF.Identity,
                                                               bias=convS[:, l, 3, cc:cc + 1], scale=convS[:, l, 1, cc:cc + 1]))
                            dve.wait(ta)
                            td1 = dve.mark(nc.vector.scalar_tensor_tensor(out=zb[:, 1:nv + 1], in0=pb[:, 0:nv],
                                                                          scalar=convS[:, l, 0, cc:cc + 1], in1=zb[:, 1:nv + 1],
                                                                          op0=ALU.mult, op1=ALU.add))
                            dve.wait(td1)
                            td2 = dve.mark(nc.vector.scalar_tensor_tensor(out=zb[:, 1:nv + 1], in0=pb[:, 2:nv + 2],
                                                                          scalar=convS[:, l, 2, cc:cc + 1], in1=zb[:, 1:nv + 1],
                                                                          op0=ALU.mult, op1=ALU.add))
                            psZ.release(pi, td2)
                            res.append((zi, zb, td2))
                        (zia, zba, tda), (zig, zbg, tdg) = res
                        si, sbuf_ = sa.get(act)
                        act.wait(tda)
                        tsl = act.mark(nc.scalar.activation(out=sbuf_[:, 1:nv + 1], in_=zba[:, 1:nv + 1], func=AF.Silu))
                        zc.release(zia, tsl)
                        dve.wait(tsl, tdg, a_rel if i == 0 else None)
                        tg = dve.mark(nc.vector.tensor_tensor(out=actT[:, i, 1:nv + 1], in0=sbuf_[:, 1:nv + 1], in1=zbg[:, 1:nv + 1],
                                                              op=ALU.mult))
                        zc.release(zig, tg)
                        sa.release(si, tg)
                        tact.append(tg)
                    x2_rel = tm
                    tdn = []
                    for dc in range(KC):
                        pi, pb = psO.get(pe)
                        pe.wait(tact)
                        for c in range(NPAIR):
                            mm = nc.tensor.matmul(pb[:, 0:nv], Wdn[:, c, dc * 128:(dc + 1) * 128], actT[:, c, 1:nv + 1],
                                                  start=(c == 0), stop=(c == NPAIR - 1))
                        tm = pe.mark(mm)
                        dve.wait(tm)
                        td = dve.mark(nc.vector.scalar_tensor_tensor(out=hb[:, dc, 1:nv + 1], in0=pb[:, 0:nv], scalar=modv(l, v, 40 + dc),
                                                                     in1=hb[:, dc, 1:nv + 1], op0=ALU.mult, op1=ALU.add))
                        psO.release(pi, td)
                        tdn.append(td)
                    a_rel = tm
                    if not final:
                        pool.wait(tdn)
                        tst = K.dma(pool, st_h[hi], DSTv[:, :, base + ts:base + ts + nv], hb[:, :, 1:nv + 1])
                        hring.release(hi, tst)
                    else:
                        pi, pb = psS.get(pe)
                        for kc in range(KC):
                            si, sq = bufs["sq"].get(act)
                            act.wait(tdn[kc])
                            tsq = act.mark(nc.scalar.activation(out=sq[:, 0:nv], in_=hb[:, kc, 1:nv + 1], func=AF.Square))
                            pe.wait(tsq)
                            tm = pe.mark(nc.tensor.matmul(pb[:, 0:nv], ones_bf[:], sq[:, 0:nv], start=(kc == 0), stop=(kc == KC - 1)))
                            bufs["sq"].release(si, tm)
                        act.wait(tm, bufs.get("sd_rel"))
                        t_sd = act.mark(nc.scalar.activation(out=bufs["sd"][:, 0:nv], in_=pb[:, 0:nv], func=AF.Sqrt,
                                                             bias=epsb[:, 0:1], scale=1.0 / D))
                        psS.release(pi, t_sd)
                        dve.wait(t_sd, bufs.get("r_rel"))
                        t_r = dve.mark(nc.vector.reciprocal(out=bufs["r"][:, 0:nv], in_=bufs["sd"][:, 0:nv]))
                        bufs["sd_rel"] = t_r
                        tyn = []
                        for kc in range(KC):
                            dve.wait(t_r, tdn[kc])
                            tyn.append(dve.mark(nc.vector.scalar_tensor_tensor(out=hb[:, kc, 1:nv + 1], in0=hb[:, kc, 1:nv + 1],
                                                                               scalar=fng[:, kc:kc + 1], in1=bufs["r"][:, 0:nv],
                                                                               op0=ALU.mult, op1=ALU.mult)))
                        bufs["r_rel"] = tyn[-1]
                        j0 = 0
                        while j0 < nv:
                            rows = min(128, nv - j0)
                            oi, ob = oring.get(act)
                            for q4 in range(2):
                                pi, pb = psT.get(pe)
                                pe.wait(tyn)
                                for kk in range(4):
                                    kc = q4 * 4 + kk
                                    mm = nc.tensor.transpose(pb[0:rows, kk * 128:(kk + 1) * 128], hb[:, kc, 1 + j0:1 + j0 + rows], ident[:])
                                tm = pe.mark(mm)
                                act.wait(tm)
                                ta = act.mark(nc.scalar.copy(out=ob[0:rows, q4 * 512:(q4 + 1) * 512], in_=pb[0:rows, :]))
                                psT.release(pi, ta)
                            pool.wait(ta)
                            tst = K.dma(pool, st_o[oi], out[ts + j0:ts + j0 + rows, :], ob[0:rows, :])
                            oring.release(oi, tst)
                            j0 += rows
                        hring.release(hi, tm)
                K.barrier()

        def phase_Q():
            st = ExitStack()
            with st:
                NB = 512
                Wq = sb(nc, st, "Wq", [128, KC, D], BF16)
                Wk = sb(nc, st, "Wk", [128, KC, D], BF16)
                Wv = sb(nc, st, "Wv", [128, KC, D], BF16)
                Wqr = sb(nc, st, "Wqr", [128, KC, D], BF16)
                Wkr = sb(nc, st, "Wkr", [128, KC, D], BF16)
                hT = [sb(nc, st, "qh%d" % i, [128, KC, NB], F32) for i in range(2)]
                cs = [sb(nc, st, "cs%d" % i, [128, 2, NB], F32) for i in range(2)]
                xlT = sb(nc, st, "qxl", [128, KC, NB], BF16)
                qst = [sb(nc, st, "qst%d" % i, [128, HEADS, NB], BF16) for i in range(2)]
                kst = [sb(nc, st, "kst%d" % i, [128, HEADS, NB], BF16) for i in range(2)]
                vst = [sb(nc, st, "vst%d" % i, [128, D], BF16) for i in range(2)]
                r1 = Ring([sb(nc, st, "r1_%d" % i, [128, NB], F32) for i in range(2)])
                r2 = Ring([sb(nc, st, "r2_%d" % i, [128, NB], F32) for i in range(2)])
                bufs = {
                    "sq": Ring([sb(nc, st, "qsq%d" % i, [128, NB], BF16) for i in range(2)]),
                    "sd": sb(nc, st, "qsd", [128, NB], F32),
                    "r": sb(nc, st, "qrb", [128, NB], F32),
                    "tm": Ring([sb(nc, st, "qtm%d" % i, [128, NB], F32) for i in range(2)]),
                }
                ldw = K.dsem("ldw_Q")
                for kc in range(KC):
                    K.dma(pool, ldw, Wq[:, kc, :], w_qkv[kc * 128:(kc + 1) * 128, 0:D])
                    K.dma(pool, ldw, Wk[:, kc, :], w_qkv[kc * 128:(kc + 1) * 128, D:2 * D])
                    t_w = K.dma(pool, ldw, Wv[:, kc, :], w_qkv[kc * 128:(kc + 1) * 128, 2 * D:3 * D])
                act.wait(t_w)
                dve.wait(t_w)
                for (Wsrc, Wrot) in ((Wq, Wqr), (Wk, Wkr)):
                    sv = Wsrc[:].rearrange("p k (a t b) -> p (k a) t b", t=2, b=16)
                    dv = Wrot[:].rearrange("p k (a t b) -> p (k a) t b", t=2, b=16)
                    act.mark(nc.scalar.mul(out=dv[:, :, 0, :], in_=sv[:, :, 1, :], mul=-1.0))
                    dve.mark(nc.vector.tensor_copy(out=dv[:, :, 1, :], in_=sv[:, :, 0, :]))
                pe.wait(t_w, act.last(), dve.last())

                ldh = [K.dsem("ldh_Q%d" % i) for i in range(2)]
                stq = [K.dsem("st_Q%d" % i) for i in range(2)]
                stv = [K.dsem("st_V%d" % i) for i in range(2)]
                hring = Ring(hT)
                vring = Ring(vst)
                psQ = Ring([(PSB[0], PSB[1]), (PSB[2], PSB[3])])
                psV = Ring([PSB[4], PSB[5]])
                psS = Ring([PSB[6]])
                xl_rel = None
                H2v = H2T.rearrange("(k p) t -> p k t", p=128)
                blocks = [(S, CTX, True)] + [(t0, NB, False) for t0 in range(0, S, NB)]
                st_rel = [None, None]
                for bi, (c0, n, is_ctx) in enumerate(blocks):
                    v = 1 if is_ctx else 0
                    hi, hb = hring.get(sp)
                    t_h = K.dma(sp, ldh[hi], hb[:, :, 0:n], H2v[:, :, c0:c0 + n])
                    csb_ = cs[hi]
                    if not is_ctx:
                        K.dma(sp, ldh[hi], csb_[:, 0, 0:n], cosT_d[:, c0:c0 + n])
                        t_h = K.dma(sp, ldh[hi], csb_[:, 1, 0:n], sinT_d[:, c0:c0 + n])
                    tx = stats_modulate(lambda kc: hb[:, kc, 0:n], n, 1, v, 8, 0, lambda kc: xlT[:, kc, 0:n], bufs,
                                        [t_h], psS, extra_wait=xl_rel)
                    si = bi % 2
                    qb, kb = qst[si], kst[si]
                    dve.wait(st_rel[si])
                    act.wait(st_rel[si])
                    tq = []
                    for (W, Wr, dstb, skip) in ((Wq, Wqr, qb, is_ctx), (Wk, Wkr, kb, False)):
                        if skip:
                            continue
                        for h in range(HEADS):
                            pi, (pa, pbr) = psQ.get(pe)
                            pe.wait(tx)
                            for kc in range(KC):
                                mm = nc.tensor.matmul(pa[:, 0:n], W[:, kc, h * 128:(h + 1) * 128], xlT[:, kc, 0:n],
                                                      start=(kc == 0), stop=(kc == KC - 1))
                            if is_ctx:
                                tm = pe.mark(mm)
                                act.wait(tm)
                                ta = act.mark(nc.scalar.copy(out=dstb[:, h, 0:n], in_=pa[:, 0:n]))
                                psQ.release(pi, ta)
                                tq.append(ta)
                                continue
                            for kc in range(KC):
                                mm = nc.tensor.matmul(pbr[:, 0:n], Wr[:, kc, h * 128:(h + 1) * 128], xlT[:, kc, 0:n],
                                                      start=(kc == 0), stop=(kc == KC - 1))
                            tm = pe.mark(mm)
                            i1, b1 = r1.get(dve)
                            i2, b2 = r2.get(dve)
                            dve.wait(tm, t_h)
                            ta = dve.mark(nc.vector.tensor_tensor(out=b1[:, 0:n], in0=pa[:, 0:n], in1=csb_[:, 0, 0:n], op=ALU.mult))
                            tb = dve.mark(nc.vector.tensor_tensor(out=b2[:, 0:n], in0=pbr[:, 0:n], in1=csb_[:, 1, 0:n], op=ALU.mult))
                            psQ.release(pi, tb)
                            dve.wait(ta, tb)
                            tc_ = dve.mark(nc.vector.tensor_tensor(out=dstb[:, h, 0:n], in0=b1[:, 0:n], in1=b2[:, 0:n], op=ALU.add))
                            r1.release(i1, tc_)
                            r2.release(i2, tc_)
                            tq.append(tc_)
                    pool.wait(tq)
                    koff = 0 if is_ctx else CTX
                    kc0 = (c0 - S) if is_ctx else c0
                    if not is_ctx:
                        K.dma(pool, stq[si], QT.rearrange("h p t -> p h t")[:, :, c0:c0 + n], qb[:, :, 0:n])
                    tstq = K.dma(pool, stq[si], KT.rearrange("h p t -> p h t")[:, :, koff + kc0:koff + kc0 + n], kb[:, :, 0:n])
                    st_rel[si] = tstq
                    for j in range(n // 128):
                        vi, vb = vring.get(act)
                        for dh in range(2):
                            pi, pb = psV.get(pe)
                            pe.wait(tx)
                            for kc in range(KC):
                                mm = nc.tensor.matmul(pb[:, :], xlT[:, kc, j * 128:(j + 1) * 128], Wv[:, kc, dh * 512:(dh + 1) * 512],
                                                      start=(kc == 0), stop=(kc == KC - 1))
                            tm = pe.mark(mm)
                            act.wait(tm)
                            ta = act.mark(nc.scalar.copy(out=vb[:, dh * 512:(dh + 1) * 512], in_=pb[:, :]))
                            psV.release(pi, ta)
                        kt = (koff + kc0) // 128 + j
                        pool.wait(ta)
                        tsv = K.dma(pool, stv[vi], VH.rearrange("h p (t e) -> p h t e", e=128)[:, :, kt, :],
                                    vb[:].rearrange("p (h e) -> p h e", e=128))
                        vring.release(vi, tsv)
                    xl_rel = tm
                    hring.release(hi, tx[-1], tq[-1])
                K.barrier()

        def phase_A():
            st = ExitStack()
            with st:
                QB = 512
                kTb = [sb(nc, st, "kTb%d" % i, [128, ST], BF16) for i in range(2)]
                vHb = [sb(nc, st, "vHb%d" % i, [128, NT, 128], BF16) for i in range(2)]
                qb = [sb(nc, st, "qb%d" % i, [128, QB], BF16) for i in range(2)]
                pT = Ring([sb(nc, st, "pT%d" % i, [128, 2 * QB], BF16) for i in range(4)])
                sel = sb(nc, st, "sel", [64, 128], F32)
                ones_f = sb(nc, st, "ones_f", [128, 128], F32)
                accr = Ring([sb(nc, st, "acc%d" % i, [128, QB], F32) for i in range(2)])
                sums_sb = sb(nc, st, "sums_sb", [64, QB], F32)
                pocp = sb(nc, st, "pocp", [128, 2 * QB], F32)
                rs = sb(nc, st, "rs", [128, 2 * QB], F32)
                o0 = sb(nc, st, "o0", [128, QB], F32)
                o1 = sb(nc, st, "o1", [128, QB], F32)
                osq = sb(nc, st, "osq", [128, QB], BF16)
                osd = sb(nc, st, "osd", [128, QB], F32)
                orr = sb(nc, st, "orr", [128, QB], F32)
                ob = [sb(nc, st, "aob%d" % i, [128, QB], BF16) for i in range(2)]
                subg2 = sb(nc, st, "subg2", [128, 1], F32)
                dve.mark(nc.vector.memset(sel[:], 1.0 / 32.0))
                dve.mark(nc.vector.memset(ones_f[:], 1.0))
                t_init = dve.mark(nc.vector.tensor_scalar(out=subg2[:], in0=subg[:], scalar1=(1.0 - LAMBDA_INIT), scalar2=None, op0=ALU.mult))
                pe.wait(t_init)
                ldk = [K.dsem("ldk%d" % i) for i in range(2)]
                ldq = [K.dsem("ldq%d" % i) for i in range(2)]
                sto = [K.dsem("sto%d" % i) for i in range(2)]
                kring = Ring(list(zip(kTb, vHb)))
                qring = Ring(qb)
                oring = Ring(ob)
                psS = Ring([PSD[0], PSD[1]])
                po = PSD[2]
                PSx = PSB[6]
                nqb = S // QB
                units = [(h, q_) for h in range(HEADS) for q_ in range(nqb)]
                kv = {}
                qd = {}
                state = {"fin_rel": None, "acc_rel": None}

                def load_kv(h):
                    ki, (kb_, vb_) = kring.get(sp)
                    K.dma(sp, ldk[ki], kb_[:], KT[h])
                    t_k = K.dma(sp, ldk[ki], vb_[:].rearrange("p t e -> p (t e)"), VH[h])
                    kv[h] = (ki, kb_, vb_, t_k)

                def load_q(ui):
                    h, q_ = units[ui]
                    qi, qbuf = qring.get(sp)
                    t_q = K.dma(sp, ldq[qi], qbuf[:], QT[h, :, q_ * QB:(q_ + 1) * QB])
                    qd[ui] = (qi, qbuf, t_q)

                def issue_S(kb_, qbuf, kt):
                    pi, sp_ = psS.get(pe)
                    nc.tensor.matmul(sp_[:, 0:QB], kb_[0:64, kt * 128:(kt + 1) * 128], qbuf[0:64, :], start=True, stop=True)
                    ts = pe.mark(nc.tensor.matmul(sp_[:, QB:2 * QB], kb_[64:128, kt * 128:(kt + 1) * 128], qbuf[64:128, :],
                                                  start=True, stop=True))
                    ppi, pbuf = pT.get(act)
                    act.wait(ts)
                    te = act.mark(nc.scalar.activation(out=pbuf[:, :], in_=sp_[:, :], func=AF.Exp, scale=0.125))
                    psS.release(pi, te)
                    return (ppi, pbuf, te)

                def finalize(tv, h, q_, acc, ai, td):
                    dve.wait(tv, state["fin_rel"])
                    dve.mark(nc.vector.tensor_copy(out=sums_sb[32:64, :], in_=PSx[32:64, :]))
                    tcp = dve.mark(nc.vector.tensor_copy(out=pocp[:, :], in_=po[:, :]))
                    state["acc_rel"] = tcp
                    si, sp_ = psS.get(pe)
                    pe.wait(tcp, td)
                    nc.tensor.matmul(sp_[:, 0:QB], ones_f[:], acc[:, :], start=True, stop=True)
                    tsum = pe.mark(nc.tensor.matmul(sp_[:, QB:2 * QB], sel[32:64, :], sums_sb[32:64, :], start=True, stop=True))
                    accr.release(ai, tsum)
                    dve.wait(tsum)
                    t1 = dve.mark(nc.vector.reciprocal(out=rs[:, :], in_=sp_[:, :]))
                    psS.release(si, t1)
                    dve.wait(t1)
                    dve.mark(nc.vector.tensor_tensor(out=o0[:], in0=pocp[:, 0:QB], in1=rs[:, 0:QB], op=ALU.mult))
                    t2 = dve.mark(nc.vector.tensor_tensor(out=o1[:], in0=pocp[:, QB:2 * QB], in1=rs[:, QB:2 * QB], op=ALU.mult))
                    dve.wait(t2)
                    t3 = dve.mark(nc.vector.scalar_tensor_tensor(out=o0[:], in0=o1[:], scalar=neglam[:, 0:1], in1=o0[:],
                                                                 op0=ALU.mult, op1=ALU.add))
                    act.wait(t3)
                    t4 = act.mark(nc.scalar.activation(out=osq[:], in_=o0[:], func=AF.Square))
                    si2, sp2 = psS.get(pe)
                    pe.wait(t4)
                    t5 = pe.mark(nc.tensor.matmul(sp2[:, 0:QB], ones_bf[:], osq[:], start=True, stop=True))
                    act.wait(t5)
                    t6 = act.mark(nc.scalar.activation(out=osd[:], in_=sp2[:, 0:QB], func=AF.Sqrt, bias=epsb[:, 0:1], scale=1.0 / 128))
                    psS.release(si2, t6)
                    dve.wait(t6)
                    t7 = dve.mark(nc.vector.reciprocal(out=orr[:], in_=osd[:]))
                    oi2, obuf = oring.get(dve)
                    dve.wait(t7)
                    t8 = dve.mark(nc.vector.scalar_tensor_tensor(out=obuf[:], in0=o0[:], scalar=subg2[:, 0:1], in1=orr[:],
                                                                 op0=ALU.mult, op1=ALU.mult))
                    state["fin_rel"] = t8
                    pool.wait(t8)
                    tso = K.dma(pool, sto[oi2], AT[h * 128:(h + 1) * 128, q_ * QB:(q_ + 1) * QB], obuf[:])
                    oring.release(oi2, tso)

                load_kv(0)
                load_q(0)
                pending = None
                for ui, (h, q_) in enumerate(units):
                    if q_ == 0 and h + 1 < HEADS:
                        load_kv(h + 1)
                    if ui + 1 < len(units):
                        load_q(ui + 1)
                    ki, kb_, vb_, t_k = kv[h]
                    qi, qbuf, t_q = qd[ui]
                    if pending is None:
                        pe.wait(t_k, t_q)
                        cur = issue_S(kb_, qbuf, 0)
                    else:
                        cur = pending
                    ai, acc = accr.get(dve)
                    for kt in range(NT):
                        nxt = issue_S(kb_, qbuf, kt + 1) if kt + 1 < NT else None
                        ppi, pbuf, te = cur
                        pe.wait(te, state["acc_rel"] if kt == 0 else None)
                        f, l_ = (kt == 0), (kt == NT - 1)
                        nc.tensor.matmul(po[:, 0:QB], vb_[:, kt, :], pbuf[:, 0:QB], start=f, stop=l_)
                        nc.tensor.matmul(po[:, QB:2 * QB], vb_[:, kt, :], pbuf[:, QB:2 * QB], start=f, stop=l_)
                        tv = pe.mark(nc.tensor.matmul(PSx[32:64, :], ones_bf[:, 0:32], pbuf[:, QB:2 * QB], start=f, stop=l_,
                                                      tile_position=(0, 32)))
                        dve.wait(te)
                        if kt == 0:
                            td = dve.mark(nc.vector.tensor_copy(out=acc[:, :], in_=pbuf[:, 0:QB]))
                        else:
                            td = dve.mark(nc.vector.tensor_tensor(out=acc[:, :], in0=acc[:, :], in1=pbuf[:, 0:QB], op=ALU.add))
                        pT.release(ppi, tv, td)
                        cur = nxt
                    qring.release(qi, tv)
                    if q_ == nqb - 1:
                        kring.release(ki, tv)
                    if ui + 1 < len(units):
                        h2, _q2 = units[ui + 1]
                        _ki2, kb2, _vb2, t_k2 = kv[h2]
                        _qi2, qbuf2, t_q2 = qd[ui + 1]
                        pe.wait(t_k2, t_q2)
                        pending = issue_S(kb2, qbuf2, 0)
                    else:
                        pending = None
                    finalize(tv, h, q_, acc, ai, td)
                K.barrier()

        def phase_O():
            st = ExitStack()
            with st:
                NB = 512
                Wo = sb(nc, st, "Wo", [128, KC, D], BF16)
                hT = [sb(nc, st, "oh%d" % i, [128, KC, NB], F32) for i in range(2)]
                aT = [sb(nc, st, "oa%d" % i, [128, KC, NB], BF16) for i in range(2)]
                ldw = K.dsem("ldw_O")
                for kc in range(KC):
                    t_w = K.dma(pool, ldw, Wo[:, kc, :], w_o[kc * 128:(kc + 1) * 128, :])
                pe.wait(t_w)
                ldh = [K.dsem("ldh_O%d" % i) for i in range(2)]
                st_h = [K.dsem("st_O%d" % i) for i in range(2)]
                hring = Ring(hT)
                psO = Ring([PSB[0], PSB[1], PSB[2], PSB[3]])
                H2v = H2T.rearrange("(k p) t -> p k t", p=128)
                H1v = H1T.rearrange("(k p) t -> p k t", p=128)
                ATv = AT.rearrange("(k p) t -> p k t", p=128)
                for c0 in range(0, S, NB):
                    hi, hb = hring.get(sp)
                    ab = aT[hi]
                    K.dma(sp, ldh[hi], hb[:], H2v[:, :, c0:c0 + NB])
                    t_h = K.dma(sp, ldh[hi], ab[:], ATv[:, :, c0:c0 + NB])
                    for dc in range(KC):
                        pi, pb = psO.get(pe)
                        pe.wait(t_h)
                        for hc in range(KC):
                            mm = nc.tensor.matmul(pb[:, :], Wo[:, hc, dc * 128:(dc + 1) * 128], ab[:, hc, :],
                                                  start=(hc == 0), stop=(hc == KC - 1))
                        tm = pe.mark(mm)
                        dve.wait(tm, t_h)
                        td = dve.mark(nc.vector.scalar_tensor_tensor(out=hb[:, dc, :], in0=pb[:, :], scalar=modv(1, 0, 16 + dc),
                                                                     in1=hb[:, dc, :], op0=ALU.mult, op1=ALU.add))
                        psO.release(pi, td)
                    pool.wait(td)
                    tst = K.dma(pool, st_h[hi], H1v[:, :, c0:c0 + NB], hb[:])
                    hring.release(hi, tst)
                K.barrier()

        plan = [("mod", phase_mod), ("G", lambda: phase_G(3)),
                ("F0", lambda: phase_F(0, H1T, H2T, False, False, 360)),
                ("Q", phase_Q), ("A", phase_A), ("O", phase_O),
                ("F1", lambda: phase_F(1, H1T, None, False, True, 360))]
        for name, fn in plan:
            fn()
            if stop_after == name:
                break
        K.barrier()
    return nc


def _rope_tables(S):
    rows = S // 64
    row_pos = np.broadcast_to(np.arange(rows, dtype=np.float32)[:, None], (rows, 64)).reshape(-1)
    col_pos = np.broadcast_to(np.arange(64, dtype=np.float32)[None, :], (rows, 64)).reshape(-1)
    n_freq = 16
    inv_freq = (np.float32(10000.0) ** (-np.arange(n_freq, dtype=np.float32) / np.float32(n_freq))).astype(np.float32)
    ang_r = row_pos[:, None] * inv_freq
    ang_c = col_pos[:, None] * inv_freq
    ang = np.concatenate([ang_r, ang_r, ang_c, ang_c], axis=-1).astype(np.float32)
    cosT = np.ascontiguousarray(np.cos(ang).T.astype(np.float32))
    sinT = np.ascontiguousarray(np.sin(ang).T.astype(np.float32))
    return np.concatenate([cosT, cosT], 0), np.concatenate([sinT, sinT], 0)


def _fm(vec, nchunk):
    return np.ascontiguousarray(np.asarray(vec, np.float32).reshape(nchunk, 128).T)


def make_in_maps(inp, S, ncores):
    f = lambda a: np.ascontiguousarray(np.asarray(a, np.float32))
    cosT, sinT = _rope_tables(S)
    shared = {
        "ada_w": f(inp["ada_w"]),
        "ada_bT": np.ascontiguousarray(np.stack([_fm(inp["ada_b"][l], 48) for l in range(2)], 1).reshape(128, 96)),
        "gm_w_in": f(inp["gm_w_in"][0]),
        "gm_gT": _fm(inp["gm_norm_g"][0], 16),
        "gm_wsT": np.ascontiguousarray(np.transpose(f(inp["gm_w_s"][0]), (2, 0, 1)).reshape(128, 1024)),
        "gm_bs": f(inp["gm_b_s"][0]).reshape(1, 1024),
        "gm_w_out": f(inp["gm_w_out"][0]),
        "w_qkv": f(inp["da_w_qkv"][0]),
        "lamv": np.concatenate([f(inp["da_lambda_q1"][0]), f(inp["da_lambda_k1"][0]),
                                f(inp["da_lambda_q2"][0]), f(inp["da_lambda_k2"][0])]).reshape(1, 256),
        "sublnT": f(inp["da_subln_g"][0]).reshape(128, 1),
        "w_o": f(inp["da_w_out"][0]),
        "w_up": f(inp["ffn_w_up"]),
        "convT": np.ascontiguousarray(np.stack(
            [np.stack([_fm(inp["ffn_conv_w"][l][0], NCC), _fm(inp["ffn_conv_w"][l][1], NCC),
                       _fm(inp["ffn_conv_w"][l][2], NCC), _fm(inp["ffn_conv_b"][l], NCC)], 1) for l in range(2)], 1).reshape(128, 2 * 4 * NCC)),
        "w_dn": f(inp["ffn_w_down"]),
        "fngT": _fm(inp["final_norm_g"], KC),
        "ident": np.eye(128, dtype=np.float32),
        "cosT": cosT, "sinT": sinT,
    }
    cc = _fm(inp["c_ctx"], KC)
    maps = []
    for b in range(ncores):
        m = dict(shared)
        m["x"] = f(inp["x"][b])
        m["ctx"] = f(inp["ctx"][b])
        cb = _fm(inp["c"][b], KC)
        m["cvec"] = np.ascontiguousarray(np.stack([cb, cc], 2).reshape(128, 2 * KC))
        maps.append(m)
    return maps


_NC_CACHE = {}


def kernel(**inputs):
    S = int(inputs["x"].shape[1])
    B = int(inputs["x"].shape[0])
    key = (S,)
    if key not in _NC_CACHE:
        _NC_CACHE[key] = build_program(S)
    nc = _NC_CACHE[key]
    in_maps = make_in_maps(inputs, S, B)
    res = run_bass_kernel_spmd(nc, in_maps, core_ids=list(range(B)))
    return np.stack([np.asarray(r["out"], np.float32) for r in res.results], 0)
```

```python
import math
from contextlib import ExitStack

import numpy as np
import ml_dtypes
import concourse.bass as bass
import concourse.mybir as mybir
from concourse.bass_utils import run_bass_kernel_spmd

F32 = mybir.dt.float32
BF16 = mybir.dt.bfloat16
AF = mybir.ActivationFunctionType
ALU = mybir.AluOpType

D = 1024
KC = 8
CTX = 256
NCORES = 8
SEQ = 8192
EPS = 1e-6
GMW = 2048
FFN = 2816
NCC = 44
NPAIR = 22
HEADS = 8
LAMBDA_INIT = 0.8 - 0.6 * math.exp(-0.3 * 1)


class Eng:
    def __init__(self, K, name, eng):
        self.K, self.name, self.e = K, name, eng
        self.sem = K.newsem("e_" + name)
        self.cnt = 0
        self.seen = {}

    def wait(self, *tks):
        for tk in tks:
            if tk is None:
                continue
            if isinstance(tk, list):
                self.wait(*tk)
                continue
            sem, v = tk
            if self.seen.get(sem, 0) < v:
                self.e.wait_ge(sem, v)
                self.seen[sem] = v

    def mark(self, ins):
        self.cnt += 1
        ins.then_inc(self.sem, 1)
        return (self.sem, self.cnt)

    def last(self):
        return (self.sem, self.cnt) if self.cnt else None


class DSem:
    def __init__(self, K, name):
        self.sem = K.newsem(name)
        self.cnt = 0

    def tk(self):
        return (self.sem, self.cnt) if self.cnt else None


class Ring:
    def __init__(self, bufs):
        self.bufs = list(bufs)
        self.rel = [None] * len(self.bufs)
        self.i = 0

    def get(self, *engs):
        i = self.i % len(self.bufs)
        self.i += 1
        for e in engs:
            e.wait(self.rel[i])
        self.rel[i] = None
        return i, self.bufs[i]

    def release(self, i, *tks):
        cur = self.rel[i]
        lst = [] if cur is None else list(cur)
        lst.extend(t for t in tks if t is not None)
        self.rel[i] = lst


class Kern:
    def __init__(self, nc, es):
        self.nc, self.es = nc, es
        self.nsem = 0
        self.pe = Eng(self, "pe", nc.tensor)
        self.act = Eng(self, "act", nc.scalar)
        self.dve = Eng(self, "dve", nc.vector)
        self.pool = Eng(self, "pool", nc.gpsimd)
        self.sp = Eng(self, "sp", nc.sync)
        self.engs = [self.pe, self.act, self.dve, self.pool, self.sp]
        self.dsems = []

    def newsem(self, name):
        self.nsem += 1
        return self.es.enter_context(self.nc.semaphore(name))

    def dsem(self, name):
        d = DSem(self, name)
        self.dsems.append(d)
        return d

    def dma(self, q, ds, out, in_):
        ins = q.e.dma_start(out=out, in_=in_)
        ds.cnt += 16
        ins.then_inc(ds.sem, 16)
        return (ds.sem, ds.cnt)

    def barrier(self):
        tks = [e.last() for e in self.engs] + [d.tk() for d in self.dsems]
        for e in self.engs:
            e.wait(*tks)


_UID = [0]


def sb(nc, st, name, shape, dt):
    _UID[0] += 1
    return st.enter_context(nc.sbuf_tensor("s%d_%s" % (_UID[0], name), list(shape), dt))


def build_program(S=SEQ, stop_after=None, dbg=False):
    nc = bass.Bass("TRN2", target_bir_lowering=False)
    ST = S + CTX
    NT = ST // 128
    kind_scr = "ExternalOutput" if dbg else "Internal"

    def din(name, shape, dt=F32):
        return nc.dram_tensor(name, list(shape), dt, kind="ExternalInput").ap()

    def dscr(name, shape, dt):
        return nc.dram_tensor(name, list(shape), dt, kind=kind_scr).ap()

    x = din("x", [S, D])
    ctx = din("ctx", [CTX, D])
    cvec = din("cvec", [128, KC * 2])
    ada_w = din("ada_w", [2, D, 6 * D])
    ada_bT = din("ada_bT", [128, 2 * 48])
    gm_w_in = din("gm_w_in", [D, 2 * GMW])
    gm_gT = din("gm_gT", [128, 16])
    gm_wsT = din("gm_wsT", [128, 8 * 128])
    gm_bs = din("gm_bs", [1, 8 * 128])
    gm_w_out = din("gm_w_out", [GMW, D])
    w_qkv = din("w_qkv", [D, 3 * D])
    lamv = din("lamv", [1, 256])
    sublnT = din("sublnT", [128, 1])
    w_o = din("w_o", [D, D])
    w_up = din("w_up", [2, D, 2 * FFN])
    convT = din("convT", [128, 2 * 4 * NCC])
    w_dn = din("w_dn", [2, FFN, D])
    fngT = din("fngT", [128, KC])
    ident_d = din("ident", [128, 128])
    cosT_d = din("cosT", [128, S])
    sinT_d = din("sinT", [128, S])
    out = nc.dram_tensor("out", [S, D], F32, kind="ExternalOutput").ap()

    H1T = dscr("H1T", [D, ST], F32)
    H2T = dscr("H2T", [D, ST], F32)
    QT = dscr("QT", [HEADS, 128, S], BF16)
    KT = dscr("KT", [HEADS, 128, ST], BF16)
    VH = dscr("VH", [HEADS, 128, NT * 128], BF16)
    AT = dscr("AT", [D, S], BF16)

    es = ExitStack()
    with es:
        K = Kern(nc, es)
        pe, act, dve, pool, sp = K.pe, K.act, K.dve, K.pool, K.sp

        PSD = [es.enter_context(nc.psum_tensor("psd%d" % i, [128, 1024], F32)) for i in range(4)]
        PSB = [PSD[i // 2][:, (i % 2) * 512:(i % 2 + 1) * 512] for i in range(8)]
        ident = sb(nc, es, "ident", [128, 128], F32)
        ones_bf = sb(nc, es, "ones_bf", [128, 128], BF16)
        modT = sb(nc, es, "modT", [128, 2, 2, 48], F32)
        convS = sb(nc, es, "convS", [128, 2, 4, NCC], F32)
        gmg = sb(nc, es, "gmg", [128, 16], F32)
        subg = sb(nc, es, "subg", [128, 1], F32)
        fng = sb(nc, es, "fng", [128, KC], F32)
        neglam = sb(nc, es, "neglam", [128, 1], F32)
        epsb = sb(nc, es, "epsb", [128, 1], F32)

        ld_c = K.dsem("ld_c")
        K.dma(sp, ld_c, ident[:], ident_d[:])
        K.dma(sp, ld_c, convS[:].rearrange("p a b c -> p (a b c)"), convT[:])
        K.dma(sp, ld_c, gmg[:], gm_gT[:])
        K.dma(sp, ld_c, subg[:], sublnT[:])
        t_const = K.dma(sp, ld_c, fng[:], fngT[:])
        dve.mark(nc.vector.memset(ones_bf[:], 1.0))
        t_ones = dve.mark(nc.vector.memset(epsb[:], EPS))
        for e in (pe, act, dve):
            e.wait(t_const, t_ones)

        def modv(l, v, j):
            return modT[:, l, v, j:j + 1]

        def phase_mod():
            st = ExitStack()
            with st:
                c32 = sb(nc, st, "c32", [128, KC * 2], F32)
                csb = sb(nc, st, "csb", [128, KC, 2], BF16)
                abT = sb(nc, st, "abT", [128, 2, 48], F32)
                lam_s = sb(nc, st, "lam_s", [1, 256], F32)
                lam_p = sb(nc, st, "lam_p", [1, 128], F32)
                lam_r = sb(nc, st, "lam_r", [1, 4], F32)
                ones1 = sb(nc, st, "ones1", [1, 128], F32)
                wa = [sb(nc, st, "wa%d" % i, [128, KC, 1024], BF16) for i in range(2)]
                wring = Ring(wa)
                ld = K.dsem("ld_mod")
                ldw = [K.dsem("ld_wa%d" % i) for i in range(2)]
                K.dma(sp, ld, c32[:], cvec[:])
                K.dma(sp, ld, abT[:].rearrange("p a b -> p (a b)"), ada_bT[:])
                t_in = K.dma(sp, ld, lam_s[:], lamv[:])
                act.wait(t_in)
                t_c = act.mark(nc.scalar.activation(out=csb[:].rearrange("p a b -> p (a b)"), in_=c32[:], func=AF.Silu))
                dve.wait(t_in)
                dve.mark(nc.vector.memset(ones1[:], 1.0))
                dve.mark(nc.vector.tensor_tensor(out=lam_p[:, 0:64], in0=lam_s[:, 0:64], in1=lam_s[:, 64:128], op=ALU.mult))
                t1 = dve.mark(nc.vector.tensor_tensor(out=lam_p[:, 64:128], in0=lam_s[:, 128:192], in1=lam_s[:, 192:256], op=ALU.mult))
                dve.wait(t1)
                t2 = dve.mark(nc.vector.reduce_sum(out=lam_r[:, 0:2], in_=lam_p[:].rearrange("p (a b) -> p a b", a=2),
                                                   axis=mybir.AxisListType.X))
                act.wait(t2)
                t3 = act.mark(nc.scalar.activation(out=lam_r[:, 2:4], in_=lam_r[:, 0:2], func=AF.Exp))
                dve.wait(t3)
                t4 = dve.mark(nc.vector.tensor_tensor(out=lam_r[:, 0:1], in0=lam_r[:, 3:4], in1=lam_r[:, 2:3], op=ALU.subtract))
                dve.wait(t4)
                t5 = dve.mark(nc.vector.tensor_scalar(out=lam_r[:, 1:2], in0=lam_r[:, 0:1], scalar1=-LAMBDA_INIT, scalar2=None, op0=ALU.add))
                pe.wait(t5)
                t6 = pe.mark(nc.tensor.matmul(PSB[7][:, 0:1], ones1[:], lam_r[:, 1:2], start=True, stop=True))
                dve.wait(t6)
                t7 = dve.mark(nc.vector.tensor_copy(out=neglam[:], in_=PSB[7][:, 0:1]))
                pring = Ring([PSB[0], PSB[1]])
                for l in range(2):
                    for cg in range(6):
                        wi, wbuf = wring.get(pool)
                        for kc in range(KC):
                            tw = K.dma(pool, ldw[wi], wbuf[:, kc, :], ada_w[l, kc * 128:(kc + 1) * 128, cg * 1024:(cg + 1) * 1024])
                        pi, pb = pring.get(pe)
                        pe.wait(tw, t_c)
                        for fcl in range(8):
                            for kc in range(KC):
                                mm = nc.tensor.matmul(pb[:, fcl * 2:fcl * 2 + 2], wbuf[:, kc, fcl * 128:(fcl + 1) * 128],
                                                      csb[:, kc, :], start=(kc == 0), stop=(kc == KC - 1))
                        tm = pe.mark(mm)
                        wring.release(wi, tm)
                        dve.wait(tm, t_in)
                        td = dve.mark(nc.vector.tensor_tensor(
                            out=modT[:, l, :, cg * 8:(cg + 1) * 8],
                            in0=pb[:, 0:16].rearrange("p (f v) -> p v f", v=2),
                            in1=abT[:, l, cg * 8:(cg + 1) * 8].unsqueeze(1).to_broadcast([128, 2, 8]),
                            op=ALU.add))
                        pring.release(pi, td)
                dve.wait(dve.last())
                for l in range(2):
                    for j0 in (8, 32):
                        dve.mark(nc.vector.tensor_scalar(out=modT[:, l, :, j0:j0 + 8], in0=modT[:, l, :, j0:j0 + 8],
                                                         scalar1=1.0, scalar2=None, op0=ALU.add))
                K.barrier()

        def stats_modulate(hsrc, n, l, v, jsc, jsh, xdst, bufs, t_h, ps_ring, extra_wait=None):
            sqr, sdb, rb, tmr = bufs["sq"], bufs["sd"], bufs["r"], bufs["tm"]
            pi, pb = ps_ring.get(pe)
            for kc in range(KC):
                si, sq = sqr.get(act)
                act.wait(t_h)
                ts = act.mark(nc.scalar.activation(out=sq[:, 0:n], in_=hsrc(kc), func=AF.Square))
                pe.wait(ts)
                mm = nc.tensor.matmul(pb[:, 0:n], ones_bf[:], sq[:, 0:n], start=(kc == 0), stop=(kc == KC - 1))
                tm = pe.mark(mm)
                sqr.release(si, tm)
            act.wait(tm, bufs.get("sd_rel"))
            t_sd = act.mark(nc.scalar.activation(out=sdb[:, 0:n], in_=pb[:, 0:n], func=AF.Sqrt, bias=epsb[:, 0:1], scale=1.0 / D))
            ps_ring.release(pi, t_sd)
            dve.wait(t_sd, bufs.get("r_rel"))
            t_r = dve.mark(nc.vector.reciprocal(out=rb[:, 0:n], in_=sdb[:, 0:n]))
            bufs["sd_rel"] = t_r
            tks = []
            for kc in range(KC):
                ti, tmb = tmr.get(dve)
                dve.wait(t_r, t_h)
                tt = dve.mark(nc.vector.scalar_tensor_tensor(out=tmb[:, 0:n], in0=hsrc(kc), scalar=modv(l, v, jsc + kc),
                                                             in1=rb[:, 0:n], op0=ALU.mult, op1=ALU.mult))
                act.wait(tt, extra_wait)
                ta = act.mark(nc.scalar.activation(out=xdst(kc), in_=tmb[:, 0:n], func=AF.Identity,
                                                   bias=modv(l, v, jsh + kc), scale=1.0))
                tmr.release(ti, ta)
                tks.append(ta)
            bufs["r_rel"] = tt
            return tks

        def phase_G(TBG=3):
            st = ExitStack()
            with st:
                NB = TBG * 128
                WinU = sb(nc, st, "WinU", [128, KC, GMW], BF16)
                WinV = sb(nc, st, "WinV", [128, KC, GMW], BF16)
                WsT = sb(nc, st, "WsT", [128, 8, 128], BF16)
                Wout = sb(nc, st, "Wout", [128, 16, D], BF16)
                bsr = sb(nc, st, "bsr", [128, 8, 128], F32)
                xt = [sb(nc, st, "xt%d" % i, [128, D], F32) for i in range(TBG)]
                hT = [sb(nc, st, "hT%d" % i, [128, KC, NB], F32) for i in range(2)]
                xlT = sb(nc, st, "xlT", [128, KC, NB], BF16)
                uT = sb(nc, st, "uT", [128, 16, NB], BF16)
                gv = [sb(nc, st, "gv%d" % i, [128, GMW], F32) for i in range(2)]
                vn = sb(nc, st, "vn", [128, TBG, GMW], BF16)
                junk = sb(nc, st, "junk", [128, GMW], BF16)
                vst = sb(nc, st, "vst", [128, 4 * TBG * 2], F32)
                bufs = {
                    "sq": Ring([sb(nc, st, "sq%d" % i, [128, NB], BF16) for i in range(2)]),
                    "sd": sb(nc, st, "sdb", [128, NB], F32),
                    "r": sb(nc, st, "rb", [128, NB], F32),
                    "tm": Ring([sb(nc, st, "tm%d" % i, [128, NB], F32) for i in range(2)]),
                }
                t2r = Ring([sb(nc, st, "t2_%d" % i, [128, NB], F32) for i in range(2)])
                ldw = K.dsem("ldw_G")
                for kc in range(KC):
                    K.dma(pool, ldw, WinU[:, kc, :], gm_w_in[kc * 128:(kc + 1) * 128, 0:GMW])
                    K.dma(pool, ldw, WinV[:, kc, :], gm_w_in[kc * 128:(kc + 1) * 128, GMW:2 * GMW])
                for fc in range(16):
                    K.dma(pool, ldw, Wout[:, fc, :], gm_w_out[fc * 128:(fc + 1) * 128, :])
                K.dma(pool, ldw, WsT[:].rearrange("p a b -> p (a b)"), gm_wsT[:])
                t_w = K.dma(pool, ldw, bsr[:].rearrange("p a b -> p (a b)"), gm_bs.partition_broadcast(128))
                pe.wait(t_w)
                dve.wait(t_w)

                ldx = [K.dsem("ldx_G%d" % i) for i in range(TBG)]
                st_h = [K.dsem("st_G%d" % i) for i in range(2)]
                xring = Ring(xt)
                hring = Ring(hT)
                psT = Ring([PSB[0], PSB[1]])
                psA = Ring([PSB[2], PSB[3]])
                psM = Ring([PSB[4], PSB[5]])
                psO = Ring([PSB[6], PSB[7]])
                gvr = Ring(gv)
                vn_rel = None
                xl_rel = None
                u_rel = None

                blocks = []
                t0 = 0
                while t0 < S:
                    nt = min(TBG, (S - t0) // 128)
                    blocks.append((x, t0, t0, nt, 0))
                    t0 += nt * 128
                blocks.append((ctx, 0, S, CTX // 128, 1))

                for bidx, (src, r0, c0, nt, v) in enumerate(blocks):
                    n = nt * 128
                    tl = []
                    for j in range(nt):
                        xi, xb = xring.get(sp)
                        tl.append((xi, xb, K.dma(sp, ldx[xi], xb[:], src[r0 + j * 128:r0 + (j + 1) * 128, :])))
                    hi, hb = hring.get(act)
                    tev = []
                    for kc in range(KC):
                        pi, pb = psT.get(pe)
                        for j in range(nt):
                            pe.wait(tl[j][2])
                            mm = nc.tensor.transpose(pb[:, j * 128:(j + 1) * 128], tl[j][1][:, kc * 128:(kc + 1) * 128], ident[:])
                        tm = pe.mark(mm)
                        act.wait(tm)
                        ta = act.mark(nc.scalar.copy(out=hb[:, kc, 0:n], in_=pb[:, 0:n]))
                        psT.release(pi, ta)
                        tev.append(ta)
                    for j in range(nt):
                        xring.release(tl[j][0], tm)
                    tx = stats_modulate(lambda kc: hb[:, kc, 0:n], n, 0, v, 8, 0, lambda kc: xlT[:, kc, 0:n], bufs,
                                        tev, psT, extra_wait=xl_rel)
                    tvn = []
                    for j in range(nt):
                        gi, gb = gvr.get(act)
                        for cb in range(4):
                            pi, pb = psA.get(pe)
                            pe.wait(tx)
                            for kc in range(KC):
                                mm = nc.tensor.matmul(pb[:, :], xlT[:, kc, j * 128:(j + 1) * 128], WinV[:, kc, cb * 512:(cb + 1) * 512],
                                                      start=(kc == 0), stop=(kc == KC - 1))
                            tm = pe.mark(mm)
                            act.wait(tm)
                            ta = act.mark(nc.scalar.activation(out=gb[:, cb * 512:(cb + 1) * 512], in_=pb[:, :], func=AF.Gelu_apprx_tanh))
                            psA.release(pi, ta)
                        act.wait(ta)
                        c = j * 4 + (bidx % 2) * 4 * TBG
                        ts = act.mark(nc.scalar.activation(out=junk[:], in_=gb[:], func=AF.Square, accum_out=vst[:, c:c + 1]))
                        act.wait(ts)
                        ts2 = act.mark(nc.scalar.activation(out=vst[:, c + 1:c + 2], in_=vst[:, c:c + 1], func=AF.Sqrt,
                                                            bias=epsb[:, 0:1], scale=1.0 / GMW))
                        dve.wait(ts2)
                        tr = dve.mark(nc.vector.reciprocal(out=vst[:, c + 2:c + 3], in_=vst[:, c + 1:c + 2]))
                        dve.wait(tr, vn_rel if j == 0 else None)
                        tn = dve.mark(nc.vector.tensor_scalar(out=vn[:, j, :], in0=gb[:], scalar1=vst[:, c + 2:c + 3], scalar2=None,
                                                              op0=ALU.mult))
                        gvr.release(gi, tn)
                        tvn.append(tn)
                    tu = []
                    for fc in range(16):
                        pi, pb = psA.get(pe)
                        pe.wait(tx)
                        for kc in range(KC):
                            mm = nc.tensor.matmul(pb[:, 0:n], WinU[:, kc, fc * 128:(fc + 1) * 128], xlT[:, kc, 0:n],
                                                  start=(kc == 0), stop=(kc == KC - 1))
                        tm = pe.mark(mm)
                        act.wait(tm, u_rel)
                        ta = act.mark(nc.scalar.activation(out=uT[:, fc, 0:n], in_=pb[:, 0:n], func=AF.Gelu_apprx_tanh))
                        psA.release(pi, ta)
                        tu.append(ta)
                    xl_rel = tm
                    tuv = []
                    for fc in range(16):
                        g = fc // 2
                        pi, pb = psM.get(pe)
                        pe.wait(tvn)
                        for j in range(nt):
                            mm = nc.tensor.matmul(pb[:, j * 128:(j + 1) * 128], vn[:, j, fc * 128:(fc + 1) * 128], WsT[:, g, :],
                                                  start=True, stop=True)
                        tm = pe.mark(mm)
                        ti, tb = t2r.get(dve)
                        dve.wait(tm)
                        td = dve.mark(nc.vector.scalar_tensor_tensor(
                            out=tb[:, 0:n].rearrange("p (a b) -> p a b", a=nt),
                            in0=pb[:, 0:n].rearrange("p (a b) -> p a b", a=nt), scalar=gmg[:, fc:fc + 1],
                            in1=bsr[:, g, :].unsqueeze(1).to_broadcast([128, nt, 128]), op0=ALU.mult, op1=ALU.add))
                        psM.release(pi, td)
                        dve.wait(td, tu[fc])
                        td2 = dve.mark(nc.vector.tensor_tensor(out=uT[:, fc, 0:n], in0=tb[:, 0:n], in1=uT[:, fc, 0:n], op=ALU.mult))
                        t2r.release(ti, td2)
                        tuv.append(td2)
                    vn_rel = tm
                    for dc in range(KC):
                        pi, pb = psO.get(pe)
                        pe.wait(tuv)
                        for fc in range(16):
                            mm = nc.tensor.matmul(pb[:, 0:n], Wout[:, fc, dc * 128:(dc + 1) * 128], uT[:, fc, 0:n],
                                                  start=(fc == 0), stop=(fc == 15))
                        tm = pe.mark(mm)
                        dve.wait(tm, tev[dc])
                        td = dve.mark(nc.vector.scalar_tensor_tensor(out=hb[:, dc, 0:n], in0=pb[:, 0:n], scalar=modv(0, v, 16 + dc),
                                                                     in1=hb[:, dc, 0:n], op0=ALU.mult, op1=ALU.add))
                        psO.release(pi, td)
                    u_rel = tm
                    pool.wait(td)
                    tst = K.dma(pool, st_h[hi], H1T.rearrange("(k p) t -> p k t", p=128)[:, :, c0:c0 + n], hb[:, :, 0:n])
                    hring.release(hi, tst)
                K.barrier()

        def phase_F(l, SRC, DST, with_o, final, NV):
            st = ExitStack()
            with st:
                NCOL = NV + 2
                Wup = sb(nc, st, "Wup", [128, KC, 2 * FFN], BF16)
                Wdn = sb(nc, st, "Wdn", [128, NPAIR, D], BF16)
                hT = [sb(nc, st, "fh%d" % i, [128, KC, NCOL], F32) for i in range(2)]
                x2T = sb(nc, st, "x2T", [128, KC, NCOL], BF16)
                actT = sb(nc, st, "actT", [128, NPAIR, NCOL], BF16)
                zc = Ring([sb(nc, st, "zc%d" % i, [128, NCOL], F32) for i in range(2)])
                sa = Ring([sb(nc, st, "sa%d" % i, [128, NCOL], F32) for i in range(2)])
                bufs = {
                    "sq": Ring([sb(nc, st, "fsq%d" % i, [128, NCOL], BF16) for i in range(2)]),
                    "sd": sb(nc, st, "fsd", [128, NCOL], F32),
                    "r": sb(nc, st, "frb", [128, NCOL], F32),
                    "tm": Ring([sb(nc, st, "ftm%d" % i, [128, NCOL], F32) for i in range(2)]),
                }
                ldw = K.dsem("ldw_F%d" % l)
                for kc in range(KC):
                    K.dma(pool, ldw, Wup[:, kc, :], w_up[l, kc * 128:(kc + 1) * 128, :])
                for c in range(NPAIR):
                    t_w = K.dma(pool, ldw, Wdn[:, c, :], w_dn[l, c * 128:(c + 1) * 128, :])
                if with_o:
                    Wo = sb(nc, st, "Wo", [128, KC, D], BF16)
                    aT = [sb(nc, st, "aT%d" % i, [128, KC, NCOL], BF16) for i in range(2)]
                    for kc in range(KC):
                        t_w = K.dma(pool, ldw, Wo[:, kc, :], w_o[kc * 128:(kc + 1) * 128, :])
                if final:
                    ost = [sb(nc, st, "ost%d" % i, [128, D], F32) for i in range(2)]
                    oring = Ring(ost)
                    st_o = [K.dsem("st_o%d" % i) for i in range(2)]
                pe.wait(t_w)
                ldh = [K.dsem("ldh_F%d_%d" % (l, i)) for i in range(2)]
                st_h = [K.dsem("st_F%d_%d" % (l, i)) for i in range(2)]
                hring = Ring(hT)
                psZ = Ring([PSB[0], PSB[1], PSB[2]])
                psO = Ring([PSB[3], PSB[4]])
                psS = Ring([PSB[5]])
                psT = Ring([PSB[6], PSB[7]])
                x2_rel = None
                a_rel = None
                for i in range(2):
                    dve.mark(nc.vector.memset(hT[i][:].rearrange("p a b -> p (a b)"), 0.0))
                    if with_o:
                        dve.mark(nc.vector.memset(aT[i][:].rearrange("p a b -> p (a b)"), 0.0))
                t_ms = dve.last()

                segs = [(0, S)] + ([] if final else [(S, CTX)])
                blocks = []
                for (base, L) in segs:
                    nb = -(-L // NV)
                    nv = -(-L // nb)
                    ts = 0
                    while ts < L:
                        blocks.append((base, L, ts, min(nv, L - ts), 1 if base == S else 0))
                        ts += nv
                SRCv = SRC.rearrange("(k p) t -> p k t", p=128)
                if DST is not None:
                    DSTv = DST.rearrange("(k p) t -> p k t", p=128)
                ATv = AT.rearrange("(k p) t -> p k t", p=128)

                for (base, L, ts, nv, v) in blocks:
                    ncol = nv + 2
                    lo = max(ts - 1, 0)
                    hi_ = min(ts + nv + 1, L)
                    c_lo = lo - (ts - 1)
                    c_hi = c_lo + (hi_ - lo)
                    hi, hb = hring.get(sp)
                    sp.wait(t_ms)
                    t_h = K.dma(sp, ldh[hi], hb[:, :, c_lo:c_hi], SRCv[:, :, base + lo:base + hi_])
                    if with_o:
                        ab = aT[hi]
                        t_h = K.dma(sp, ldh[hi], ab[:, :, c_lo:c_hi], ATv[:, :, lo:hi_])
                        tres = []
                        for dc in range(KC):
                            pi, pb = psO.get(pe)
                            pe.wait(t_h)
                            for hc in range(KC):
                                mm = nc.tensor.matmul(pb[:, 0:ncol], Wo[:, hc, dc * 128:(dc + 1) * 128], ab[:, hc, 0:ncol],
                                                      start=(hc == 0), stop=(hc == KC - 1))
                            tm = pe.mark(mm)
                            dve.wait(tm, t_h)
                            td = dve.mark(nc.vector.scalar_tensor_tensor(out=hb[:, dc, 0:ncol], in0=pb[:, 0:ncol],
                                                                         scalar=modv(l, v, 16 + dc), in1=hb[:, dc, 0:ncol],
                                                                         op0=ALU.mult, op1=ALU.add))
                            psO.release(pi, td)
                            tres.append(td)
                        t_hh = tres
                    else:
                        t_hh = [t_h]
                    tx = stats_modulate(lambda kc: hb[:, kc, 0:ncol], ncol, l, v, 32, 24, lambda kc: x2T[:, kc, 0:ncol], bufs,
                                        t_hh, psS, extra_wait=x2_rel)
                    tz = []
                    if c_lo > 0:
                        dve.wait(tx)
                        tz.append(dve.mark(nc.vector.memset(x2T[:, :, 0:1], 0.0)))
                    if c_hi < ncol:
                        dve.wait(tx)
                        tz.append(dve.mark(nc.vector.memset(x2T[:, :, ncol - 1:ncol], 0.0)))
                    tact = []
                    for i in range(NPAIR):
                        res = []
                        for half in range(2):
                            cc = i + half * NPAIR
                            pi, pb = psZ.get(pe)
                            pe.wait(tx, tz)
                            for kc in range(KC):
                                mm = nc.tensor.matmul(pb[:, 0:ncol], Wup[:, kc, cc * 128:(cc + 1) * 128], x2T[:, kc, 0:ncol],
                                                      start=(kc == 0), stop=(kc == KC - 1))
                            tm = pe.mark(mm)
                            zi, zb = zc.get(act)
                            act.wait(tm)
                            ta = act.mark(nc.scalar.activation(out=zb[:, 0:ncol], in_=pb[:, 0:ncol], func=AF.Identity,
                                                               bias=convS[:, l, 3, cc:cc + 1], scale=convS[:, l, 1, cc:cc + 1]))
                            dve.wait(ta)
                            td1 = dve.mark(nc.vector.scalar_tensor_tensor(out=zb[:, 1:nv + 1], in0=pb[:, 0:nv],
                                                                          scalar=convS[:, l, 0, cc:cc + 1], in1=zb[:, 1:nv + 1],
                                                                          op0=ALU.mult, op1=ALU.add))
                            dve.wait(td1)
                            td2 = dve.mark(nc.vector.scalar_tensor_tensor(out=zb[:, 1:nv + 1], in0=pb[:, 2:nv + 2],
                                                                          scalar=convS[:, l, 2, cc:cc + 1], in1=zb[:, 1:nv + 1],
                                                                          op0=ALU.mult, op1=ALU.add))
                            psZ.release(pi, td2)
                            res.append((zi, zb, td2))
                        (zia, zba, tda), (zig, zbg, tdg) = res
                        si, sbuf_ = sa.get(act)
                        act.wait(tda)
                        tsl = act.mark(nc.scalar.activation(out=sbuf_[:, 1:nv + 1], in_=zba[:, 1:nv + 1], func=AF.Silu))
                        zc.release(zia, tsl)
                        dve.wait(tsl, tdg, a_rel if i == 0 else None)
                        tg = dve.mark(nc.vector.tensor_tensor(out=actT[:, i, 1:nv + 1], in0=sbuf_[:, 1:nv + 1], in1=zbg[:, 1:nv + 1],
                                                              op=ALU.mult))
                        zc.release(zig, tg)
                        sa.release(si, tg)
                        tact.append(tg)
                    x2_rel = tm
                    tdn = []
                    for dc in range(KC):
                        pi, pb = psO.get(pe)
                        pe.wait(tact)
                        for c in range(NPAIR):
                            mm = nc.tensor.matmul(pb[:, 0:nv], Wdn[:, c, dc * 128:(dc + 1) * 128], actT[:, c, 1:nv + 1],
                                                  start=(c == 0), stop=(c == NPAIR - 1))
                        tm = pe.mark(mm)
                        dve.wait(tm)
                        td = dve.mark(nc.vector.scalar_tensor_tensor(out=hb[:, dc, 1:nv + 1], in0=pb[:, 0:nv], scalar=modv(l, v, 40 + dc),
                                                                     in1=hb[:, dc, 1:nv + 1], op0=ALU.mult, op1=ALU.add))
                        psO.release(pi, td)
                        tdn.append(td)
                    a_rel = tm
                    if not final:
                        pool.wait(tdn)
                        tst = K.dma(pool, st_h[hi], DSTv[:, :, base + ts:base + ts + nv], hb[:, :, 1:nv + 1])
                        hring.release(hi, tst)
                    else:
                        pi, pb = psS.get(pe)
                        for kc in range(KC):
                            si, sq = bufs["sq"].get(act)
                            act.wait(tdn[kc])
                            tsq = act.mark(nc.scalar.activation(out=sq[:, 0:nv], in_=hb[:, kc, 1:nv + 1], func=AF.Square))
                            pe.wait(tsq)
                            tm = pe.mark(nc.tensor.matmul(pb[:, 0:nv], ones_bf[:], sq[:, 0:nv], start=(kc == 0), stop=(kc == KC - 1)))
                            bufs["sq"].release(si, tm)
                        act.wait(tm, bufs.get("sd_rel"))
                        t_sd = act.mark(nc.scalar.activation(out=bufs["sd"][:, 0:nv], in_=pb[:, 0:nv], func=AF.Sqrt,
                                                             bias=epsb[:, 0:1], scale=1.0 / D))
                        psS.release(pi, t_sd)
                        dve.wait(t_sd, bufs.get("r_rel"))
                        t_r = dve.mark(nc.vector.reciprocal(out=bufs["r"][:, 0:nv], in_=bufs["sd"][:, 0:nv]))
                        bufs["sd_rel"] = t_r
                        tyn = []
                        for kc in range(KC):
                            dve.wait(t_r, tdn[kc])
                            tyn.append(dve.mark(nc.vector.scalar_tensor_tensor(out=hb[:, kc, 1:nv + 1], in0=hb[:, kc, 1:nv + 1],
                                                                               scalar=fng[:, kc:kc + 1], in1=bufs["r"][:, 0:nv],
                                                                               op0=ALU.mult, op1=ALU.mult)))
                        bufs["r_rel"] = tyn[-1]
                        j0 = 0
                        while j0 < nv:
                            rows = min(128, nv - j0)
                            oi, ob = oring.get(act)
                            for q4 in range(2):
                                pi, pb = psT.get(pe)
                                pe.wait(tyn)
                                for kk in range(4):
                                    kc = q4 * 4 + kk
                                    mm = nc.tensor.transpose(pb[0:rows, kk * 128:(kk + 1) * 128], hb[:, kc, 1 + j0:1 + j0 + rows], ident[:])
                                tm = pe.mark(mm)
                                act.wait(tm)
                                ta = act.mark(nc.scalar.copy(out=ob[0:rows, q4 * 512:(q4 + 1) * 512], in_=pb[0:rows, :]))
                                psT.release(pi, ta)
                            pool.wait(ta)
                            tst = K.dma(pool, st_o[oi], out[ts + j0:ts + j0 + rows, :], ob[0:rows, :])
                            oring.release(oi, tst)
                            j0 += rows
                        hring.release(hi, tm)
                K.barrier()

        def phase_Q():
            st = ExitStack()
            with st:
                NB = 512
                Wq = sb(nc, st, "Wq", [128, KC, D], BF16)
                Wk = sb(nc, st, "Wk", [128, KC, D], BF16)
                Wv = sb(nc, st, "Wv", [128, KC, D], BF16)
                Wqr = sb(nc, st, "Wqr", [128, KC, D], BF16)
                Wkr = sb(nc, st, "Wkr", [128, KC, D], BF16)
                hT = [sb(nc, st, "qh%d" % i, [128, KC, NB], F32) for i in range(2)]
                cs = [sb(nc, st, "cs%d" % i, [128, 2, NB], F32) for i in range(2)]
                xlT = sb(nc, st, "qxl", [128, KC, NB], BF16)
                qst = [sb(nc, st, "qst%d" % i, [128, HEADS, NB], BF16) for i in range(2)]
                kst = [sb(nc, st, "kst%d" % i, [128, HEADS, NB], BF16) for i in range(2)]
                vst = [sb(nc, st, "vst%d" % i, [128, D], BF16) for i in range(2)]
                r1 = Ring([sb(nc, st, "r1_%d" % i, [128, NB], F32) for i in range(2)])
                r2 = Ring([sb(nc, st, "r2_%d" % i, [128, NB], F32) for i in range(2)])
                bufs = {
                    "sq": Ring([sb(nc, st, "qsq%d" % i, [128, NB], BF16) for i in range(2)]),
                    "sd": sb(nc, st, "qsd", [128, NB], F32),
                    "r": sb(nc, st, "qrb", [128, NB], F32),
                    "tm": Ring([sb(nc, st, "qtm%d" % i, [128, NB], F32) for i in range(2)]),
                }
                ldw = K.dsem("ldw_Q")
                for kc in range(KC):
                    K.dma(pool, ldw, Wq[:, kc, :], w_qkv[kc * 128:(kc + 1) * 128, 0:D])
                    K.dma(pool, ldw, Wk[:, kc, :], w_qkv[kc * 128:(kc + 1) * 128, D:2 * D])
                    t_w = K.dma(pool, ldw, Wv[:, kc, :], w_qkv[kc * 128:(kc + 1) * 128, 2 * D:3 * D])
                act.wait(t_w)
                dve.wait(t_w)
                for (Wsrc, Wrot) in ((Wq, Wqr), (Wk, Wkr)):
                    sv = Wsrc[:].rearrange("p k (a t b) -> p (k a) t b", t=2, b=16)
                    dv = Wrot[:].rearrange("p k (a t b) -> p (k a) t b", t=2, b=16)
                    act.mark(nc.scalar.mul(out=dv[:, :, 0, :], in_=sv[:, :, 1, :], mul=-1.0))
                    dve.mark(nc.vector.tensor_copy(out=dv[:, :, 1, :], in_=sv[:, :, 0, :]))
                pe.wait(t_w, act.last(), dve.last())

                ldh = [K.dsem("ldh_Q%d" % i) for i in range(2)]
                stq = [K.dsem("st_Q%d" % i) for i in range(2)]
                stv = [K.dsem("st_V%d" % i) for i in range(2)]
                hring = Ring(hT)
                vring = Ring(vst)
                psQ = Ring([(PSB[0], PSB[1]), (PSB[2], PSB[3])])
                psV = Ring([PSB[4], PSB[5]])
                psS = Ring([PSB[6]])
                xl_rel = None
                H2v = H2T.rearrange("(k p) t -> p k t", p=128)
                blocks = [(S, CTX, True)] + [(t0, NB, False) for t0 in range(0, S, NB)]
                st_rel = [None, None]
                for bi, (c0, n, is_ctx) in enumerate(blocks):
                    v = 1 if is_ctx else 0
                    hi, hb = hring.get(sp)
                    t_h = K.dma(sp, ldh[hi], hb[:, :, 0:n], H2v[:, :, c0:c0 + n])
                    csb_ = cs[hi]
                    if not is_ctx:
                        K.dma(sp, ldh[hi], csb_[:, 0, 0:n], cosT_d[:, c0:c0 + n])
                        t_h = K.dma(sp, ldh[hi], csb_[:, 1, 0:n], sinT_d[:, c0:c0 + n])
                    tx = stats_modulate(lambda kc: hb[:, kc, 0:n], n, 1, v, 8, 0, lambda kc: xlT[:, kc, 0:n], bufs,
                                        [t_h], psS, extra_wait=xl_rel)
                    si = bi % 2
                    qb, kb = qst[si], kst[si]
                    dve.wait(st_rel[si])
                    act.wait(st_rel[si])
                    tq = []
                    for (W, Wr, dstb, skip) in ((Wq, Wqr, qb, is_ctx), (Wk, Wkr, kb, False)):
                        if skip:
                            continue
                        for h in range(HEADS):
                            pi, (pa, pbr) = psQ.get(pe)
                            pe.wait(tx)
                            for kc in range(KC):
                                mm = nc.tensor.matmul(pa[:, 0:n], W[:, kc, h * 128:(h + 1) * 128], xlT[:, kc, 0:n],
                                                      start=(kc == 0), stop=(kc == KC - 1))
                            if is_ctx:
                                tm = pe.mark(mm)
                                act.wait(tm)
                                ta = act.mark(nc.scalar.copy(out=dstb[:, h, 0:n], in_=pa[:, 0:n]))
                                psQ.release(pi, ta)
                                tq.append(ta)
                                continue
                            for kc in range(KC):
                                mm = nc.tensor.matmul(pbr[:, 0:n], Wr[:, kc, h * 128:(h + 1) * 128], xlT[:, kc, 0:n],
                                                      start=(kc == 0), stop=(kc == KC - 1))
                            tm = pe.mark(mm)
                            i1, b1 = r1.get(dve)
                            i2, b2 = r2.get(dve)
                            dve.wait(tm, t_h)
                            ta = dve.mark(nc.vector.tensor_tensor(out=b1[:, 0:n], in0=pa[:, 0:n], in1=csb_[:, 0, 0:n], op=ALU.mult))
                            tb = dve.mark(nc.vector.tensor_tensor(out=b2[:, 0:n], in0=pbr[:, 0:n], in1=csb_[:, 1, 0:n], op=ALU.mult))
                            psQ.release(pi, tb)
                            dve.wait(ta, tb)
                            tc_ = dve.mark(nc.vector.tensor_tensor(out=dstb[:, h, 0:n], in0=b1[:, 0:n], in1=b2[:, 0:n], op=ALU.add))
                            r1.release(i1, tc_)
                            r2.release(i2, tc_)
                            tq.append(tc_)
                    pool.wait(tq)
                    koff = 0 if is_ctx else CTX
                    kc0 = (c0 - S) if is_ctx else c0
                    if not is_ctx:
                        K.dma(pool, stq[si], QT.rearrange("h p t -> p h t")[:, :, c0:c0 + n], qb[:, :, 0:n])
                    tstq = K.dma(pool, stq[si], KT.rearrange("h p t -> p h t")[:, :, koff + kc0:koff + kc0 + n], kb[:, :, 0:n])
                    st_rel[si] = tstq
                    for j in range(n // 128):
                        vi, vb = vring.get(act)
                        for dh in range(2):
                            pi, pb = psV.get(pe)
                            pe.wait(tx)
                            for kc in range(KC):
                                mm = nc.tensor.matmul(pb[:, :], xlT[:, kc, j * 128:(j + 1) * 128], Wv[:, kc, dh * 512:(dh + 1) * 512],
                                                      start=(kc == 0), stop=(kc == KC - 1))
                            tm = pe.mark(mm)
                            act.wait(tm)
                            ta = act.mark(nc.scalar.copy(out=vb[:, dh * 512:(dh + 1) * 512], in_=pb[:, :]))
                            psV.release(pi, ta)
                        kt = (koff + kc0) // 128 + j
                        pool.wait(ta)
                        tsv = K.dma(pool, stv[vi], VH.rearrange("h p (t e) -> p h t e", e=128)[:, :, kt, :],
                                    vb[:].rearrange("p (h e) -> p h e", e=128))
                        vring.release(vi, tsv)
                    xl_rel = tm
                    hring.release(hi, tx[-1], tq[-1])
                K.barrier()

        def phase_A():
            st = ExitStack()
            with st:
                QB = 512
                kTb = [sb(nc, st, "kTb%d" % i, [128, ST], BF16) for i in range(2)]
                vHb = [sb(nc, st, "vHb%d" % i, [128, NT, 128], BF16) for i in range(2)]
                qb = [sb(nc, st, "qb%d" % i, [128, QB], BF16) for i in range(2)]
                pT = Ring([sb(nc, st, "pT%d" % i, [128, 2 * QB], BF16) for i in range(4)])
                sel = sb(nc, st, "sel", [64, 128], F32)
                ones_f = sb(nc, st, "ones_f", [128, 128], F32)
                accr = Ring([sb(nc, st, "acc%d" % i, [128, QB], F32) for i in range(2)])
                sums_sb = sb(nc, st, "sums_sb", [64, QB], F32)
                pocp = sb(nc, st, "pocp", [128, 2 * QB], F32)
                rs = sb(nc, st, "rs", [128, 2 * QB], F32)
                o0 = sb(nc, st, "o0", [128, QB], F32)
                o1 = sb(nc, st, "o1", [128, QB], F32)
                osq = sb(nc, st, "osq", [128, QB], BF16)
                osd = sb(nc, st, "osd", [128, QB], F32)
                orr = sb(nc, st, "orr", [128, QB], F32)
                ob = [sb(nc, st, "aob%d" % i, [128, QB], BF16) for i in range(2)]
                subg2 = sb(nc, st, "subg2", [128, 1], F32)
                dve.mark(nc.vector.memset(sel[:], 1.0 / 32.0))
                dve.mark(nc.vector.memset(ones_f[:], 1.0))
                t_init = dve.mark(nc.vector.tensor_scalar(out=subg2[:], in0=subg[:], scalar1=(1.0 - LAMBDA_INIT), scalar2=None, op0=ALU.mult))
                pe.wait(t_init)
                ldk = [K.dsem("ldk%d" % i) for i in range(2)]
                ldq = [K.dsem("ldq%d" % i) for i in range(2)]
                sto = [K.dsem("sto%d" % i) for i in range(2)]
                kring = Ring(list(zip(kTb, vHb)))
                qring = Ring(qb)
                oring = Ring(ob)
                psS = Ring([PSD[0], PSD[1]])
                po = PSD[2]
                PSx = PSB[6]
                nqb = S // QB
                units = [(h, q_) for h in range(HEADS) for q_ in range(nqb)]
                kv = {}
                qd = {}
                state = {"fin_rel": None, "acc_rel": None}

                def load_kv(h):
                    ki, (kb_, vb_) = kring.get(sp)
                    K.dma(sp, ldk[ki], kb_[:], KT[h])
                    t_k = K.dma(sp, ldk[ki], vb_[:].rearrange("p t e -> p (t e)"), VH[h])
                    kv[h] = (ki, kb_, vb_, t_k)

                def load_q(ui):
                    h, q_ = units[ui]
                    qi, qbuf = qring.get(sp)
                    t_q = K.dma(sp, ldq[qi], qbuf[:], QT[h, :, q_ * QB:(q_ + 1) * QB])
                    qd[ui] = (qi, qbuf, t_q)

                def issue_S(kb_, qbuf, kt):
                    pi, sp_ = psS.get(pe)
                    nc.tensor.matmul(sp_[:, 0:QB], kb_[0:64, kt * 128:(kt + 1) * 128], qbuf[0:64, :], start=True, stop=True)
                    ts = pe.mark(nc.tensor.matmul(sp_[:, QB:2 * QB], kb_[64:128, kt * 128:(kt + 1) * 128], qbuf[64:128, :],
                                                  start=True, stop=True))
                    ppi, pbuf = pT.get(act)
                    act.wait(ts)
                    te = act.mark(nc.scalar.activation(out=pbuf[:, :], in_=sp_[:, :], func=AF.Exp, scale=0.125))
                    psS.release(pi, te)
                    return (ppi, pbuf, te)

                PSY = PSB[7]

                def finalize(tv, h, q_, acc, ai, td):
                    dve.wait(tv, state["fin_rel"])
                    dve.mark(nc.vector.tensor_copy(out=sums_sb[32:64, :], in_=PSx[32:64, :]))
                    tcp = dve.mark(nc.vector.tensor_copy(out=pocp[:, :], in_=po[:, :]))
                    state["acc_rel"] = tcp
                    yield
                    pe.wait(tcp, td, state.get("y_rel"))
                    tb0 = pe.mark(nc.tensor.matmul(PSY[:, :], ones_f[:], acc[:, :], start=True, stop=True))
                    accr.release(ai, tb0)
                    yield
                    dve.wait(tb0)
                    t1a = dve.mark(nc.vector.reciprocal(out=rs[:, 0:QB], in_=PSY[:, :]))
                    yield
                    pe.wait(t1a)
                    tb1 = pe.mark(nc.tensor.matmul(PSY[:, :], sel[32:64, :], sums_sb[32:64, :], start=True, stop=True))
                    yield
                    dve.wait(tb1)
                    t1b = dve.mark(nc.vector.reciprocal(out=rs[:, QB:2 * QB], in_=PSY[:, :]))
                    dve.wait(t1b)
                    dve.mark(nc.vector.tensor_tensor(out=o0[:], in0=pocp[:, 0:QB], in1=rs[:, 0:QB], op=ALU.mult))
                    t2 = dve.mark(nc.vector.tensor_tensor(out=o1[:], in0=pocp[:, QB:2 * QB], in1=rs[:, QB:2 * QB], op=ALU.mult))
                    dve.wait(t2)
                    t3 = dve.mark(nc.vector.scalar_tensor_tensor(out=o0[:], in0=o1[:], scalar=neglam[:, 0:1], in1=o0[:],
                                                                 op0=ALU.mult, op1=ALU.add))
                    yield
                    act.wait(t3)
                    t4 = act.mark(nc.scalar.activation(out=osq[:], in_=o0[:], func=AF.Square))
                    yield
                    pe.wait(t4, t1b)
                    t5 = pe.mark(nc.tensor.matmul(PSY[:, :], ones_bf[:], osq[:], start=True, stop=True))
                    yield
                    act.wait(t5)
                    t6 = act.mark(nc.scalar.activation(out=osd[:], in_=PSY[:, :], func=AF.Sqrt, bias=epsb[:, 0:1], scale=1.0 / 128))
                    state["y_rel"] = t6
                    yield
                    dve.wait(t6)
                    t7 = dve.mark(nc.vector.reciprocal(out=orr[:], in_=osd[:]))
                    oi2, obuf = oring.get(dve)
                    dve.wait(t7)
                    t8 = dve.mark(nc.vector.scalar_tensor_tensor(out=obuf[:], in0=o0[:], scalar=subg2[:, 0:1], in1=orr[:],
                                                                 op0=ALU.mult, op1=ALU.mult))
                    state["fin_rel"] = t8
                    pool.wait(t8)
                    tso = K.dma(pool, sto[oi2], AT[h * 128:(h + 1) * 128, q_ * QB:(q_ + 1) * QB], obuf[:])
                    oring.release(oi2, tso)
                    yield

                load_kv(0)
                load_q(0)
                flat = [(ui, kt) for ui in range(len(units)) for kt in range(NT)]
                issued = []
                state["sptr"] = 0

                def issue_next():
                    if state["sptr"] >= len(flat):
                        return
                    ui, kt = flat[state["sptr"]]
                    state["sptr"] += 1
                    hh, _ = units[ui]
                    if kt == 0:
                        pe.wait(kv[hh][3], qd[ui][2])
                    issued.append(issue_S(kv[hh][1], qd[ui][1], kt))

                for ui, (h, q_) in enumerate(units):
                    if q_ == 0 and h + 1 < HEADS:
                        load_kv(h + 1)
                    if ui + 1 < len(units):
                        load_q(ui + 1)
                    if ui == 0:
                        issue_next()
                        issue_next()
                    ki, kb_, vb_, t_k = kv[h]
                    qi, qbuf, t_q = qd[ui]
                    ai, acc = accr.get(dve)
                    for kt in range(NT):
                        if state.get("fin") is not None and kt >= 2 and kt % 2 == 0:
                            if next(state["fin"], "done") == "done":
                                state["fin"] = None
                        ppi, pbuf, te = issued.pop(0)
                        pe.wait(te, state["acc_rel"] if kt == 0 else None)
                        issue_next()
                        f, l_ = (kt == 0), (kt == NT - 1)
                        nc.tensor.matmul(po[:, 0:QB], vb_[:, kt, :], pbuf[:, 0:QB], start=f, stop=l_)
                        nc.tensor.matmul(po[:, QB:2 * QB], vb_[:, kt, :], pbuf[:, QB:2 * QB], start=f, stop=l_)
                        tv = pe.mark(nc.tensor.matmul(PSx[32:64, :], ones_bf[:, 0:32], pbuf[:, QB:2 * QB], start=f, stop=l_,
                                                      tile_position=(0, 32)))
                        dve.wait(te)
                        if kt == 0:
                            td = dve.mark(nc.vector.tensor_copy(out=acc[:, :], in_=pbuf[:, 0:QB]))
                        else:
                            td = dve.mark(nc.vector.tensor_tensor(out=acc[:, :], in0=acc[:, :], in1=pbuf[:, 0:QB], op=ALU.add))
                        pT.release(ppi, tv, td)
                    qring.release(qi, tv)
                    if q_ == nqb - 1:
                        kring.release(ki, tv)
                    if state.get("fin") is not None:
                        for _ in state["fin"]:
                            pass
                    state["fin"] = finalize(tv, h, q_, acc, ai, td)
                    next(state["fin"])
                for _ in state["fin"]:
                    pass
                K.barrier()

        def phase_O():
            st = ExitStack()
            with st:
                NB = 512
                Wo = sb(nc, st, "Wo", [128, KC, D], BF16)
                hT = [sb(nc, st, "oh%d" % i, [128, KC, NB], F32) for i in range(2)]
                aT = [sb(nc, st, "oa%d" % i, [128, KC, NB], BF16) for i in range(2)]
                ldw = K.dsem("ldw_O")
                for kc in range(KC):
                    t_w = K.dma(pool, ldw, Wo[:, kc, :], w_o[kc * 128:(kc + 1) * 128, :])
                pe.wait(t_w)
                ldh = [K.dsem("ldh_O%d" % i) for i in range(2)]
                st_h = [K.dsem("st_O%d" % i) for i in range(2)]
                hring = Ring(hT)
                psO = Ring([PSB[0], PSB[1], PSB[2], PSB[3]])
                H2v = H2T.rearrange("(k p) t -> p k t", p=128)
                H1v = H1T.rearrange("(k p) t -> p k t", p=128)
                ATv = AT.rearrange("(k p) t -> p k t", p=128)
                for c0 in range(0, S, NB):
                    hi, hb = hring.get(sp)
                    ab = aT[hi]
                    K.dma(sp, ldh[hi], hb[:], H2v[:, :, c0:c0 + NB])
                    t_h = K.dma(sp, ldh[hi], ab[:], ATv[:, :, c0:c0 + NB])
                    for dc in range(KC):
                        pi, pb = psO.get(pe)
                        pe.wait(t_h)
                        for hc in range(KC):
                            mm = nc.tensor.matmul(pb[:, :], Wo[:, hc, dc * 128:(dc + 1) * 128], ab[:, hc, :],
                                                  start=(hc == 0), stop=(hc == KC - 1))
                        tm = pe.mark(mm)
                        dve.wait(tm, t_h)
                        td = dve.mark(nc.vector.scalar_tensor_tensor(out=hb[:, dc, :], in0=pb[:, :], scalar=modv(1, 0, 16 + dc),
                                                                     in1=hb[:, dc, :], op0=ALU.mult, op1=ALU.add))
                        psO.release(pi, td)
                    pool.wait(td)
                    tst = K.dma(pool, st_h[hi], H1v[:, :, c0:c0 + NB], hb[:])
                    hring.release(hi, tst)
                K.barrier()

        plan = [("mod", phase_mod), ("G", lambda: phase_G(3)),
                ("F0", lambda: phase_F(0, H1T, H2T, False, False, 360)),
                ("Q", phase_Q), ("A", phase_A), ("O", phase_O),
                ("F1", lambda: phase_F(1, H1T, None, False, True, 360))]
        for name, fn in plan:
            fn()
            if stop_after == name:
                break
        K.barrier()
    return nc


def _rope_tables(S):
    rows = S // 64
    row_pos = np.broadcast_to(np.arange(rows, dtype=np.float32)[:, None], (rows, 64)).reshape(-1)
    col_pos = np.broadcast_to(np.arange(64, dtype=np.float32)[None, :], (rows, 64)).reshape(-1)
    n_freq = 16
    inv_freq = (np.float32(10000.0) ** (-np.arange(n_freq, dtype=np.float32) / np.float32(n_freq))).astype(np.float32)
    ang_r = row_pos[:, None] * inv_freq
    ang_c = col_pos[:, None] * inv_freq
    ang = np.concatenate([ang_r, ang_r, ang_c, ang_c], axis=-1).astype(np.float32)
    cosT = np.ascontiguousarray(np.cos(ang).T.astype(np.float32))
    sinT = np.ascontiguousarray(np.sin(ang).T.astype(np.float32))
    return np.concatenate([cosT, cosT], 0), np.concatenate([sinT, sinT], 0)


def _fm(vec, nchunk):
    return np.ascontiguousarray(np.asarray(vec, np.float32).reshape(nchunk, 128).T)


def make_in_maps(inp, S, ncores):
    f = lambda a: np.ascontiguousarray(np.asarray(a, np.float32))
    cosT, sinT = _rope_tables(S)
    shared = {
        "ada_w": f(inp["ada_w"]),
        "ada_bT": np.ascontiguousarray(np.stack([_fm(inp["ada_b"][l], 48) for l in range(2)], 1).reshape(128, 96)),
        "gm_w_in": f(inp["gm_w_in"][0]),
        "gm_gT": _fm(inp["gm_norm_g"][0], 16),
        "gm_wsT": np.ascontiguousarray(np.transpose(f(inp["gm_w_s"][0]), (2, 0, 1)).reshape(128, 1024)),
        "gm_bs": f(inp["gm_b_s"][0]).reshape(1, 1024),
        "gm_w_out": f(inp["gm_w_out"][0]),
        "w_qkv": f(inp["da_w_qkv"][0]),
        "lamv": np.concatenate([f(inp["da_lambda_q1"][0]), f(inp["da_lambda_k1"][0]),
                                f(inp["da_lambda_q2"][0]), f(inp["da_lambda_k2"][0])]).reshape(1, 256),
        "sublnT": f(inp["da_subln_g"][0]).reshape(128, 1),
        "w_o": f(inp["da_w_out"][0]),
        "w_up": f(inp["ffn_w_up"]),
        "convT": np.ascontiguousarray(np.stack(
            [np.stack([_fm(inp["ffn_conv_w"][l][0], NCC), _fm(inp["ffn_conv_w"][l][1], NCC),
                       _fm(inp["ffn_conv_w"][l][2], NCC), _fm(inp["ffn_conv_b"][l], NCC)], 1) for l in range(2)], 1).reshape(128, 2 * 4 * NCC)),
        "w_dn": f(inp["ffn_w_down"]),
        "fngT": _fm(inp["final_norm_g"], KC),
        "ident": np.eye(128, dtype=np.float32),
        "cosT": cosT, "sinT": sinT,
    }
    cc = _fm(inp["c_ctx"], KC)
    maps = []
    for b in range(ncores):
        m = dict(shared)
        m["x"] = f(inp["x"][b])
        m["ctx"] = f(inp["ctx"][b])
        cb = _fm(inp["c"][b], KC)
        m["cvec"] = np.ascontiguousarray(np.stack([cb, cc], 2).reshape(128, 2 * KC))
        maps.append(m)
    return maps


_NC_CACHE = {}


def kernel(**inputs):
    S = int(inputs["x"].shape[1])
    B = int(inputs["x"].shape[0])
    key = (S,)
    if key not in _NC_CACHE:
        _NC_CACHE[key] = build_program(S)
    nc = _NC_CACHE[key]
    in_maps = make_in_maps(inputs, S, B)
    res = run_bass_kernel_spmd(nc, in_maps, core_ids=list(range(B)))
    return np.stack([np.asarray(r["out"], np.float32) for r in res.results], 0)
```

```python
import math
from contextlib import ExitStack

import numpy as np
import ml_dtypes
import concourse.bass as bass
import concourse.mybir as mybir
from concourse.bass_utils import run_bass_kernel_spmd

F32 = mybir.dt.float32
BF16 = mybir.dt.bfloat16
AF = mybir.ActivationFunctionType
ALU = mybir.AluOpType

D = 1024
KC = 8
CTX = 256
NCORES = 8
SEQ = 8192
EPS = 1e-6
GMW = 2048
FFN = 2816
NCC = 44
NPAIR = 22
HEADS = 8
LAMBDA_INIT = 0.8 - 0.6 * math.exp(-0.3 * 1)


class Eng:
    def __init__(self, K, name, eng):
        self.K, self.name, self.e = K, name, eng
        self.sem = K.newsem("e_" + name)
        self.cnt = 0
        self.seen = {}

    def wait(self, *tks):
        for tk in tks:
            if tk is None:
                continue
            if isinstance(tk, list):
                self.wait(*tk)
                continue
            sem, v = tk
            if self.seen.get(sem, 0) < v:
                self.e.wait_ge(sem, v)
                self.seen[sem] = v

    def mark(self, ins):
        self.cnt += 1
        ins.then_inc(self.sem, 1)
        return (self.sem, self.cnt)

    def last(self):
        return (self.sem, self.cnt) if self.cnt else None


class DSem:
    def __init__(self, K, name):
        self.sem = K.newsem(name)
        self.cnt = 0

    def tk(self):
        return (self.sem, self.cnt) if self.cnt else None


class Ring:
    def __init__(self, bufs):
        self.bufs = list(bufs)
        self.rel = [None] * len(self.bufs)
        self.i = 0

    def get(self, *engs):
        i = self.i % len(self.bufs)
        self.i += 1
        for e in engs:
            e.wait(self.rel[i])
        self.rel[i] = None
        return i, self.bufs[i]

    def release(self, i, *tks):
        cur = self.rel[i]
        lst = [] if cur is None else list(cur)
        lst.extend(t for t in tks if t is not None)
        self.rel[i] = lst


class Kern:
    def __init__(self, nc, es):
        self.nc, self.es = nc, es
        self.nsem = 0
        self.pe = Eng(self, "pe", nc.tensor)
        self.act = Eng(self, "act", nc.scalar)
        self.dve = Eng(self, "dve", nc.vector)
        self.pool = Eng(self, "pool", nc.gpsimd)
        self.sp = Eng(self, "sp", nc.sync)
        self.engs = [self.pe, self.act, self.dve, self.pool, self.sp]
        self.dsems = []

    def newsem(self, name):
        self.nsem += 1
        return self.es.enter_context(self.nc.semaphore(name))

    def dsem(self, name):
        d = DSem(self, name)
        self.dsems.append(d)
        return d

    def dma(self, q, ds, out, in_):
        ins = q.e.dma_start(out=out, in_=in_)
        ds.cnt += 16
        ins.then_inc(ds.sem, 16)
        return (ds.sem, ds.cnt)

    def barrier(self):
        tks = [e.last() for e in self.engs] + [d.tk() for d in self.dsems]
        for e in self.engs:
            e.wait(*tks)


_UID = [0]


def sb(nc, st, name, shape, dt):
    _UID[0] += 1
    return st.enter_context(nc.sbuf_tensor("s%d_%s" % (_UID[0], name), list(shape), dt))


def build_program(S=SEQ, stop_after=None, dbg=False):
    nc = bass.Bass("TRN2", target_bir_lowering=False)
    ST = S + CTX
    NT = ST // 128
    kind_scr = "ExternalOutput" if dbg else "Internal"

    def din(name, shape, dt=F32):
        return nc.dram_tensor(name, list(shape), dt, kind="ExternalInput").ap()

    def dscr(name, shape, dt):
        return nc.dram_tensor(name, list(shape), dt, kind=kind_scr).ap()

    x = din("x", [S, D])
    ctx = din("ctx", [CTX, D])
    cvec = din("cvec", [128, KC * 2])
    ada_w = din("ada_w", [2, D, 6 * D])
    ada_bT = din("ada_bT", [128, 2 * 48])
    gm_w_in = din("gm_w_in", [D, 2 * GMW])
    gm_gT = din("gm_gT", [128, 16])
    gm_wsT = din("gm_wsT", [128, 8 * 128])
    gm_bs = din("gm_bs", [1, 8 * 128])
    gm_w_out = din("gm_w_out", [GMW, D])
    w_qkv = din("w_qkv", [D, 3 * D])
    lamv = din("lamv", [1, 256])
    sublnT = din("sublnT", [128, 1])
    w_o = din("w_o", [D, D])
    w_up = din("w_up", [2, D, 2 * FFN])
    convT = din("convT", [128, 2 * 4 * NCC])
    w_dn = din("w_dn", [2, FFN, D])
    fngT = din("fngT", [128, KC])
    ident_d = din("ident", [128, 128])
    cosT_d = din("cosT", [128, S])
    sinT_d = din("sinT", [128, S])
    out = nc.dram_tensor("out", [S, D], F32, kind="ExternalOutput").ap()

    H1T = dscr("H1T", [D, ST], F32)
    H2T = dscr("H2T", [D, ST], F32)
    QT = dscr("QT", [HEADS, 128, S], BF16)
    KT = dscr("KT", [HEADS, 128, ST], BF16)
    VH = dscr("VH", [HEADS, 128, NT * 128], BF16)
    AT = dscr("AT", [D, S], BF16)

    es = ExitStack()
    with es:
        K = Kern(nc, es)
        pe, act, dve, pool, sp = K.pe, K.act, K.dve, K.pool, K.sp

        PSD = [es.enter_context(nc.psum_tensor("psd%d" % i, [128, 1024], F32)) for i in range(4)]
        PSB = [PSD[i // 2][:, (i % 2) * 512:(i % 2 + 1) * 512] for i in range(8)]
        ident = sb(nc, es, "ident", [128, 128], F32)
        ones_bf = sb(nc, es, "ones_bf", [128, 128], BF16)
        modT = sb(nc, es, "modT", [128, 2, 2, 48], F32)
        convS = sb(nc, es, "convS", [128, 2, 4, NCC], F32)
        gmg = sb(nc, es, "gmg", [128, 16], F32)
        subg = sb(nc, es, "subg", [128, 1], F32)
        fng = sb(nc, es, "fng", [128, KC], F32)
        neglam = sb(nc, es, "neglam", [128, 1], F32)
        epsb = sb(nc, es, "epsb", [128, 1], F32)

        ld_c = K.dsem("ld_c")
        K.dma(sp, ld_c, ident[:], ident_d[:])
        K.dma(sp, ld_c, convS[:].rearrange("p a b c -> p (a b c)"), convT[:])
        K.dma(sp, ld_c, gmg[:], gm_gT[:])
        K.dma(sp, ld_c, subg[:], sublnT[:])
        t_const = K.dma(sp, ld_c, fng[:], fngT[:])
        dve.mark(nc.vector.memset(ones_bf[:], 1.0))
        t_ones = dve.mark(nc.vector.memset(epsb[:], EPS))
        for e in (pe, act, dve):
            e.wait(t_const, t_ones)

        def modv(l, v, j):
            return modT[:, l, v, j:j + 1]

        def phase_mod():
            st = ExitStack()
            with st:
                c32 = sb(nc, st, "c32", [128, KC * 2], F32)
                csb = sb(nc, st, "csb", [128, KC, 2], BF16)
                abT = sb(nc, st, "abT", [128, 2, 48], F32)
                lam_s = sb(nc, st, "lam_s", [1, 256], F32)
                lam_p = sb(nc, st, "lam_p", [1, 128], F32)
                lam_r = sb(nc, st, "lam_r", [1, 4], F32)
                ones1 = sb(nc, st, "ones1", [1, 128], F32)
                wa = [sb(nc, st, "wa%d" % i, [128, KC, 1024], BF16) for i in range(2)]
                wring = Ring(wa)
                ld = K.dsem("ld_mod")
                ldw = [K.dsem("ld_wa%d" % i) for i in range(2)]
                K.dma(sp, ld, c32[:], cvec[:])
                K.dma(sp, ld, abT[:].rearrange("p a b -> p (a b)"), ada_bT[:])
                t_in = K.dma(sp, ld, lam_s[:], lamv[:])
                act.wait(t_in)
                t_c = act.mark(nc.scalar.activation(out=csb[:].rearrange("p a b -> p (a b)"), in_=c32[:], func=AF.Silu))
                dve.wait(t_in)
                dve.mark(nc.vector.memset(ones1[:], 1.0))
                dve.mark(nc.vector.tensor_tensor(out=lam_p[:, 0:64], in0=lam_s[:, 0:64], in1=lam_s[:, 64:128], op=ALU.mult))
                t1 = dve.mark(nc.vector.tensor_tensor(out=lam_p[:, 64:128], in0=lam_s[:, 128:192], in1=lam_s[:, 192:256], op=ALU.mult))
                dve.wait(t1)
                t2 = dve.mark(nc.vector.reduce_sum(out=lam_r[:, 0:2], in_=lam_p[:].rearrange("p (a b) -> p a b", a=2),
                                                   axis=mybir.AxisListType.X))
                act.wait(t2)
                t3 = act.mark(nc.scalar.activation(out=lam_r[:, 2:4], in_=lam_r[:, 0:2], func=AF.Exp))
                dve.wait(t3)
                t4 = dve.mark(nc.vector.tensor_tensor(out=lam_r[:, 0:1], in0=lam_r[:, 3:4], in1=lam_r[:, 2:3], op=ALU.subtract))
                dve.wait(t4)
                t5 = dve.mark(nc.vector.tensor_scalar(out=lam_r[:, 1:2], in0=lam_r[:, 0:1], scalar1=-LAMBDA_INIT, scalar2=None, op0=ALU.add))
                pe.wait(t5)
                t6 = pe.mark(nc.tensor.matmul(PSB[7][:, 0:1], ones1[:], lam_r[:, 1:2], start=True, stop=True))
                dve.wait(t6)
                t7 = dve.mark(nc.vector.tensor_copy(out=neglam[:], in_=PSB[7][:, 0:1]))
                pring = Ring([PSB[0], PSB[1]])
                for l in range(2):
                    for cg in range(6):
                        wi, wbuf = wring.get(pool)
                        for kc in range(KC):
                            tw = K.dma(pool, ldw[wi], wbuf[:, kc, :], ada_w[l, kc * 128:(kc + 1) * 128, cg * 1024:(cg + 1) * 1024])
                        pi, pb = pring.get(pe)
                        pe.wait(tw, t_c)
                        for fcl in range(8):
                            for kc in range(KC):
                                mm = nc.tensor.matmul(pb[:, fcl * 2:fcl * 2 + 2], wbuf[:, kc, fcl * 128:(fcl + 1) * 128],
                                                      csb[:, kc, :], start=(kc == 0), stop=(kc == KC - 1))
                        tm = pe.mark(mm)
                        wring.release(wi, tm)
                        dve.wait(tm, t_in)
                        td = dve.mark(nc.vector.tensor_tensor(
                            out=modT[:, l, :, cg * 8:(cg + 1) * 8],
                            in0=pb[:, 0:16].rearrange("p (f v) -> p v f", v=2),
                            in1=abT[:, l, cg * 8:(cg + 1) * 8].unsqueeze(1).to_broadcast([128, 2, 8]),
                            op=ALU.add))
                        pring.release(pi, td)
                dve.wait(dve.last())
                for l in range(2):
                    for j0 in (8, 32):
                        dve.mark(nc.vector.tensor_scalar(out=modT[:, l, :, j0:j0 + 8], in0=modT[:, l, :, j0:j0 + 8],
                                                         scalar1=1.0, scalar2=None, op0=ALU.add))
                K.barrier()

        def stats_modulate(hsrc, n, l, v, jsc, jsh, xdst, bufs, t_h, ps_ring, extra_wait=None):
            sqr, sdb, rb, tmr = bufs["sq"], bufs["sd"], bufs["r"], bufs["tm"]
            pi, pb = ps_ring.get(pe)
            for kc in range(KC):
                si, sq = sqr.get(act)
                act.wait(t_h)
                ts = act.mark(nc.scalar.activation(out=sq[:, 0:n], in_=hsrc(kc), func=AF.Square))
                pe.wait(ts)
                mm = nc.tensor.matmul(pb[:, 0:n], ones_bf[:], sq[:, 0:n], start=(kc == 0), stop=(kc == KC - 1))
                tm = pe.mark(mm)
                sqr.release(si, tm)
            act.wait(tm, bufs.get("sd_rel"))
            t_sd = act.mark(nc.scalar.activation(out=sdb[:, 0:n], in_=pb[:, 0:n], func=AF.Sqrt, bias=epsb[:, 0:1], scale=1.0 / D))
            ps_ring.release(pi, t_sd)
            dve.wait(t_sd, bufs.get("r_rel"))
            t_r = dve.mark(nc.vector.reciprocal(out=rb[:, 0:n], in_=sdb[:, 0:n]))
            bufs["sd_rel"] = t_r
            tks = []
            for kc in range(KC):
                ti, tmb = tmr.get(dve)
                dve.wait(t_r, t_h)
                tt = dve.mark(nc.vector.scalar_tensor_tensor(out=tmb[:, 0:n], in0=hsrc(kc), scalar=modv(l, v, jsc + kc),
                                                             in1=rb[:, 0:n], op0=ALU.mult, op1=ALU.mult))
                act.wait(tt, extra_wait)
                ta = act.mark(nc.scalar.activation(out=xdst(kc), in_=tmb[:, 0:n], func=AF.Identity,
                                                   bias=modv(l, v, jsh + kc), scale=1.0))
                tmr.release(ti, ta)
                tks.append(ta)
            bufs["r_rel"] = tt
            return tks

        def phase_G(TBG=3):
            st = ExitStack()
            with st:
                NB = TBG * 128
                WinU = sb(nc, st, "WinU", [128, KC, GMW], BF16)
                WinV = sb(nc, st, "WinV", [128, KC, GMW], BF16)
                WsT = sb(nc, st, "WsT", [128, 8, 128], BF16)
                Wout = sb(nc, st, "Wout", [128, 16, D], BF16)
                bsr = sb(nc, st, "bsr", [128, 8, 128], F32)
                xt = [sb(nc, st, "xt%d" % i, [128, D], F32) for i in range(TBG)]
                hT = [sb(nc, st, "hT%d" % i, [128, KC, NB], F32) for i in range(2)]
                xlT = sb(nc, st, "xlT", [128, KC, NB], BF16)
                uT = sb(nc, st, "uT", [128, 16, NB], BF16)
                gv = [sb(nc, st, "gv%d" % i, [128, GMW], F32) for i in range(2)]
                vn = sb(nc, st, "vn", [128, TBG, GMW], BF16)
                junk = sb(nc, st, "junk", [128, GMW], BF16)
                vst = sb(nc, st, "vst", [128, 4 * TBG * 2], F32)
                bufs = {
                    "sq": Ring([sb(nc, st, "sq%d" % i, [128, NB], BF16) for i in range(2)]),
                    "sd": sb(nc, st, "sdb", [128, NB], F32),
                    "r": sb(nc, st, "rb", [128, NB], F32),
                    "tm": Ring([sb(nc, st, "tm%d" % i, [128, NB], F32) for i in range(2)]),
                }
                t2r = Ring([sb(nc, st, "t2_%d" % i, [128, NB], F32) for i in range(2)])
                ldw = K.dsem("ldw_G")
                for kc in range(KC):
                    K.dma(pool, ldw, WinU[:, kc, :], gm_w_in[kc * 128:(kc + 1) * 128, 0:GMW])
                    K.dma(pool, ldw, WinV[:, kc, :], gm_w_in[kc * 128:(kc + 1) * 128, GMW:2 * GMW])
                for fc in range(16):
                    K.dma(pool, ldw, Wout[:, fc, :], gm_w_out[fc * 128:(fc + 1) * 128, :])
                K.dma(pool, ldw, WsT[:].rearrange("p a b -> p (a b)"), gm_wsT[:])
                t_w = K.dma(pool, ldw, bsr[:].rearrange("p a b -> p (a b)"), gm_bs.partition_broadcast(128))
                pe.wait(t_w)
                dve.wait(t_w)

                ldx = [K.dsem("ldx_G%d" % i) for i in range(TBG)]
                st_h = [K.dsem("st_G%d" % i) for i in range(2)]
                xring = Ring(xt)
                hring = Ring(hT)
                psT = Ring([PSB[0], PSB[1]])
                psA = Ring([PSB[2], PSB[3]])
                psM = Ring([PSB[4], PSB[5]])
                psO = Ring([PSB[6], PSB[7]])
                gvr = Ring(gv)
                vn_rel = None
                xl_rel = None
                u_rel = None

                blocks = []
                t0 = 0
                while t0 < S:
                    nt = min(TBG, (S - t0) // 128)
                    blocks.append((x, t0, t0, nt, 0))
                    t0 += nt * 128
                blocks.append((ctx, 0, S, CTX // 128, 1))

                for bidx, (src, r0, c0, nt, v) in enumerate(blocks):
                    n = nt * 128
                    tl = []
                    for j in range(nt):
                        xi, xb = xring.get(sp)
                        tl.append((xi, xb, K.dma(sp, ldx[xi], xb[:], src[r0 + j * 128:r0 + (j + 1) * 128, :])))
                    hi, hb = hring.get(act)
                    tev = []
                    for kc in range(KC):
                        pi, pb = psT.get(pe)
                        for j in range(nt):
                            pe.wait(tl[j][2])
                            mm = nc.tensor.transpose(pb[:, j * 128:(j + 1) * 128], tl[j][1][:, kc * 128:(kc + 1) * 128], ident[:])
                        tm = pe.mark(mm)
                        act.wait(tm)
                        ta = act.mark(nc.scalar.copy(out=hb[:, kc, 0:n], in_=pb[:, 0:n]))
                        psT.release(pi, ta)
                        tev.append(ta)
                    for j in range(nt):
                        xring.release(tl[j][0], tm)
                    tx = stats_modulate(lambda kc: hb[:, kc, 0:n], n, 0, v, 8, 0, lambda kc: xlT[:, kc, 0:n], bufs,
                                        tev, psT, extra_wait=xl_rel)
                    tvn = []
                    for j in range(nt):
                        gi, gb = gvr.get(act)
                        for cb in range(4):
                            pi, pb = psA.get(pe)
                            pe.wait(tx)
                            for kc in range(KC):
                                mm = nc.tensor.matmul(pb[:, :], xlT[:, kc, j * 128:(j + 1) * 128], WinV[:, kc, cb * 512:(cb + 1) * 512],
                                                      start=(kc == 0), stop=(kc == KC - 1))
                            tm = pe.mark(mm)
                            act.wait(tm)
                            ta = act.mark(nc.scalar.activation(out=gb[:, cb * 512:(cb + 1) * 512], in_=pb[:, :], func=AF.Gelu_apprx_tanh))
                            psA.release(pi, ta)
                        act.wait(ta)
                        c = j * 4 + (bidx % 2) * 4 * TBG
                        ts = act.mark(nc.scalar.activation(out=junk[:], in_=gb[:], func=AF.Square, accum_out=vst[:, c:c + 1]))
                        act.wait(ts)
                        ts2 = act.mark(nc.scalar.activation(out=vst[:, c + 1:c + 2], in_=vst[:, c:c + 1], func=AF.Sqrt,
                                                            bias=epsb[:, 0:1], scale=1.0 / GMW))
                        dve.wait(ts2)
                        tr = dve.mark(nc.vector.reciprocal(out=vst[:, c + 2:c + 3], in_=vst[:, c + 1:c + 2]))
                        dve.wait(tr, vn_rel if j == 0 else None)
                        tn = dve.mark(nc.vector.tensor_scalar(out=vn[:, j, :], in0=gb[:], scalar1=vst[:, c + 2:c + 3], scalar2=None,
                                                              op0=ALU.mult))
                        gvr.release(gi, tn)
                        tvn.append(tn)
                    tu = []
                    for fc in range(16):
                        pi, pb = psA.get(pe)
                        pe.wait(tx)
                        for kc in range(KC):
                            mm = nc.tensor.matmul(pb[:, 0:n], WinU[:, kc, fc * 128:(fc + 1) * 128], xlT[:, kc, 0:n],
                                                  start=(kc == 0), stop=(kc == KC - 1))
                        tm = pe.mark(mm)
                        act.wait(tm, u_rel)
                        ta = act.mark(nc.scalar.activation(out=uT[:, fc, 0:n], in_=pb[:, 0:n], func=AF.Gelu_apprx_tanh))
                        psA.release(pi, ta)
                        tu.append(ta)
                    xl_rel = tm
                    tuv = []
                    for fc in range(16):
                        g = fc // 2
                        pi, pb = psM.get(pe)
                        pe.wait(tvn)
                        for j in range(nt):
                            mm = nc.tensor.matmul(pb[:, j * 128:(j + 1) * 128], vn[:, j, fc * 128:(fc + 1) * 128], WsT[:, g, :],
                                                  start=True, stop=True)
                        tm = pe.mark(mm)
                        ti, tb = t2r.get(dve)
                        dve.wait(tm)
                        td = dve.mark(nc.vector.scalar_tensor_tensor(
                            out=tb[:, 0:n].rearrange("p (a b) -> p a b", a=nt),
                            in0=pb[:, 0:n].rearrange("p (a b) -> p a b", a=nt), scalar=gmg[:, fc:fc + 1],
                            in1=bsr[:, g, :].unsqueeze(1).to_broadcast([128, nt, 128]), op0=ALU.mult, op1=ALU.add))
                        psM.release(pi, td)
                        dve.wait(td, tu[fc])
                        td2 = dve.mark(nc.vector.tensor_tensor(out=uT[:, fc, 0:n], in0=tb[:, 0:n], in1=uT[:, fc, 0:n], op=ALU.mult))
                        t2r.release(ti, td2)
                        tuv.append(td2)
                    vn_rel = tm
                    for dc in range(KC):
                        pi, pb = psO.get(pe)
                        pe.wait(tuv)
                        for fc in range(16):
                            mm = nc.tensor.matmul(pb[:, 0:n], Wout[:, fc, dc * 128:(dc + 1) * 128], uT[:, fc, 0:n],
                                                  start=(fc == 0), stop=(fc == 15))
                        tm = pe.mark(mm)
                        dve.wait(tm, tev[dc])
                        td = dve.mark(nc.vector.scalar_tensor_tensor(out=hb[:, dc, 0:n], in0=pb[:, 0:n], scalar=modv(0, v, 16 + dc),
                                                                     in1=hb[:, dc, 0:n], op0=ALU.mult, op1=ALU.add))
                        psO.release(pi, td)
                    u_rel = tm
                    pool.wait(td)
                    tst = K.dma(pool, st_h[hi], H1T.rearrange("(k p) t -> p k t", p=128)[:, :, c0:c0 + n], hb[:, :, 0:n])
                    hring.release(hi, tst)
                K.barrier()

        def phase_F(l, SRC, DST, with_o, final, NV):
            st = ExitStack()
            with st:
                NCOL = NV + 2
                Wup = sb(nc, st, "Wup", [128, KC, 2 * FFN], BF16)
                Wdn = sb(nc, st, "Wdn", [128, NPAIR, D], BF16)
                hT = [sb(nc, st, "fh%d" % i, [128, KC, NCOL], F32) for i in range(2)]
                x2T = sb(nc, st, "x2T", [128, KC, NCOL], BF16)
                actT = sb(nc, st, "actT", [128, NPAIR, NCOL], BF16)
                zc = Ring([sb(nc, st, "zc%d" % i, [128, NCOL], F32) for i in range(2)])
                sa = Ring([sb(nc, st, "sa%d" % i, [128, NCOL], F32) for i in range(2)])
                bufs = {
                    "sq": Ring([sb(nc, st, "fsq%d" % i, [128, NCOL], BF16) for i in range(2)]),
                    "sd": sb(nc, st, "fsd", [128, NCOL], F32),
                    "r": sb(nc, st, "frb", [128, NCOL], F32),
                    "tm": Ring([sb(nc, st, "ftm%d" % i, [128, NCOL], F32) for i in range(2)]),
                }
                ldw = K.dsem("ldw_F%d" % l)
                for kc in range(KC):
                    K.dma(pool, ldw, Wup[:, kc, :], w_up[l, kc * 128:(kc + 1) * 128, :])
                for c in range(NPAIR):
                    t_w = K.dma(pool, ldw, Wdn[:, c, :], w_dn[l, c * 128:(c + 1) * 128, :])
                if with_o:
                    Wo = sb(nc, st, "Wo", [128, KC, D], BF16)
                    aT = [sb(nc, st, "aT%d" % i, [128, KC, NCOL], BF16) for i in range(2)]
                    for kc in range(KC):
                        t_w = K.dma(pool, ldw, Wo[:, kc, :], w_o[kc * 128:(kc + 1) * 128, :])
                if final:
                    ost = [sb(nc, st, "ost%d" % i, [128, D], F32) for i in range(2)]
                    oring = Ring(ost)
                    st_o = [K.dsem("st_o%d" % i) for i in range(2)]
                pe.wait(t_w)
                ldh = [K.dsem("ldh_F%d_%d" % (l, i)) for i in range(2)]
                st_h = [K.dsem("st_F%d_%d" % (l, i)) for i in range(2)]
                hring = Ring(hT)
                psZ = Ring([PSB[0], PSB[1], PSB[2]])
                psO = Ring([PSB[3], PSB[4]])
                psS = Ring([PSB[5]])
                psT = Ring([PSB[6], PSB[7]])
                x2_rel = None
                a_rel = None
                for i in range(2):
                    dve.mark(nc.vector.memset(hT[i][:].rearrange("p a b -> p (a b)"), 0.0))
                    if with_o:
                        dve.mark(nc.vector.memset(aT[i][:].rearrange("p a b -> p (a b)"), 0.0))
                t_ms = dve.last()

                segs = [(0, S)] + ([] if final else [(S, CTX)])
                blocks = []
                for (base, L) in segs:
                    nb = -(-L // NV)
                    nv = -(-L // nb)
                    ts = 0
                    while ts < L:
                        blocks.append((base, L, ts, min(nv, L - ts), 1 if base == S else 0))
                        ts += nv
                SRCv = SRC.rearrange("(k p) t -> p k t", p=128)
                if DST is not None:
                    DSTv = DST.rearrange("(k p) t -> p k t", p=128)
                ATv = AT.rearrange("(k p) t -> p k t", p=128)

                for (base, L, ts, nv, v) in blocks:
                    ncol = nv + 2
                    lo = max(ts - 1, 0)
                    hi_ = min(ts + nv + 1, L)
                    c_lo = lo - (ts - 1)
                    c_hi = c_lo + (hi_ - lo)
                    hi, hb = hring.get(sp)
                    sp.wait(t_ms)
                    t_h = K.dma(sp, ldh[hi], hb[:, :, c_lo:c_hi], SRCv[:, :, base + lo:base + hi_])
                    if with_o:
                        ab = aT[hi]
                        t_h = K.dma(sp, ldh[hi], ab[:, :, c_lo:c_hi], ATv[:, :, lo:hi_])
                        tres = []
                        for dc in range(KC):
                            pi, pb = psO.get(pe)
                            pe.wait(t_h)
                            for hc in range(KC):
                                mm = nc.tensor.matmul(pb[:, 0:ncol], Wo[:, hc, dc * 128:(dc + 1) * 128], ab[:, hc, 0:ncol],
                                                      start=(hc == 0), stop=(hc == KC - 1))
                            tm = pe.mark(mm)
                            dve.wait(tm, t_h)
                            td = dve.mark(nc.vector.scalar_tensor_tensor(out=hb[:, dc, 0:ncol], in0=pb[:, 0:ncol],
                                                                         scalar=modv(l, v, 16 + dc), in1=hb[:, dc, 0:ncol],
                                                                         op0=ALU.mult, op1=ALU.add))
                            psO.release(pi, td)
                            tres.append(td)
                        t_hh = tres
                    else:
                        t_hh = [t_h]
                    tx = stats_modulate(lambda kc: hb[:, kc, 0:ncol], ncol, l, v, 32, 24, lambda kc: x2T[:, kc, 0:ncol], bufs,
                                        t_hh, psS, extra_wait=x2_rel)
                    tz = []
                    if c_lo > 0:
                        dve.wait(tx)
                        tz.append(dve.mark(nc.vector.memset(x2T[:, :, 0:1], 0.0)))
                    if c_hi < ncol:
                        dve.wait(tx)
                        tz.append(dve.mark(nc.vector.memset(x2T[:, :, ncol - 1:ncol], 0.0)))
                    tact = []
                    for i in range(NPAIR):
                        res = []
                        for half in range(2):
                            cc = i + half * NPAIR
                            pi, pb = psZ.get(pe)
                            pe.wait(tx, tz)
                            for kc in range(KC):
                                mm = nc.tensor.matmul(pb[:, 0:ncol], Wup[:, kc, cc * 128:(cc + 1) * 128], x2T[:, kc, 0:ncol],
                                                      start=(kc == 0), stop=(kc == KC - 1))
                            tm = pe.mark(mm)
                            zi, zb = zc.get(act)
                            act.wait(tm)
                            ta = act.mark(nc.scalar.activation(out=zb[:, 0:ncol], in_=pb[:, 0:ncol], func=AF.Identity,
                                                               bias=convS[:, l, 3, cc:cc + 1], scale=convS[:, l, 1, cc:cc + 1]))
                            dve.wait(ta)
                            td1 = dve.mark(nc.vector.scalar_tensor_tensor(out=zb[:, 1:nv + 1], in0=pb[:, 0:nv],
                                                                          scalar=convS[:, l, 0, cc:cc + 1], in1=zb[:, 1:nv + 1],
                                                                          op0=ALU.mult, op1=ALU.add))
                            dve.wait(td1)
                            td2 = dve.mark(nc.vector.scalar_tensor_tensor(out=zb[:, 1:nv + 1], in0=pb[:, 2:nv + 2],
                                                                          scalar=convS[:, l, 2, cc:cc + 1], in1=zb[:, 1:nv + 1],
                                                                          op0=ALU.mult, op1=ALU.add))
                            psZ.release(pi, td2)
                            res.append((zi, zb, td2))
                        (zia, zba, tda), (zig, zbg, tdg) = res
                        si, sbuf_ = sa.get(act)
                        act.wait(tda)
                        tsl = act.mark(nc.scalar.activation(out=sbuf_[:, 1:nv + 1], in_=zba[:, 1:nv + 1], func=AF.Silu))
                        zc.release(zia, tsl)
                        dve.wait(tsl, tdg, a_rel if i == 0 else None)
                        tg = dve.mark(nc.vector.tensor_tensor(out=actT[:, i, 1:nv + 1], in0=sbuf_[:, 1:nv + 1], in1=zbg[:, 1:nv + 1],
                                                              op=ALU.mult))
                        zc.release(zig, tg)
                        sa.release(si, tg)
                        tact.append(tg)
                    x2_rel = tm
                    tdn = []
                    for dc in range(KC):
                        pi, pb = psO.get(pe)
                        pe.wait(tact)
                        for c in range(NPAIR):
                            mm = nc.tensor.matmul(pb[:, 0:nv], Wdn[:, c, dc * 128:(dc + 1) * 128], actT[:, c, 1:nv + 1],
                                                  start=(c == 0), stop=(c == NPAIR - 1))
                        tm = pe.mark(mm)
                        dve.wait(tm)
                        td = dve.mark(nc.vector.scalar_tensor_tensor(out=hb[:, dc, 1:nv + 1], in0=pb[:, 0:nv], scalar=modv(l, v, 40 + dc),
                                                                     in1=hb[:, dc, 1:nv + 1], op0=ALU.mult, op1=ALU.add))
                        psO.release(pi, td)
                        tdn.append(td)
                    a_rel = tm
                    if not final:
                        pool.wait(tdn)
                        tst = K.dma(pool, st_h[hi], DSTv[:, :, base + ts:base + ts + nv], hb[:, :, 1:nv + 1])
                        hring.release(hi, tst)
                    else:
                        pi, pb = psS.get(pe)
                        for kc in range(KC):
                            si, sq = bufs["sq"].get(act)
                            act.wait(tdn[kc])
                            tsq = act.mark(nc.scalar.activation(out=sq[:, 0:nv], in_=hb[:, kc, 1:nv + 1], func=AF.Square))
                            pe.wait(tsq)
                            tm = pe.mark(nc.tensor.matmul(pb[:, 0:nv], ones_bf[:], sq[:, 0:nv], start=(kc == 0), stop=(kc == KC - 1)))
                            bufs["sq"].release(si, tm)
                        act.wait(tm, bufs.get("sd_rel"))
                        t_sd = act.mark(nc.scalar.activation(out=bufs["sd"][:, 0:nv], in_=pb[:, 0:nv], func=AF.Sqrt,
                                                             bias=epsb[:, 0:1], scale=1.0 / D))
                        psS.release(pi, t_sd)
                        dve.wait(t_sd, bufs.get("r_rel"))
                        t_r = dve.mark(nc.vector.reciprocal(out=bufs["r"][:, 0:nv], in_=bufs["sd"][:, 0:nv]))
                        bufs["sd_rel"] = t_r
                        tyn = []
                        for kc in range(KC):
                            dve.wait(t_r, tdn[kc])
                            tyn.append(dve.mark(nc.vector.scalar_tensor_tensor(out=hb[:, kc, 1:nv + 1], in0=hb[:, kc, 1:nv + 1],
                                                                               scalar=fng[:, kc:kc + 1], in1=bufs["r"][:, 0:nv],
                                                                               op0=ALU.mult, op1=ALU.mult)))
                        bufs["r_rel"] = tyn[-1]
                        j0 = 0
                        while j0 < nv:
                            rows = min(128, nv - j0)
                            oi, ob = oring.get(act)
                            for q4 in range(2):
                                pi, pb = psT.get(pe)
                                pe.wait(tyn)
                                for kk in range(4):
                                    kc = q4 * 4 + kk
                                    mm = nc.tensor.transpose(pb[0:rows, kk * 128:(kk + 1) * 128], hb[:, kc, 1 + j0:1 + j0 + rows], ident[:])
                                tm = pe.mark(mm)
                                act.wait(tm)
                                ta = act.mark(nc.scalar.copy(out=ob[0:rows, q4 * 512:(q4 + 1) * 512], in_=pb[0:rows, :]))
                                psT.release(pi, ta)
                            pool.wait(ta)
                            tst = K.dma(pool, st_o[oi], out[ts + j0:ts + j0 + rows, :], ob[0:rows, :])
                            oring.release(oi, tst)
                            j0 += rows
                        hring.release(hi, tm)
                K.barrier()

        def phase_Q():
            st = ExitStack()
            with st:
                NB = 512
                Wq = sb(nc, st, "Wq", [128, KC, D], BF16)
                Wk = sb(nc, st, "Wk", [128, KC, D], BF16)
                Wv = sb(nc, st, "Wv", [128, KC, D], BF16)
                Wqr = sb(nc, st, "Wqr", [128, KC, D], BF16)
                Wkr = sb(nc, st, "Wkr", [128, KC, D], BF16)
                hT = [sb(nc, st, "qh%d" % i, [128, KC, NB], F32) for i in range(2)]
                cs = [sb(nc, st, "cs%d" % i, [128, 2, NB], F32) for i in range(2)]
                xlT = sb(nc, st, "qxl", [128, KC, NB], BF16)
                qst = [sb(nc, st, "qst%d" % i, [128, HEADS, NB], BF16) for i in range(2)]
                kst = [sb(nc, st, "kst%d" % i, [128, HEADS, NB], BF16) for i in range(2)]
                vst = [sb(nc, st, "vst%d" % i, [128, D], BF16) for i in range(2)]
                r1 = Ring([sb(nc, st, "r1_%d" % i, [128, NB], F32) for i in range(2)])
                r2 = Ring([sb(nc, st, "r2_%d" % i, [128, NB], F32) for i in range(2)])
                bufs = {
                    "sq": Ring([sb(nc, st, "qsq%d" % i, [128, NB], BF16) for i in range(2)]),
                    "sd": sb(nc, st, "qsd", [128, NB], F32),
                    "r": sb(nc, st, "qrb", [128, NB], F32),
                    "tm": Ring([sb(nc, st, "qtm%d" % i, [128, NB], F32) for i in range(2)]),
                }
                ldw = K.dsem("ldw_Q")
                for kc in range(KC):
                    K.dma(pool, ldw, Wq[:, kc, :], w_qkv[kc * 128:(kc + 1) * 128, 0:D])
                    K.dma(pool, ldw, Wk[:, kc, :], w_qkv[kc * 128:(kc + 1) * 128, D:2 * D])
                    t_w = K.dma(pool, ldw, Wv[:, kc, :], w_qkv[kc * 128:(kc + 1) * 128, 2 * D:3 * D])
                act.wait(t_w)
                dve.wait(t_w)
                for (Wsrc, Wrot) in ((Wq, Wqr), (Wk, Wkr)):
                    sv = Wsrc[:].rearrange("p k (a t b) -> p (k a) t b", t=2, b=16)
                    dv = Wrot[:].rearrange("p k (a t b) -> p (k a) t b", t=2, b=16)
                    act.mark(nc.scalar.mul(out=dv[:, :, 0, :], in_=sv[:, :, 1, :], mul=-1.0))
                    dve.mark(nc.vector.tensor_copy(out=dv[:, :, 1, :], in_=sv[:, :, 0, :]))
                pe.wait(t_w, act.last(), dve.last())

                ldh = [K.dsem("ldh_Q%d" % i) for i in range(2)]
                stq = [K.dsem("st_Q%d" % i) for i in range(2)]
                stv = [K.dsem("st_V%d" % i) for i in range(2)]
                hring = Ring(hT)
                vring = Ring(vst)
                psQ = Ring([(PSB[0], PSB[1]), (PSB[2], PSB[3])])
                psV = Ring([PSB[4], PSB[5]])
                psS = Ring([PSB[6]])
                xl_rel = None
                H2v = H2T.rearrange("(k p) t -> p k t", p=128)
                blocks = [(S, CTX, True)] + [(t0, NB, False) for t0 in range(0, S, NB)]
                st_rel = [None, None]
                for bi, (c0, n, is_ctx) in enumerate(blocks):
                    v = 1 if is_ctx else 0
                    hi, hb = hring.get(sp)
                    t_h = K.dma(sp, ldh[hi], hb[:, :, 0:n], H2v[:, :, c0:c0 + n])
                    csb_ = cs[hi]
                    if not is_ctx:
                        K.dma(sp, ldh[hi], csb_[:, 0, 0:n], cosT_d[:, c0:c0 + n])
                        t_h = K.dma(sp, ldh[hi], csb_[:, 1, 0:n], sinT_d[:, c0:c0 + n])
                    tx = stats_modulate(lambda kc: hb[:, kc, 0:n], n, 1, v, 8, 0, lambda kc: xlT[:, kc, 0:n], bufs,
                                        [t_h], psS, extra_wait=xl_rel)
                    si = bi % 2
                    qb, kb = qst[si], kst[si]
                    dve.wait(st_rel[si])
                    act.wait(st_rel[si])
                    tq = []
                    for (W, Wr, dstb, skip) in ((Wq, Wqr, qb, is_ctx), (Wk, Wkr, kb, False)):
                        if skip:
                            continue
                        for h in range(HEADS):
                            pi, (pa, pbr) = psQ.get(pe)
                            pe.wait(tx)
                            for kc in range(KC):
                                mm = nc.tensor.matmul(pa[:, 0:n], W[:, kc, h * 128:(h + 1) * 128], xlT[:, kc, 0:n],
                                                      start=(kc == 0), stop=(kc == KC - 1))
                            if is_ctx:
                                tm = pe.mark(mm)
                                act.wait(tm)
                                ta = act.mark(nc.scalar.copy(out=dstb[:, h, 0:n], in_=pa[:, 0:n]))
                                psQ.release(pi, ta)
                                tq.append(ta)
                                continue
                            for kc in range(KC):
                                mm = nc.tensor.matmul(pbr[:, 0:n], Wr[:, kc, h * 128:(h + 1) * 128], xlT[:, kc, 0:n],
                                                      start=(kc == 0), stop=(kc == KC - 1))
                            tm = pe.mark(mm)
                            i1, b1 = r1.get(dve)
                            i2, b2 = r2.get(dve)
                            dve.wait(tm, t_h)
                            ta = dve.mark(nc.vector.tensor_tensor(out=b1[:, 0:n], in0=pa[:, 0:n], in1=csb_[:, 0, 0:n], op=ALU.mult))
                            tb = dve.mark(nc.vector.tensor_tensor(out=b2[:, 0:n], in0=pbr[:, 0:n], in1=csb_[:, 1, 0:n], op=ALU.mult))
                            psQ.release(pi, tb)
                            dve.wait(ta, tb)
                            tc_ = dve.mark(nc.vector.tensor_tensor(out=dstb[:, h, 0:n], in0=b1[:, 0:n], in1=b2[:, 0:n], op=ALU.add))
                            r1.release(i1, tc_)
                            r2.release(i2, tc_)
                            tq.append(tc_)
                    pool.wait(tq)
                    koff = 0 if is_ctx else CTX
                    kc0 = (c0 - S) if is_ctx else c0
                    if not is_ctx:
                        K.dma(pool, stq[si], QT.rearrange("h p t -> p h t")[:, :, c0:c0 + n], qb[:, :, 0:n])
                    tstq = K.dma(pool, stq[si], KT.rearrange("h p t -> p h t")[:, :, koff + kc0:koff + kc0 + n], kb[:, :, 0:n])
                    st_rel[si] = tstq
                    for j in range(n // 128):
                        vi, vb = vring.get(act)
                        for dh in range(2):
                            pi, pb = psV.get(pe)
                            pe.wait(tx)
                            for kc in range(KC):
                                mm = nc.tensor.matmul(pb[:, :], xlT[:, kc, j * 128:(j + 1) * 128], Wv[:, kc, dh * 512:(dh + 1) * 512],
                                                      start=(kc == 0), stop=(kc == KC - 1))
                            tm = pe.mark(mm)
                            act.wait(tm)
                            ta = act.mark(nc.scalar.copy(out=vb[:, dh * 512:(dh + 1) * 512], in_=pb[:, :]))
                            psV.release(pi, ta)
                        kt = (koff + kc0) // 128 + j
                        pool.wait(ta)
                        tsv = K.dma(pool, stv[vi], VH.rearrange("h p (t e) -> p h t e", e=128)[:, :, kt, :],
                                    vb[:].rearrange("p (h e) -> p h e", e=128))
                        vring.release(vi, tsv)
                    xl_rel = tm
                    hring.release(hi, tx[-1], tq[-1])
                K.barrier()

        def phase_A():
            st = ExitStack()
            with st:
                QB = 512
                kTb = [sb(nc, st, "kTb%d" % i, [128, ST], BF16) for i in range(2)]
                vHb = [sb(nc, st, "vHb%d" % i, [128, NT, 128], BF16) for i in range(2)]
                qb = [sb(nc, st, "qb%d" % i, [128, QB], BF16) for i in range(2)]
                pT = Ring([sb(nc, st, "pT%d" % i, [128, 2 * QB], BF16) for i in range(4)])
                sel = sb(nc, st, "sel", [64, 128], F32)
                ones_f = sb(nc, st, "ones_f", [128, 128], F32)
                accr = Ring([sb(nc, st, "acc%d" % i, [128, QB + QB // 2], F32) for i in range(2)])
                sums_sb = sb(nc, st, "sums_sb", [64, QB], F32)
                pocp = sb(nc, st, "pocp", [128, 2 * QB], F32)
                rs = sb(nc, st, "rs", [128, 2 * QB], F32)
                o0 = sb(nc, st, "o0", [128, QB], F32)
                o1 = sb(nc, st, "o1", [128, QB], F32)
                osq = sb(nc, st, "osq", [128, QB], BF16)
                osd = sb(nc, st, "osd", [128, QB], F32)
                orr = sb(nc, st, "orr", [128, QB], F32)
                ob = [sb(nc, st, "aob%d" % i, [128, QB], BF16) for i in range(2)]
                subg2 = sb(nc, st, "subg2", [128, 1], F32)
                dve.mark(nc.vector.memset(sel[:], 1.0 / 32.0))
                dve.mark(nc.vector.memset(ones_f[:], 1.0))
                t_init = dve.mark(nc.vector.tensor_scalar(out=subg2[:], in0=subg[:], scalar1=(1.0 - LAMBDA_INIT), scalar2=None, op0=ALU.mult))
                pe.wait(t_init)
                ldk = [K.dsem("ldk%d" % i) for i in range(2)]
                ldq = [K.dsem("ldq%d" % i) for i in range(2)]
                sto = [K.dsem("sto%d" % i) for i in range(2)]
                kring = Ring(list(zip(kTb, vHb)))
                qring = Ring(qb)
                oring = Ring(ob)
                psS = Ring([PSD[0], PSD[1]])
                po = PSD[2]
                PSx = PSB[6]
                nqb = S // QB
                units = [(h, q_) for h in range(HEADS) for q_ in range(nqb)]
                kv = {}
                qd = {}
                state = {"fin_rel": None, "acc_rel": None}

                def load_kv(h):
                    ki, (kb_, vb_) = kring.get(sp)
                    K.dma(sp, ldk[ki], kb_[:], KT[h])
                    t_k = K.dma(sp, ldk[ki], vb_[:].rearrange("p t e -> p (t e)"), VH[h])
                    kv[h] = (ki, kb_, vb_, t_k)

                def load_q(ui):
                    h, q_ = units[ui]
                    qi, qbuf = qring.get(sp)
                    t_q = K.dma(sp, ldq[qi], qbuf[:], QT[h, :, q_ * QB:(q_ + 1) * QB])
                    qd[ui] = (qi, qbuf, t_q)

                def issue_S(kb_, qbuf, kt):
                    pi, sp_ = psS.get(pe)
                    nc.tensor.matmul(sp_[:, 0:QB], kb_[0:64, kt * 128:(kt + 1) * 128], qbuf[0:64, :], start=True, stop=True)
                    ts = pe.mark(nc.tensor.matmul(sp_[:, QB:2 * QB], kb_[64:128, kt * 128:(kt + 1) * 128], qbuf[64:128, :],
                                                  start=True, stop=True))
                    ppi, pbuf = pT.get(act)
                    act.wait(ts)
                    te = act.mark(nc.scalar.activation(out=pbuf[:, :], in_=sp_[:, :], func=AF.Exp, scale=0.125))
                    psS.release(pi, te)
                    return (ppi, pbuf, te)

                PSY = PSB[7]

                def finalize(tv, h, q_, acc, ai, td):
                    dve.wait(tv, state["fin_rel"])
                    QH = QB // 2
                    dve.mark(nc.vector.tensor_copy(out=sums_sb[32:64, 0:QH], in_=PSx[32:64, 0:QH]))
                    tcp = dve.mark(nc.vector.tensor_copy(out=pocp[:, :], in_=po[:, :]))
                    state["acc_rel"] = tcp
                    yield
                    pe.wait(tcp, td, state.get("y_rel"))
                    tb0 = pe.mark(nc.tensor.matmul(PSY[:, :], ones_f[:], acc[:, 0:QB], start=True, stop=True))
                    yield
                    dve.wait(tb0)
                    t1a = dve.mark(nc.vector.reciprocal(out=rs[:, 0:QB], in_=PSY[:, :]))
                    yield
                    pe.wait(t1a)
                    nc.tensor.matmul(PSY[:, 0:QH], ones_f[:], acc[:, QB:QB + QH], start=True, stop=True)
                    tb1 = pe.mark(nc.tensor.matmul(PSY[:, QH:QB], sel[32:64, :], sums_sb[32:64, 0:QH], start=True, stop=True))
                    accr.release(ai, tb1)
                    yield
                    dve.wait(tb1)
                    t1b = dve.mark(nc.vector.reciprocal(out=rs[:, QB:2 * QB], in_=PSY[:, :]))
                    dve.wait(t1b)
                    dve.mark(nc.vector.tensor_tensor(out=o0[:], in0=pocp[:, 0:QB], in1=rs[:, 0:QB], op=ALU.mult))
                    t2 = dve.mark(nc.vector.tensor_tensor(out=o1[:], in0=pocp[:, QB:2 * QB], in1=rs[:, QB:2 * QB], op=ALU.mult))
                    dve.wait(t2)
                    t3 = dve.mark(nc.vector.scalar_tensor_tensor(out=o0[:], in0=o1[:], scalar=neglam[:, 0:1], in1=o0[:],
                                                                 op0=ALU.mult, op1=ALU.add))
                    yield
                    act.wait(t3)
                    t4 = act.mark(nc.scalar.activation(out=osq[:], in_=o0[:], func=AF.Square))
                    yield
                    pe.wait(t4, t1b)
                    t5 = pe.mark(nc.tensor.matmul(PSY[:, :], ones_bf[:], osq[:], start=True, stop=True))
                    yield
                    act.wait(t5)
                    t6 = act.mark(nc.scalar.activation(out=osd[:], in_=PSY[:, :], func=AF.Sqrt, bias=epsb[:, 0:1], scale=1.0 / 128))
                    state["y_rel"] = t6
                    yield
                    dve.wait(t6)
                    t7 = dve.mark(nc.vector.reciprocal(out=orr[:], in_=osd[:]))
                    oi2, obuf = oring.get(dve)
                    dve.wait(t7)
                    t8 = dve.mark(nc.vector.scalar_tensor_tensor(out=obuf[:], in0=o0[:], scalar=subg2[:, 0:1], in1=orr[:],
                                                                 op0=ALU.mult, op1=ALU.mult))
                    state["fin_rel"] = t8
                    pool.wait(t8)
                    tso = K.dma(pool, sto[oi2], AT[h * 128:(h + 1) * 128, q_ * QB:(q_ + 1) * QB], obuf[:])
                    oring.release(oi2, tso)
                    yield

                load_kv(0)
                load_q(0)
                flat = [(ui, kt) for ui in range(len(units)) for kt in range(NT)]
                issued = []
                state["sptr"] = 0

                def issue_next():
                    if state["sptr"] >= len(flat):
                        return
                    ui, kt = flat[state["sptr"]]
                    state["sptr"] += 1
                    hh, _ = units[ui]
                    if kt == 0:
                        pe.wait(kv[hh][3], qd[ui][2])
                    issued.append(issue_S(kv[hh][1], qd[ui][1], kt))

                for ui, (h, q_) in enumerate(units):
                    if q_ == 0 and h + 1 < HEADS:
                        load_kv(h + 1)
                    if ui + 1 < len(units):
                        load_q(ui + 1)
                    if ui == 0:
                        issue_next()
                        issue_next()
                    ki, kb_, vb_, t_k = kv[h]
                    qi, qbuf, t_q = qd[ui]
                    ai, acc = accr.get(dve)
                    for kt in range(NT):
                        if state.get("fin") is not None and kt >= 2 and kt % 2 == 0:
                            if next(state["fin"], "done") == "done":
                                state["fin"] = None
                        ppi, pbuf, te = issued.pop(0)
                        pe.wait(te, state["acc_rel"] if kt == 0 else None)
                        issue_next()
                        f, l_ = (kt == 0), (kt == NT - 1)
                        nc.tensor.matmul(po[:, 0:QB], vb_[:, kt, :], pbuf[:, 0:QB], start=f, stop=l_)
                        nc.tensor.matmul(po[:, QB:2 * QB], vb_[:, kt, :], pbuf[:, QB:2 * QB], start=f, stop=l_)
                        QH = QB // 2
                        tv = pe.mark(nc.tensor.matmul(PSx[32:64, 0:QH], ones_bf[:, 0:32], pbuf[:, QB + QH:2 * QB], start=f, stop=l_,
                                                      tile_position=(0, 32)))
                        dve.wait(te)
                        if kt == 0:
                            td = dve.mark(nc.vector.tensor_copy(out=acc[:, :], in_=pbuf[:, 0:QB + QH]))
                        else:
                            td = dve.mark(nc.vector.tensor_tensor(out=acc[:, :], in0=acc[:, :], in1=pbuf[:, 0:QB + QH], op=ALU.add))
                        pT.release(ppi, tv, td)
                    qring.release(qi, tv)
                    if q_ == nqb - 1:
                        kring.release(ki, tv)
                    if state.get("fin") is not None:
                        for _ in state["fin"]:
                            pass
                    state["fin"] = finalize(tv, h, q_, acc, ai, td)
                    next(state["fin"])
                for _ in state["fin"]:
                    pass
                K.barrier()

        def phase_O():
            st = ExitStack()
            with st:
                NB = 512
                Wo = sb(nc, st, "Wo", [128, KC, D], BF16)
                hT = [sb(nc, st, "oh%d" % i, [128, KC, NB], F32) for i in range(2)]
                aT = [sb(nc, st, "oa%d" % i, [128, KC, NB], BF16) for i in range(2)]
                ldw = K.dsem("ldw_O")
                for kc in range(KC):
                    t_w = K.dma(pool, ldw, Wo[:, kc, :], w_o[kc * 128:(kc + 1) * 128, :])
                pe.wait(t_w)
                ldh = [K.dsem("ldh_O%d" % i) for i in range(2)]
                st_h = [K.dsem("st_O%d" % i) for i in range(2)]
                hring = Ring(hT)
                psO = Ring([PSB[0], PSB[1], PSB[2], PSB[3]])
                H2v = H2T.rearrange("(k p) t -> p k t", p=128)
                H1v = H1T.rearrange("(k p) t -> p k t", p=128)
                ATv = AT.rearrange("(k p) t -> p k t", p=128)
                for c0 in range(0, S, NB):
                    hi, hb = hring.get(sp)
                    ab = aT[hi]
                    K.dma(sp, ldh[hi], hb[:], H2v[:, :, c0:c0 + NB])
                    t_h = K.dma(sp, ldh[hi], ab[:], ATv[:, :, c0:c0 + NB])
                    for dc in range(KC):
                        pi, pb = psO.get(pe)
                        pe.wait(t_h)
                        for hc in range(KC):
                            mm = nc.tensor.matmul(pb[:, :], Wo[:, hc, dc * 128:(dc + 1) * 128], ab[:, hc, :],
                                                  start=(hc == 0), stop=(hc == KC - 1))
                        tm = pe.mark(mm)
                        dve.wait(tm, t_h)
                        td = dve.mark(nc.vector.scalar_tensor_tensor(out=hb[:, dc, :], in0=pb[:, :], scalar=modv(1, 0, 16 + dc),
                                                                     in1=hb[:, dc, :], op0=ALU.mult, op1=ALU.add))
                        psO.release(pi, td)
                    pool.wait(td)
                    tst = K.dma(pool, st_h[hi], H1v[:, :, c0:c0 + NB], hb[:])
                    hring.release(hi, tst)
                K.barrier()

        plan = [("mod", phase_mod), ("G", lambda: phase_G(3)),
                ("F0", lambda: phase_F(0, H1T, H2T, False, False, 360)),
                ("Q", phase_Q), ("A", phase_A), ("O", phase_O),
                ("F1", lambda: phase_F(1, H1T, None, False, True, 360))]
        for name, fn in plan:
            fn()
            if stop_after == name:
                break
        K.barrier()
    return nc


def _rope_tables(S):
    rows = S // 64
    row_pos = np.broadcast_to(np.arange(rows, dtype=np.float32)[:, None], (rows, 64)).reshape(-1)
    col_pos = np.broadcast_to(np.arange(64, dtype=np.float32)[None, :], (rows, 64)).reshape(-1)
    n_freq = 16
    inv_freq = (np.float32(10000.0) ** (-np.arange(n_freq, dtype=np.float32) / np.float32(n_freq))).astype(np.float32)
    ang_r = row_pos[:, None] * inv_freq
    ang_c = col_pos[:, None] * inv_freq
    ang = np.concatenate([ang_r, ang_r, ang_c, ang_c], axis=-1).astype(np.float32)
    cosT = np.ascontiguousarray(np.cos(ang).T.astype(np.float32))
    sinT = np.ascontiguousarray(np.sin(ang).T.astype(np.float32))
    return np.concatenate([cosT, cosT], 0), np.concatenate([sinT, sinT], 0)


def _fm(vec, nchunk):
    return np.ascontiguousarray(np.asarray(vec, np.float32).reshape(nchunk, 128).T)


def make_in_maps(inp, S, ncores):
    f = lambda a: np.ascontiguousarray(np.asarray(a, np.float32))
    cosT, sinT = _rope_tables(S)
    shared = {
        "ada_w": f(inp["ada_w"]),
        "ada_bT": np.ascontiguousarray(np.stack([_fm(inp["ada_b"][l], 48) for l in range(2)], 1).reshape(128, 96)),
        "gm_w_in": f(inp["gm_w_in"][0]),
        "gm_gT": _fm(inp["gm_norm_g"][0], 16),
        "gm_wsT": np.ascontiguousarray(np.transpose(f(inp["gm_w_s"][0]), (2, 0, 1)).reshape(128, 1024)),
        "gm_bs": f(inp["gm_b_s"][0]).reshape(1, 1024),
        "gm_w_out": f(inp["gm_w_out"][0]),
        "w_qkv": f(inp["da_w_qkv"][0]),
        "lamv": np.concatenate([f(inp["da_lambda_q1"][0]), f(inp["da_lambda_k1"][0]),
                                f(inp["da_lambda_q2"][0]), f(inp["da_lambda_k2"][0])]).reshape(1, 256),
        "sublnT": f(inp["da_subln_g"][0]).reshape(128, 1),
        "w_o": f(inp["da_w_out"][0]),
        "w_up": f(inp["ffn_w_up"]),
        "convT": np.ascontiguousarray(np.stack(
            [np.stack([_fm(inp["ffn_conv_w"][l][0], NCC), _fm(inp["ffn_conv_w"][l][1], NCC),
                       _fm(inp["ffn_conv_w"][l][2], NCC), _fm(inp["ffn_conv_b"][l], NCC)], 1) for l in range(2)], 1).reshape(128, 2 * 4 * NCC)),
        "w_dn": f(inp["ffn_w_down"]),
        "fngT": _fm(inp["final_norm_g"], KC),
        "ident": np.eye(128, dtype=np.float32),
        "cosT": cosT, "sinT": sinT,
    }
    cc = _fm(inp["c_ctx"], KC)
    maps = []
    for b in range(ncores):
        m = dict(shared)
        m["x"] = f(inp["x"][b])
        m["ctx"] = f(inp["ctx"][b])
        cb = _fm(inp["c"][b], KC)
        m["cvec"] = np.ascontiguousarray(np.stack([cb, cc], 2).reshape(128, 2 * KC))
        maps.append(m)
    return maps


_NC_CACHE = {}


def kernel(**inputs):
    S = int(inputs["x"].shape[1])
    B = int(inputs["x"].shape[0])
    key = (S,)
    if key not in _NC_CACHE:
        _NC_CACHE[key] = build_program(S)
    nc = _NC_CACHE[key]
    in_maps = make_in_maps(inputs, S, B)
    res = run_bass_kernel_spmd(nc, in_maps, core_ids=list(range(B)))
    return np.stack([np.asarray(r["out"], np.float32) for r in res.results], 0)
```
